# Optimizing a Trainium2 kernel written in Bass

```python
import jax
import jax.numpy as jnp
from jax import lax
import numpy as np

D_MODEL = 1024
BATCH = 8
SEQ = 2048
DEPTH = 1
DEC_BATCH = 128
DEC_SEQ = 4
PAST_LEN = 16384
PAGE_SIZE = 128

HEAD_DIM = 64
A_Q_HEADS = 8
A_KV_HEADS = 2
A_GROUP = A_Q_HEADS // A_KV_HEADS
A_WINDOW = 128
B_DIL = ((128, 1), (512, 4), (2048, 16))
B_SLOTS = 4
B_HEADS = B_SLOTS * len(B_DIL)
BLK = 128
ROPE_THETA = 10000.0
D_FF = ((8 * D_MODEL // 3 + 127) // 128) * 128
CONV_W = 3
ALPHA = (2 * DEPTH) ** 0.25
BETA = (8 * DEPTH) ** -0.25
LN_EPS = 1e-5
NEG = -1e30
SCALE = HEAD_DIM ** -0.5
A_Q = A_Q_HEADS * HEAD_DIM
A_KV = A_KV_HEADS * HEAD_DIM
B_W = B_HEADS * HEAD_DIM
N_PROJ = A_Q + 2 * A_KV + 3 * B_W + 2 * D_MODEL
SPLIT_POINTS = (A_Q, A_Q + A_KV, A_Q + 2 * A_KV, A_Q + 2 * A_KV + B_W,
                A_Q + 2 * A_KV + 2 * B_W, A_Q + 2 * A_KV + 3 * B_W,
                A_Q + 2 * A_KV + 3 * B_W + D_MODEL)

kernel_name = 'hybrid_swa_sink_dilated_convffn_step'


def _layernorm(x, g, b):
    xf = x.astype(jnp.float32)
    mu = jnp.mean(xf, axis=-1, keepdims=True)
    var = jnp.mean(jnp.square(xf - mu), axis=-1, keepdims=True)
    return ((xf - mu) * lax.rsqrt(var + LN_EPS) * g + b).astype(x.dtype)


def _rope(x, pos):
    half = HEAD_DIM // 2
    inv = ROPE_THETA ** (-jnp.arange(half, dtype=jnp.float32) / half)
    ang = pos.astype(jnp.float32)[:, None] * inv[None, :]
    cos = jnp.cos(ang)[:, None, :]
    sin = jnp.sin(ang)[:, None, :]
    xf = x.astype(jnp.float32)
    x1, x2 = xf[..., :half], xf[..., half:]
    return jnp.concatenate([x1 * cos - x2 * sin, x2 * cos + x1 * sin], axis=-1).astype(x.dtype)


def _project(x, w_in):
    n, t, _ = x.shape
    qa, ka, va, qb, kb, vb, ga, gb = jnp.split(x @ w_in, SPLIT_POINTS, axis=-1)
    heads = lambda z: z.reshape(n, t, -1, HEAD_DIM)
    return heads(qa), heads(ka), heads(va), heads(qb), heads(kb), heads(vb), ga, gb


def _sink_logits(sink_a):
    return sink_a.astype(jnp.float32).reshape(A_KV_HEADS, A_GROUP, 1, 1)


def _sink_softmax(s, sink):
    m = jnp.maximum(jnp.max(s, axis=-1, keepdims=True), sink)
    e = jnp.exp(s - m)
    return e / (jnp.sum(e, axis=-1, keepdims=True) + jnp.exp(sink - m))


def _softmax_lse(s):
    m = jnp.max(s, axis=-1, keepdims=True)
    e = jnp.exp(s - m)
    den = jnp.sum(e, axis=-1, keepdims=True)
    return e / den, (m + jnp.log(den))[..., 0]


def _blocks(x, nb):
    pad = [(0, 0), (0, nb * BLK - x.shape[1])] + [(0, 0)] * (x.ndim - 2)
    x = jnp.pad(x, pad)
    return x.reshape(x.shape[0], nb, BLK, *x.shape[2:])


def _with_prev(xb):
    prev = jnp.pad(xb[:, :-1], [(0, 0), (1, 0)] + [(0, 0)] * (xb.ndim - 2))
    return jnp.concatenate([prev, xb], axis=2)


def _band_scores(q, k, window):
    n, t = q.shape[:2]
    nb = -(-t // BLK)
    qb = _blocks(q, nb)
    kc = _with_prev(_blocks(k, nb))
    s = jnp.einsum('nbqkgd,nbskd->nbkgqs', qb, kc, preferred_element_type=jnp.float32) * SCALE
    blk = jnp.arange(nb)[:, None, None]
    qi = jnp.arange(BLK)[None, :, None] + BLK
    si = jnp.arange(2 * BLK)[None, None, :]
    dist = qi - si
    mask = (dist >= 0) & (dist < window) & ((blk > 0) | (si >= BLK))
    return jnp.where(mask[None, :, None, None], s, NEG)


def _band_values(p, v):
    n, t = v.shape[:2]
    nb = p.shape[1]
    vc = _with_prev(_blocks(v, nb))
    o = jnp.einsum('nbkgqs,nbskd->nbqkgd', p.astype(v.dtype), vc)
    return o.reshape(n, nb * BLK, *o.shape[3:])[:, :t]


def _band_rows(r, t):
    n, nb = r.shape[:2]
    return r.transpose(0, 1, 4, 2, 3).reshape(n, nb * BLK, *r.shape[2:4])[:, :t]


def _stride_split(x, d):
    n, t = x.shape[:2]
    x = x.reshape(n, t // d, d, *x.shape[2:]).swapaxes(1, 2)
    return x.reshape(n * d, t // d, *x.shape[3:])


def _stride_merge(x, n, d):
    x = x.reshape(n, d, *x.shape[1:]).swapaxes(1, 2)
    return x.reshape(n, d * x.shape[1], *x.shape[3:])


def _combine_groups(outs, lses):
    o = jnp.stack(outs)
    wts = jax.nn.softmax(jnp.stack(lses), axis=0)
    ob = jnp.sum(wts[..., None] * o.astype(jnp.float32), axis=0).astype(o.dtype)
    return ob.reshape(ob.shape[0], ob.shape[1], B_SLOTS * HEAD_DIM)


def _merge(oa, ob, ga, gb, w_br_a, w_br_b, w_o):
    m = jax.nn.sigmoid(ga) * (oa @ w_br_a) + jax.nn.sigmoid(gb) * (ob @ w_br_b)
    return m @ w_o


def _prompt_mixer(x, w_in, sink_a, w_br_a, w_br_b, w_o):
    n, t, _ = x.shape
    pos = jnp.arange(t, dtype=jnp.int32)
    qa, ka, va, qb, kb, vb, ga, gb = _project(x, w_in)
    qa, ka, qb, kb = _rope(qa, pos), _rope(ka, pos), _rope(qb, pos), _rope(kb, pos)
    s = _band_scores(qa.reshape(n, t, A_KV_HEADS, A_GROUP, HEAD_DIM), ka, A_WINDOW)
    p = _sink_softmax(s, _sink_logits(sink_a))
    oa = _band_values(p, va).reshape(n, t, A_Q)
    la = min(A_WINDOW, t)
    kv_a = jnp.stack([ka[:, t - la:], va[:, t - la:]], axis=2)
    outs, lses, kv_b = [], [], []
    for g, (win, dil) in enumerate(B_DIL):
        hs = slice(g * B_SLOTS, (g + 1) * B_SLOTS)
        qg, kg, vg = (_stride_split(z[:, :, hs], dil) for z in (qb, kb, vb))
        s = _band_scores(qg[:, :, :, None], kg, win // dil)
        p, lse = _softmax_lse(s)
        outs.append(_stride_merge(_band_values(p, vg)[:, :, :, 0], n, dil))
        lses.append(_stride_merge(_band_rows(lse, t // dil)[..., 0], n, dil))
        lb = min(win, t)
        kv_b.append(jnp.stack([kb[:, t - lb:, hs], vb[:, t - lb:, hs]], axis=2))
    ob = _combine_groups(outs, lses)
    return _merge(oa, ob, ga, gb, w_br_a, w_br_b, w_o), kv_a, kv_b


def _sample_mixer(x, cache_a, caches_b, w_in, sink_a, w_br_a, w_br_b, w_o):
    n, t, _ = x.shape
    qpos = PAST_LEN + jnp.arange(t, dtype=jnp.int32)
    qa, ka, va, qb, kb, vb, ga, gb = _project(x, w_in)
    qa, ka, qb, kb = _rope(qa, qpos), _rope(ka, qpos), _rope(qb, qpos), _rope(kb, qpos)
    la = cache_a.shape[1]
    k_all = jnp.concatenate([cache_a[:, :, 0], ka], axis=1)
    v_all = jnp.concatenate([cache_a[:, :, 1], va], axis=1)
    kpos = PAST_LEN - la + jnp.arange(la + t, dtype=jnp.int32)
    dist = qpos[:, None] - kpos[None, :]
    mask = (dist >= 0) & (dist < A_WINDOW)
    s = jnp.einsum('nqkgd,nskd->nkgqs', qa.reshape(n, t, A_KV_HEADS, A_GROUP, HEAD_DIM), k_all,
                   preferred_element_type=jnp.float32) * SCALE
    p = _sink_softmax(jnp.where(mask, s, NEG), _sink_logits(sink_a))
    oa = jnp.einsum('nkgqs,nskd->nqkgd', p.astype(v_all.dtype), v_all).reshape(n, t, A_Q)
    kv_a = jnp.stack([ka, va], axis=2)
    outs, lses, kv_b = [], [], []
    for g, ((win, dil), cache) in enumerate(zip(B_DIL, caches_b)):
        hs = slice(g * B_SLOTS, (g + 1) * B_SLOTS)
        lb = cache.shape[1]
        kg_new, vg_new = kb[:, :, hs], vb[:, :, hs]
        k_all = jnp.concatenate([cache[:, :, 0], kg_new], axis=1)
        v_all = jnp.concatenate([cache[:, :, 1], vg_new], axis=1)
        kp = qpos[:, None] - dil * jnp.arange(win // dil, dtype=jnp.int32)[None, :]
        idx = jnp.maximum(kp - (PAST_LEN - lb), 0)
        s = jnp.einsum('nqhd,nqjhd->nhqj', qb[:, :, hs], k_all[:, idx],
                       preferred_element_type=jnp.float32) * SCALE
        p, lse = _softmax_lse(jnp.where(kp >= 0, s, NEG))
        outs.append(jnp.einsum('nhqj,nqjhd->nqhd', p.astype(v_all.dtype), v_all[:, idx]))
        lses.append(lse.transpose(0, 2, 1))
        kv_b.append(jnp.stack([kg_new, vg_new], axis=2))
    ob = _combine_groups(outs, lses)
    return _merge(oa, ob, ga, gb, w_br_a, w_br_b, w_o), kv_a, kv_b


def _post_layer(x, mix, conv_prev, ln1_g, ln1_b, w_up, conv_w, conv_b, w_down, ln2_g, ln2_b):
    h = _layernorm(ALPHA * x + mix, ln1_g, ln1_b)
    u, v = jnp.split(h @ w_up, 2, axis=-1)
    if conv_prev is None:
        conv_prev = jnp.zeros((u.shape[0], CONV_W - 1, D_FF), u.dtype)
    ext = jnp.concatenate([conv_prev.astype(u.dtype), u], axis=1)
    t = u.shape[1]
    a = sum((conv_w[j] * ext[:, j:j + t] for j in range(CONV_W)), conv_b)
    f = (jax.nn.gelu(a, approximate=False) * v) @ w_down
    return _layernorm(ALPHA * h + f, ln2_g, ln2_b), ext[:, t:]


def setup_inputs(seed: int = 0) -> dict:
    key = jax.random.key(seed)
    ks = jax.random.split(key, 20)

    def nrm(k, shape, scale=1.0):
        return jax.random.normal(k, shape, jnp.float32) * scale

    la = min(A_WINDOW, PAST_LEN)
    lb1, lb2, lb3 = (min(w, PAST_LEN) for w, _ in B_DIL)
    bo = B_SLOTS * HEAD_DIM
    return {
        'x_prompt': nrm(ks[0], (BATCH, SEQ, D_MODEL)),
        'x_sample': nrm(ks[1], (DEC_BATCH, DEC_SEQ, D_MODEL)),
        'cache_a': nrm(ks[2], (DEPTH, DEC_BATCH, la, 2, A_KV_HEADS, HEAD_DIM)),
        'cache_b1': nrm(ks[3], (DEPTH, DEC_BATCH, lb1, 2, B_SLOTS, HEAD_DIM)),
        'cache_b2': nrm(ks[4], (DEPTH, DEC_BATCH, lb2, 2, B_SLOTS, HEAD_DIM)),
        'cache_b3': nrm(ks[5], (DEPTH, DEC_BATCH, lb3, 2, B_SLOTS, HEAD_DIM)),
        'state_conv': nrm(ks[6], (DEPTH, DEC_BATCH, CONV_W - 1, D_FF)),
        'w_in': nrm(ks[7], (DEPTH, D_MODEL, N_PROJ), D_MODEL ** -0.5),
        'sink_a': nrm(ks[8], (DEPTH, A_Q_HEADS), 0.5),
        'w_br_a': nrm(ks[9], (DEPTH, A_Q, D_MODEL), A_Q ** -0.5),
        'w_br_b': nrm(ks[10], (DEPTH, bo, D_MODEL), bo ** -0.5),
        'w_o': nrm(ks[11], (DEPTH, D_MODEL, D_MODEL), BETA * D_MODEL ** -0.5),
        'ln1_g': 1.0 + nrm(ks[12], (DEPTH, D_MODEL), 0.02),
        'ln1_b': nrm(ks[13], (DEPTH, D_MODEL), 0.02),
        'w_up': nrm(ks[14], (DEPTH, D_MODEL, 2 * D_FF), D_MODEL ** -0.5),
        'conv_w': nrm(ks[15], (DEPTH, CONV_W, D_FF), CONV_W ** -0.5),
        'conv_b': nrm(ks[16], (DEPTH, D_FF), 0.02),
        'w_down': nrm(ks[17], (DEPTH, D_FF, D_MODEL), BETA * D_FF ** -0.5),
        'ln2_g': 1.0 + nrm(ks[18], (DEPTH, D_MODEL), 0.02),
        'ln2_b': nrm(ks[19], (DEPTH, D_MODEL), 0.02),
    }


def reference(x_prompt, x_sample, cache_a, cache_b1, cache_b2, cache_b3, state_conv,
              w_in, sink_a, w_br_a, w_br_b, w_o, ln1_g, ln1_b,
              w_up, conv_w, conv_b, w_down, ln2_g, ln2_b):
    hp, hs = x_prompt, x_sample
    a_p, a_s, b_p, b_s, c_p, c_s = [], [], [], [], [], []
    for l in range(DEPTH):
        ffn = (ln1_g[l], ln1_b[l], w_up[l], conv_w[l], conv_b[l], w_down[l], ln2_g[l], ln2_b[l])
        mix_p, kva_p, kvb_p = _prompt_mixer(hp, w_in[l], sink_a[l], w_br_a[l], w_br_b[l], w_o[l])
        mix_s, kva_s, kvb_s = _sample_mixer(hs, cache_a[l], (cache_b1[l], cache_b2[l], cache_b3[l]),
                                            w_in[l], sink_a[l], w_br_a[l], w_br_b[l], w_o[l])
        hp, conv_p = _post_layer(hp, mix_p, None, *ffn)
        hs, conv_s = _post_layer(hs, mix_s, state_conv[l], *ffn)
        a_p.append(kva_p)
        a_s.append(kva_s)
        b_p.append(kvb_p)
        b_s.append(kvb_s)
        c_p.append(conv_p)
        c_s.append(conv_s)
    y_prompt, y_sample = hp, hs
    cache_a_prompt = jnp.stack(a_p)
    cache_a_sample = jnp.stack(a_s)
    cache_b1_prompt = jnp.stack([b[0] for b in b_p])
    cache_b1_sample = jnp.stack([b[0] for b in b_s])
    cache_b2_prompt = jnp.stack([b[1] for b in b_p])
    cache_b2_sample = jnp.stack([b[1] for b in b_s])
    cache_b3_prompt = jnp.stack([b[2] for b in b_p])
    cache_b3_sample = jnp.stack([b[2] for b in b_s])
    state_conv_prompt = jnp.stack(c_p)
    state_conv_sample = jnp.stack(c_s)
    return (y_prompt, y_sample, cache_a_prompt, cache_a_sample, cache_b1_prompt, cache_b1_sample,
            cache_b2_prompt, cache_b2_sample, cache_b3_prompt, cache_b3_sample,
            state_conv_prompt, state_conv_sample)
```

```python
import os
from contextlib import ExitStack
import numpy as np
import concourse.bass as bass
import concourse.mybir as mybir
from concourse.bass_utils import run_bass_kernel_spmd

F32 = mybir.dt.float32
BF16 = mybir.dt.bfloat16
AF = mybir.ActivationFunctionType
ALU = mybir.AluOpType

NCORES = 8
D = 1024
SEQ = 2048
NB = SEQ // 128
NS = 16
TS = 4
NTS = NS * TS
NT = SEQ + NTS
PAST = 16384
DFF = 2816
NJ = DFF // 128
ALPHA = float(2.0 ** 0.25)
SCALE = 0.125
EPS = 1e-5
ENGS = ("pe", "act", "dve", "pool", "sp")


class Prog:
    def __init__(self, nc, stack):
        self.nc = nc
        self.streams = {e: [] for e in ENGS}
        self.cnt = {e: 0 for e in ENGS}
        self.seen = {e: {} for e in ENGS}
        self.last_w = {}
        self.readers = {}
        self.dma_sems = {}
        self._stack = stack
        self.eng_sem = {e: stack.enter_context(nc.semaphore("c_" + e)) for e in ENGS}

    def _dma_sem(self, key):
        if key not in self.dma_sems:
            h = self._stack.enter_context(self.nc.semaphore("d_" + key))
            self.dma_sems[key] = [h, 0]
        return self.dma_sems[key]

    @staticmethod
    def _excl(reads, writes):
        return list(writes) + [b for b in reads if b.startswith("ps")]

    def _deps(self, reads, writes):
        toks = []
        writes = self._excl(reads, writes)
        for b in reads:
            t = self.last_w.get(b)
            if t is not None:
                toks.append(t)
        for b in writes:
            t = self.last_w.get(b)
            if t is not None:
                toks.append(t)
            toks.extend(self.readers.get(b, ()))
        return toks

    def _emit_waits(self, eng, toks, ss=False):
        need = {}
        for kind, key, val in toks:
            if kind == "c" and key == eng and (ss or eng in ("pe", "sp")):
                continue
            k = (kind, key)
            if val > need.get(k, 0):
                need[k] = val
        for (kind, key), val in need.items():
            if self.seen[eng].get((kind, key), 0) >= val:
                continue
            self.seen[eng][(kind, key)] = val
            sem = self.eng_sem[key] if kind == "c" else self.dma_sems[key][0]
            self.streams[eng].append(("wait", sem, val))

    def _commit(self, tok, reads, writes):
        writes = self._excl(reads, writes)
        for b in reads:
            self.readers.setdefault(b, []).append(tok)
        for b in writes:
            self.last_w[b] = tok
            self.readers[b] = []

    def op(self, eng, fns, r=(), w=(), ss=False):
        if callable(fns):
            fns = [fns]
        self._emit_waits(eng, self._deps(r, w), ss)
        self.cnt[eng] += 1
        tok = ("c", eng, self.cnt[eng])
        for f in fns[:-1]:
            self.streams[eng].append(("op", f, None))
        self.streams[eng].append(("op", fns[-1], self.eng_sem[eng]))
        self._commit(tok, r, w)
        return tok

    def dma(self, queue, key, fn, r=(), w=()):
        self._emit_waits(queue, self._deps(r, w))
        s = self._dma_sem(key)
        s[1] += 16
        tok = ("d", key, s[1])
        self.streams[queue].append(("dma", fn, s[0]))
        self._commit(tok, r, w)
        return tok

    def barrier(self):
        toks = [("c", f, self.cnt[f]) for f in ENGS if self.cnt[f] > 0]
        toks += [("d", k, v[1]) for k, v in self.dma_sems.items() if v[1] > 0]
        for e in ENGS:
            self._emit_waits(e, toks)
        self.last_w = {}
        self.readers = {}

    def replay(self, block):
        def run(engine, stream):
            for it in stream:
                if it[0] == "wait":
                    engine.wait_ge(it[1], it[2])
                elif it[0] == "op":
                    ins = it[1](engine)
                    if it[2] is not None:
                        ins.then_inc(it[2], 1)
                else:
                    it[1](engine).then_inc(it[2], 16)

        @block.tensor
        def _(e):
            run(e, self.streams["pe"])

        @block.scalar
        def _(e):
            run(e, self.streams["act"])

        @block.vector
        def _(e):
            run(e, self.streams["dve"])

        @block.gpsimd
        def _(e):
            run(e, self.streams["pool"])

        @block.sync
        def _(e):
            run(e, self.streams["sp"])


def MM(out, lhsT, rhs, start=True, stop=True, skip=False):
    return lambda e: e.matmul(out, lhsT, rhs, start=start, stop=stop, skip_group_check=skip)


def TR(out, in_, idn):
    return lambda e: e.transpose(out, in_, idn)


def ACTF(out, in_, func, scale=1.0, bias=0.0):
    return lambda e: e.activation(out=out, in_=in_, func=func, bias=bias, scale=scale)


def CP(out, in_):
    return lambda e: e.copy(out=out, in_=in_)


def TC(out, in_):
    return lambda e: e.tensor_copy(out=out, in_=in_)


def TT(out, in0, in1, op):
    return lambda e: e.tensor_tensor(out=out, in0=in0, in1=in1, op=op)


def STT(out, in0, scalar, in1, op0=None, op1=None):
    return lambda e: e.scalar_tensor_tensor(out=out, in0=in0, scalar=scalar, in1=in1, op0=op0, op1=op1)


def DMA(out, in_):
    return lambda e: e.dma_start(out=out, in_=in_)


class Arena:
    def __init__(self, t, nbytes):
        self.t = t
        self.nbytes = nbytes
        self.off = 0

    def at(self, off):
        self.off = off

    def alloc(self, shape, dt):
        n = int(np.prod(shape))
        sz = 4 if dt == F32 else 2
        off = (self.off + 63) // 64 * 64
        nb = n * sz
        assert off + nb <= self.nbytes, ("arena overflow", off, nb, self.nbytes)
        self.off = off + nb
        v = self.t[:, off // 2:(off + nb) // 2]
        if dt == F32:
            v = v.bitcast(F32)
        if len(shape) == 2:
            return v.rearrange("p (a b) -> p a b", a=shape[0], b=shape[1])
        if len(shape) == 3:
            return v.rearrange("p (a b c) -> p a b c", a=shape[0], b=shape[1], c=shape[2])
        return v


def _rope_tables():
    half = 32
    inv = (np.float32(10000.0) ** (-(np.arange(half, dtype=np.float32) / np.float32(half)))).astype(np.float32)
    p = np.arange(128)
    pos = np.zeros((128, 49), np.int64)
    for b in range(16):
        pos[:, b] = b * 128 + p
    for r in range(4):
        for mb in range(4):
            pos[:, 16 + r * 4 + mb] = (mb * 128 + p) * 4 + r
    for r in range(16):
        pos[:, 32 + r] = p * 16 + r
    pos[:, 48] = PAST + (p % 4)
    ang = pos.astype(np.float32)[:, :, None] * inv[None, None, :]
    tab = np.stack([np.cos(ang), np.sin(ang)], axis=2).astype(np.float32)
    return np.ascontiguousarray(tab)


def _consts():
    s = np.arange(128)[:, None]
    q = np.arange(128)[None, :]
    masks = np.stack([(s <= q), (s > q)], axis=1).astype(np.float32)
    t = np.arange(4)[None, :]
    sm = np.stack([(s > t), (s > 0) & (t >= 0), (s <= t), (s == t)], axis=1).astype(np.float32)
    return dict(ident=np.eye(128, dtype=np.float32), masks=np.ascontiguousarray(masks),
                smask=np.ascontiguousarray(sm), rope=_rope_tables())


def _prep_weights(w_in, w_br_a, conv_w, conv_b, ln):
    qa, ka, va = w_in[:, 0:512], w_in[:, 512:640], w_in[:, 640:768]
    qb, kb, vb = w_in[:, 768:1536], w_in[:, 1536:2304], w_in[:, 2304:3072]
    hperm = np.concatenate([np.arange(64) + 64 * h for c in range(4) for h in (c, 4 + c)])
    wN = np.concatenate([qa[:, hperm], ka, qb[:, 0:256], kb[:, 0:256], va, vb[:, 0:256]], axis=1)
    wD = np.concatenate([qb[:, 256:512], kb[:, 256:512], vb[:, 256:512],
                         qb[:, 512:768], kb[:, 512:768], vb[:, 512:768]], axis=1)
    wg = w_in[:, 3072:5120]
    return dict(wN=np.ascontiguousarray(wN), wD=np.ascontiguousarray(wD), wg=np.ascontiguousarray(wg),
                wbra=np.ascontiguousarray(w_br_a[hperm, :]),
                convw=np.ascontiguousarray(conv_w.reshape(3, NJ, 128).transpose(2, 1, 0)),
                convb=np.ascontiguousarray(conv_b.reshape(NJ, 128).T), ln=np.ascontiguousarray(ln),
                lncol=np.ascontiguousarray(ln.reshape(4, 8, 128).transpose(2, 0, 1)))


IN_SPECS = [
    ("xp", [SEQ, D]), ("xs", [NTS, D]),
    ("ca", [NS, 128, 256]), ("cb1", [NS, 128, 512]), ("cb2", [NS, 512, 512]), ("cb3", [NS, 2048, 512]),
    ("stc", [NS * 2, DFF]),
    ("wN", [D, 1536]), ("wD", [D, 1536]), ("wg", [D, 2048]), ("wbra", [512, D]), ("wbrb", [256, D]),
    ("wo", [D, D]), ("wup", [D, 2 * DFF]), ("wdn", [DFF, D]),
    ("convw", [128, NJ, 3]), ("convb", [128, NJ]), ("ln", [4, D]), ("lncol", [128, 4, 8]), ("sink", [1, 8]),
    ("ident", [128, 128]), ("masks", [128, 2, 128]), ("smask", [128, 4, 4]), ("rope", [128, 49, 2, 32]),
]
OUT_SPECS = [
    ("y_p", [SEQ, D]), ("y_s", [NTS, D]),
    ("ca_p", [128, 256]), ("ca_s", [NTS, 256]), ("cb1_p", [128, 512]), ("cb1_s", [NTS, 512]),
    ("cb2_p", [512, 512]), ("cb2_s", [NTS, 512]), ("cb3_p", [2048, 512]), ("cb3_s", [NTS, 512]),
    ("sc_p", [2, DFF]), ("sc_s", [NS * 2, DFF]),
]

PHASES = os.environ.get("MK_PHASES", "12345")
SUB = os.environ.get("MK_SUB", "abs")


def build():
    nc = bass.Bass("TRN2", target_bir_lowering=False)
    I = {n: nc.dram_tensor(n, s, F32, kind="ExternalInput").ap() for n, s in IN_SPECS}
    O = {n: nc.dram_tensor(n, s, F32, kind="ExternalOutput").ap() for n, s in OUT_SPECS}
    h_scr = nc.dram_tensor("h_scr", [NT, D], F32).ap()

    with ExitStack() as st:
        P = Prog(nc, st)
        ARENA_BYTES = 204 * 1024
        arena_t = st.enter_context(nc.sbuf_tensor("arena", [128, ARENA_BYTES // 2], BF16))
        A = Arena(arena_t, ARENA_BYTES)
        sb = lambda name, shape, dt: st.enter_context(nc.sbuf_tensor(name, shape, dt))
        ident = sb("ident_sb", [128, 128], F32)
        idb = sb("idb", [128, 128], BF16)
        mask_f = sb("mask_f", [128, 2, 128], F32)
        mask_b = sb("mask_b", [128, 2, 128], BF16)
        smask_f = sb("smask_f", [128, 4, 4], F32)
        smask_b = sb("smask_b", [128, 4, 4], BF16)
        ones_b = sb("ones_b", [128, 128], BF16)
        sink_f = sb("sink_f", [1, 8], F32)
        PS = st.enter_context(nc.psum_tensor("psum", [128, 4096], F32))

        def bank(b, n=512, o=0, p0=0, p1=128):
            return PS[p0:p1, b * 512 + o:b * 512 + o + n]

        def bank_bf(b):
            return PS[:, b * 512:(b + 1) * 512].bitcast(BF16)

        pb = lambda *bs: ["ps%d" % b for b in bs]

        P.dma("sp", "ident", lambda e: e.dma_start(out=ident[:], in_=I["ident"][:, :]), w=["ident"])
        P.dma("sp", "mask_f", lambda e: e.dma_start(out=mask_f[:], in_=I["masks"][:, :, :]), w=["mask_f"])
        P.dma("sp", "smask_f", lambda e: e.dma_start(out=smask_f[:], in_=I["smask"][:, :, :]), w=["smask_f"])
        P.dma("sp", "sink_f", lambda e: e.dma_start(out=sink_f[:], in_=I["sink"][:, :]), w=["sink_f"])
        P.op("pool", lambda e: e.tensor_copy(out=idb[:], in_=ident[:]), r=["ident"], w=["idb"])
        P.op("pool", lambda e: e.tensor_copy(out=mask_b[:], in_=mask_f[:]), r=["mask_f"], w=["mask_b"])
        P.op("pool", lambda e: e.tensor_copy(out=smask_b[:], in_=smask_f[:]), r=["smask_f"], w=["smask_b"])
        P.op("pool", lambda e: e.memset(ones_b[:], 1.0), w=["ones_b"])
        P.op("act", lambda e: e.activation(out=sink_f[:], in_=sink_f[:], func=AF.Exp), r=["sink_f"], w=["sink_f"])

        A.at(0)
        QKN = A.alloc([9, SEQ], BF16)
        QKD4 = A.alloc([4, SEQ], BF16)
        QKD16 = A.alloc([4, SEQ], BF16)
        VN = A.alloc([NB, 384], BF16)
        VD4 = A.alloc([NB, 256], BF16)
        VD16 = A.alloc([NB, 256], BF16)
        OFF_QKV_END = A.off
        QS = A.alloc([17, NTS], BF16)
        OFF_P = A.off
        oaT = A.alloc([4, NT], BF16)
        obT = A.alloc([2, NT], BF16)
        accN = A.alloc([2, NT], F32)
        accD = A.alloc([2, NT], F32)
        OFF_P2T = A.off

        if "1" in PHASES:
            A.at(OFF_P)
            xT = A.alloc([8, NT], BF16)
            Wb = A.alloc([8, 1536], BF16)
            rope = A.alloc([49, 64], F32)
            xst = [A.alloc([1, D], F32) for _ in range(2)]
            xbf = [A.alloc([1, D], BF16) for _ in range(2)]
            R = [A.alloc([1, 1536], F32) for _ in range(2)]
            Bt = A.alloc([1, 1152], F32)
            Rb = [A.alloc([1, 1152], BF16) for _ in range(3)]

            P.dma("sp", "rope", lambda e: e.dma_start(out=rope[:, :, :], in_=I["rope"].rearrange("p b c d -> p b (c d)")), w=["rope"])
            for c3 in range(3):
                P.dma("pool", "Wb%d" % c3, lambda e, c3=c3: e.dma_start(
                    out=Wb[:, :, c3 * 512:(c3 + 1) * 512],
                    in_=I["wN"].rearrange("(kc p) n -> p kc n", p=128)[:, :, c3 * 512:(c3 + 1) * 512]), w=["Wb%d" % c3])
            for tb in range(NB + 1):
                ntok = 128 if tb < NB else NTS
                src = I["xp"][tb * 128:(tb + 1) * 128, :] if tb < NB else I["xs"][:, :]
                xs_ = xst[tb % 2]
                P.dma("sp" if tb % 2 == 0 else "act", "xst%d" % (tb % 2),
                      lambda e, xs_=xs_, src=src, ntok=ntok: e.dma_start(out=xs_[0:ntok, 0, :], in_=src), w=["xst%d" % (tb % 2)])
                xb_ = xbf[tb % 2]
                P.op("act", CP(xb_[0:ntok, 0, :], xs_[0:ntok, 0, :]), r=["xst%d" % (tb % 2)], w=["xbf%d" % (tb % 2)])
                bk = 6 + tb % 2
                P.op("pe", [TR(bank_bf(bk)[:, kc * 128:kc * 128 + ntok], xb_[0:ntok, 0, kc * 128:(kc + 1) * 128], idb[0:ntok, 0:ntok])
                            for kc in range(8)], r=["xbf%d" % (tb % 2), "idb"], w=pb(bk))
                src_v = bank_bf(bk).rearrange("p (a b) -> p a b", b=128)[:, :, 0:ntok]
                dst_v = xT[:, 0:8, tb * 128:tb * 128 + ntok]
                P.op("dve", TC(dst_v, src_v), r=pb(bk), w=["xT%da" % tb, "xT%db" % tb])

            def xcols(kind, i):
                if kind == "N":
                    return lambda kc: xT[:, kc, i * 128:(i + 1) * 128]
                if kind == "S":
                    return lambda kc: xT[:, kc, SEQ:SEQ + NTS]
                if kind == "D4":
                    r_, mb = i // 4, i % 4
                    return lambda kc: xT[:, kc, mb * 512:(mb + 1) * 512].rearrange("p (m r) -> p r m", r=4)[:, r_, :]
                return lambda kc: xT[:, kc, 0:SEQ].rearrange("p (m r) -> p r m", r=16)[:, i, :]

            xn = lambda tbs: [("xT%d" % t) + h for t in tbs for h in "ab"]
            jobs = []
            for b in range(NB):
                outs = []
                if b == NB - 1:
                    outs = [(O["ca_p"][:, 0:128], 512, 128), (O["ca_p"][:, 128:256], 1152, 128),
                            (O["cb1_p"][:, 0:256], 896, 256), (O["cb1_p"][:, 256:512], 1280, 256)]
                jobs.append(dict(x=xcols("N", b), ntok=128, c0=0, nc_=1536, nqk=1152, ti=b,
                                 qk=[(QKN[:, 0:9, b * 128:(b + 1) * 128], 0, 9)],
                                 v=[(VN[:, b, :], 1152, 384)], outs=outs, wsel="N", xr=xn([b])))
            jobs.append(dict(x=xcols("S", 0), ntok=NTS, c0=0, nc_=1536, nqk=1152, ti=48,
                             qk=[(QS[:, 0:9, :], 0, 9)], v=[],
                             outs=[(O["ca_s"][:, 0:128], 512, 128), (O["ca_s"][:, 128:256], 1152, 128),
                                   (O["cb1_s"][:, 0:256], 896, 256), (O["cb1_s"][:, 256:512], 1280, 256)], wsel="N", xr=xn([NB])))
            for i in range(16):
                r_, mb = i // 4, i % 4
                outs = []
                if mb == 3:
                    dv = O["cb2_p"].rearrange("(p r) c -> r p c", r=4)
                    outs = [(dv[r_, :, 0:256], 256, 256), (dv[r_, :, 256:512], 512, 256)]
                jobs.append(dict(x=xcols("D4", i), ntok=128, c0=0, nc_=768, nqk=512, ti=16 + i,
                                 qk=[(QKD4[:, 0:4, i * 128:(i + 1) * 128], 0, 4)],
                                 v=[(VD4[:, i, :], 512, 256)], outs=outs, wsel="D", xr=xn(range(4 * mb, 4 * mb + 4))))
            for i in range(16):
                dv = O["cb3_p"].rearrange("(p r) c -> r p c", r=16)
                jobs.append(dict(x=xcols("D16", i), ntok=128, c0=768, nc_=768, nqk=512, ti=32 + i,
                                 qk=[(QKD16[:, 0:4, i * 128:(i + 1) * 128], 0, 4)],
                                 v=[(VD16[:, i, :], 512, 256)],
                                 outs=[(dv[i, :, 0:256], 256, 256), (dv[i, :, 256:512], 512, 256)], wsel="D", xr=xn(range(NB))))
            jobs.append(dict(x=xcols("S", 0), ntok=NTS, c0=0, nc_=1536, nqk=None, ti=48,
                             ropes=[(0, 512), (768, 512)],
                             qk=[(QS[:, 9:13, :], 0, 4), (QS[:, 13:17, :], 768, 4)], v=[],
                             outs=[(O["cb2_s"][:, 0:256], 256, 256), (O["cb2_s"][:, 256:512], 512, 256),
                                   (O["cb3_s"][:, 0:256], 1024, 256), (O["cb3_s"][:, 256:512], 1280, 256)], wsel="D", xr=xn([NB])))

            def emit_mm(j, jb):
                if jb["wsel"] == "D" and jobs[j - 1]["wsel"] == "N":
                    for c3 in range(3):
                        P.dma("pool", "Wb%d" % c3, lambda e, c3=c3: e.dma_start(
                            out=Wb[:, :, c3 * 512:(c3 + 1) * 512],
                            in_=I["wD"].rearrange("(kc p) n -> p kc n", p=128)[:, :, c3 * 512:(c3 + 1) * 512]), w=["Wb%d" % c3])
                s = j % 2
                ntok, c0, ncol = jb["ntok"], jb["c0"], jb["nc_"]
                for cc in range(0, ncol, 512):
                    n = min(512, ncol - cc)
                    bk = 3 * s + cc // 512
                    wnames = sorted(set(["Wb%d" % ((c0 + cc) // 512), "Wb%d" % ((c0 + cc + n - 1) // 512)]))
                    P.op("pe", [(lambda e, kc=kc, bk=bk, n=n, cc=cc: e.matmul(
                        bank(bk, n, 0, 0, ntok), jb["x"](kc), Wb[:, kc, c0 + cc:c0 + cc + n], start=(kc == 0), stop=(kc == 7)))
                        for kc in range(8)], r=jb["xr"] + wnames, w=pb(bk))

            def emit_rope(j, jb):
                s = j % 2
                ntok, ncol = jb["ntok"], jb["nc_"]
                banks = pb(*range(3 * s, 3 * s + (ncol + 511) // 512))
                psj = PS[0:ntok, 3 * s * 512:3 * s * 512 + ncol]
                Rj, Rbj = R[s], Rb[j % 3]
                cosb = rope[0:ntok, jb["ti"], 0:32]
                sinb = rope[0:ntok, jb["ti"], 32:64]
                ropes = jb.get("ropes") or [(0, jb["nqk"])]
                for (r0, rn) in ropes:
                    nh = rn // 64
                    pv = psj[:, r0:r0 + rn].rearrange("p (h two d) -> p h two d", two=2, d=32)
                    Rv = Rj[0:ntok, 0, r0:r0 + rn].rearrange("p (h two d) -> p h two d", two=2, d=32)
                    Bv = Bt[0:ntok, 0, 0:rn].rearrange("p (h two d) -> p h two d", two=2, d=32)
                    cos4 = cosb.unsqueeze(1).unsqueeze(1).to_broadcast([ntok, nh, 2, 32])
                    sin3 = sinb.unsqueeze(1).to_broadcast([ntok, nh, 32])
                    P.op("dve", lambda e, Rv=Rv, pv=pv, cos4=cos4: e.tensor_tensor(out=Rv, in0=pv, in1=cos4, op=ALU.mult),
                         r=banks + ["rope"], w=["R%d" % s])
                    P.op("dve", lambda e, Bv=Bv, pv=pv, sin3=sin3: e.scalar_tensor_tensor(
                        out=Bv[:, :, 0, :], in0=pv[:, :, 1, :], scalar=-1.0, in1=sin3, op0=ALU.mult, op1=ALU.mult),
                        r=banks + ["rope"], w=["Bt"])
                    P.op("dve", lambda e, Bv=Bv, pv=pv, sin3=sin3: e.tensor_tensor(
                        out=Bv[:, :, 1, :], in0=pv[:, :, 0, :], in1=sin3, op=ALU.mult), r=banks + ["rope"], w=["Bt"])
                    P.op("dve", TT(Rj[0:ntok, 0, r0:r0 + rn], Rj[0:ntok, 0, r0:r0 + rn], Bt[0:ntok, 0, 0:rn], ALU.add), r=["R%d" % s, "Bt"], w=["R%d" % s])
                if jb.get("ropes"):
                    vranges = [(512, 256), (1280, 256)]
                else:
                    vranges = [(jb["nqk"], ncol - jb["nqk"])]
                for (v0, vn) in vranges:
                    P.op("act", lambda e, d=Rj[0:ntok, 0, v0:v0 + vn], s_=psj[:, v0:v0 + vn]: e.copy(out=d, in_=s_),
                         r=banks, w=["RV%d" % s])
                for (dst, v0, vn) in jb["v"]:
                    P.op("act", lambda e, d=dst, s_=psj[:, v0:v0 + vn]: e.copy(out=d, in_=s_), r=banks)
                for (r0, rn) in ropes:
                    P.op("act", CP(Rbj[0:ntok, 0, jb_rb_off(jb, r0):jb_rb_off(jb, r0) + rn], Rj[0:ntok, 0, r0:r0 + rn]), r=["R%d" % s], w=["Rb%d" % (j % 3)])
                for k, (dram, c_, n_) in enumerate(jb["outs"]):
                    P.dma("sp", "R%d_o%d" % (s, k), lambda e, dram=dram, src=Rj[0:ntok, 0, c_:c_ + n_]: e.dma_start(out=dram, in_=src),
                          r=["R%d" % s, "RV%d" % s])

            def jb_rb_off(jb, r0):
                if jb.get("ropes"):
                    return 0 if r0 == 0 else 512
                return r0

            def emit_tr(j, jb):
                s = j % 3
                ntok = jb["ntok"]
                Rbj = Rb[s]
                pos = 0
                for (dst, c_, nch) in jb["qk"]:
                    rb0 = jb_rb_off(jb, c_) if jb.get("ropes") else c_
                    slots = list(range(pos, pos + nch))
                    for bk in (6, 7):
                        sl = [q_ for q_ in slots if q_ // 8 == bk - 6]
                        if not sl:
                            continue
                        fns = []
                        for q_ in sl:
                            o_ap = bank_bf(bk)[:, (q_ % 8) * 128:(q_ % 8) * 128 + ntok]
                            i_ap = Rbj[0:ntok, 0, rb0 + (q_ - pos) * 128:rb0 + (q_ - pos + 1) * 128]
                            fns.append(lambda e, o_ap=o_ap, i_ap=i_ap, id_ap=idb[0:ntok, 0:ntok]: e.transpose(o_ap, i_ap, id_ap))
                        P.op("pe", fns, r=["Rb%d" % s, "idb"], w=pb(bk))
                        srcv = bank_bf(bk).rearrange("p (a b) -> p a b", b=128)[:, sl[0] % 8:sl[-1] % 8 + 1, 0:ntok]
                        dstv = dst[:, sl[0] - pos:sl[-1] - pos + 1, :]
                        P.op("act", lambda e, d=dstv, s_=srcv: e.copy(out=d, in_=s_), r=pb(bk))
                    pos += nch

            for j in range(len(jobs) + 2):
                if j < len(jobs):
                    emit_mm(j, jobs[j])
                if j >= 2:
                    emit_tr(j - 2, jobs[j - 2])
                if j < len(jobs):
                    emit_rope(j, jobs[j])
            P.barrier()

        MUL, ADD = ALU.mult, ALU.add

        if "2" in PHASES:
            A.at(OFF_P2T)
            PT = [[A.alloc([1, 512], BF16) for _ in range(4)] for _ in range(2)]
            rec = [A.alloc([1, 512], F32) for _ in range(2)]
            rsc = [A.alloc([1, 512], F32) for _ in range(2)]
            negone = A.alloc([1, 512], F32)
            P.op("pool", lambda e: e.memset(negone[:, 0, :], -1.0), w=["negone"])
            sink_row = A.alloc([8, 128], BF16)
            P.op("dve", TC(sink_row[0:1, :, :], sink_f[0:1, :].unsqueeze(2).to_broadcast([1, 8, 128])), r=["sink_f"], w=["sink_row"])
            mbA = A.alloc([2, 512], BF16)
            mbB = A.alloc([4, 128], BF16)
            for wh in range(2):
                P.op("dve", lambda e, wh=wh: e.tensor_scalar(
                    out=mbA[:, wh, :].rearrange("p (h q) -> p h q", q=128), in0=mask_f[:, wh, :].unsqueeze(1).to_broadcast([128, 4, 128]),
                    scalar1=-1.0, scalar2=30000.0, op0=ALU.add, op1=ALU.mult), r=["mask_f"], w=["mbA"])
                P.op("dve", lambda e, wh=wh: e.tensor_scalar(
                    out=mbB[:, 2 * wh:2 * wh + 2, :], in0=mask_f[:, wh, :].unsqueeze(1).to_broadcast([128, 2, 128]),
                    scalar1=-1.0, scalar2=30000.0, op0=ALU.add, op1=ALU.mult), r=["mask_f"], w=["mbB"])

            def a_scores(b):
                par = b % 2
                whs = [0] + ([1] if b > 0 else [])
                items = [(kvh, wh) for wh in whs for kvh in range(2)]
                P.op("pe", [MM(bank(kvh * 2 + wh), idb[:, :], mbA[:, wh, :], start=True, stop=False) for (kvh, wh) in items],
                     r=["mbA", "idb"], w=pb(*[kvh * 2 + wh for (kvh, wh) in items]))
                for (kvh, wh) in items:
                    p0, p1 = kvh * 64, kvh * 64 + 64
                    kb = b - wh
                    bk = kvh * 2 + wh
                    P.op("pe", MM(bank(bk), QKN[p0:p1, 4, kb * 128:(kb + 1) * 128], QKN[p0:p1, 0:4, b * 128:(b + 1) * 128], start=False, stop=True),
                         w=pb(bk))
                for (kvh, wh) in items:
                    bk = kvh * 2 + wh
                    nm = "PT%d_%d" % (par, kvh * 2 + wh)
                    P.op("act", ACTF(PT[par][kvh * 2 + wh][:, 0, :], bank(bk), AF.Exp, scale=SCALE), r=pb(bk), w=[nm])

            def a_values(b):
                par = b % 2
                bo, bd = (4, 5) if par == 0 else (6, 7)
                lst = [(0, b)] + ([(1, b - 1)] if b > 0 else [])
                nms = ["PT%d_%d" % (par, kvh * 2 + wh) for wh, _ in lst for kvh in range(2)]
                fo, fd = [], []
                for i, (wh, kb) in enumerate(lst):
                    for kvh in range(2):
                        p0, p1 = kvh * 64, kvh * 64 + 64
                        pt = PT[par][kvh * 2 + wh][:, 0, :]
                        fo.append(MM(bank(bo, 512, 0, p0, p1), VN[:, kb, kvh * 64:(kvh + 1) * 64], pt, start=(i == 0), stop=(i == len(lst) - 1)))
                for i, (wh, kb) in enumerate(lst):
                    for kvh in range(2):
                        p0, p1 = kvh * 64, kvh * 64 + 64
                        pt = PT[par][kvh * 2 + wh][:, 0, :]
                        fd.append(MM(bank(bd, 512, 0, p0, p1), ones_b[:, 0:64], pt, start=(i == 0), stop=False))
                for kvh in range(2):
                    p0, p1 = kvh * 64, kvh * 64 + 64
                    fd.append(MM(bank(bd, 512, 0, p0, p1), ones_b[0:1, 0:64], sink_row[0:1, kvh * 4:(kvh + 1) * 4, :], start=False, stop=True))
                P.op("pe", fo, r=nms, w=pb(bo))
                P.op("pe", fd, r=nms + ["ones_b", "sink_row"], w=pb(bd))
                rc = rec[par]
                ob_ = rsc[par]
                P.op("act", CP(rc[:, 0, :], bank(bd)), r=pb(bd), w=["rec%d" % par])
                P.op("dve", TC(ob_[:, 0, :], bank(bo)), r=pb(bo), w=["rsc%d" % par])
                P.op("pool", TT(rc[:, 0, :], rc[:, 0, :], negone[:, 0, :], ALU.pow), r=["rec%d" % par, "negone"], w=["rec%d" % par])
                P.op("pool", TT(oaT[:, 0:4, b * 128:(b + 1) * 128], ob_[:, 0, :].rearrange("p (h q) -> p h q", q=128),
                                rc[:, 0, :].rearrange("p (h q) -> p h q", q=128), MUL), r=["rec%d" % par, "rsc%d" % par])

            if "a" in SUB:
                a_scores(0)
                for b in range(NB):
                    if b + 1 < NB:
                        a_scores(b + 1)
                    a_values(b)

            blocks = []
            for i in range(NB):
                blocks.append(dict(g=0, QK=QKN, qc=5, kc=7, V=VN, vo=128, i=i, prev=(i - 1 if i > 0 else None),
                                   dst=lambda T_, i=i: T_[:, :, i * 128:(i + 1) * 128], span=[i // 4]))
            for i in range(16):
                r_, mb = i // 4, i % 4
                blocks.append(dict(g=1, QK=QKD4, qc=0, kc=2, V=VD4, vo=0, i=i, prev=(i - 1 if mb > 0 else None),
                                   dst=lambda T_, r_=r_, mb=mb: T_[:, :, mb * 512:(mb + 1) * 512].rearrange("p c (m r) -> p c r m", r=4)[:, :, r_, :],
                                   span=[mb]))
            for i in range(16):
                blocks.append(dict(g=2, QK=QKD16, qc=0, kc=2, V=VD16, vo=0, i=i, prev=None,
                                   dst=lambda T_, i=i: T_[:, :, 0:SEQ].rearrange("p c (m r) -> p c r m", r=16)[:, :, i, :],
                                   span=[0, 1, 2, 3]))

            def b_scores(k, bl):
                par = k % 2
                lst = [(0, bl["i"])] + ([(1, bl["prev"])] if bl["prev"] is not None else [])
                ncol = 256 * len(lst)
                bks = [2 * par, 2 * par + 1]
                P.op("pe", [MM(bank(bk, ncol), idb[:, :], mbB[:, 0:2 * len(lst), :], start=True, stop=False) for bk in bks],
                     r=["mbB", "idb"], w=pb(*bks))
                fns = []
                for wi, (wh, kb) in enumerate(lst):
                    for c in range(2):
                        for half in range(2):
                            p0 = half * 64
                            fns.append(MM(bank(bks[half], 128, wh * 256 + c * 128), bl["QK"][p0:p0 + 64, bl["kc"] + c, kb * 128:(kb + 1) * 128],
                                          bl["QK"][p0:p0 + 64, bl["qc"] + c, bl["i"] * 128:(bl["i"] + 1) * 128],
                                          start=False, stop=(wi == len(lst) - 1 and c == 1)))
                P.op("pe", fns, w=pb(*bks))
                for half in range(2):
                    nm = "PT%d_%d" % (par, half)
                    P.op("act", ACTF(PT[par][half][:, 0, 0:ncol], bank(bks[half], ncol), AF.Exp, scale=SCALE), r=pb(bks[half]), w=[nm])

            def b_values(k, bl):
                par = k % 2
                bo, bd = (4, 5) if par == 0 else (6, 7)
                lst = [(0, bl["i"])] + ([(1, bl["prev"])] if bl["prev"] is not None else [])
                nms = ["PT%d_%d" % (par, hf_) for hf_ in range(2)]
                fo, fd = [], []
                for c in range(2):
                    for i_, (wh, kb) in enumerate(lst):
                        for hf_ in range(2):
                            h, p0 = 2 * c + hf_, hf_ * 64
                            pt = PT[par][hf_][:, 0, wh * 256 + c * 128:wh * 256 + (c + 1) * 128]
                            fo.append(MM(bank(bo, 128, c * 128, p0, p0 + 64), bl["V"][:, kb, bl["vo"] + h * 64:bl["vo"] + (h + 1) * 64], pt,
                                         start=(i_ == 0), stop=(i_ == len(lst) - 1)))
                for c in range(2):
                    for i_, (wh, kb) in enumerate(lst):
                        for hf_ in range(2):
                            p0 = hf_ * 64
                            pt = PT[par][hf_][:, 0, wh * 256 + c * 128:wh * 256 + (c + 1) * 128]
                            fd.append(MM(bank(bd, 128, c * 128, p0, p0 + 64), ones_b[:, 0:64], pt, start=(i_ == 0), stop=(i_ == len(lst) - 1)))
                P.op("pe", fo, r=nms, w=pb(bo))
                P.op("pe", fd, r=nms + ["ones_b"], w=pb(bd))
                names = ["acc%d" % sp for sp in bl["span"]]
                srcO = bank(bo, 256).rearrange("p (c q) -> p c q", q=128)
                srcD = bank(bd, 256).rearrange("p (c q) -> p c q", q=128)
                if bl["g"] == 0:
                    P.op("act", CP(bl["dst"](accN), srcO), r=pb(bo), w=names, ss=True)
                    P.op("act", CP(bl["dst"](accD), srcD), r=pb(bd), w=names, ss=True)
                else:
                    P.op("dve", TT(bl["dst"](accN), srcO, bl["dst"](accN), ADD), r=pb(bo), w=names, ss=(bl["g"] != 2))
                    P.op("dve", TT(bl["dst"](accD), srcD, bl["dst"](accD), ADD), r=pb(bd), w=names, ss=(bl["g"] != 2))

            if "b" in SUB:
                b_scores(0, blocks[0])
                for k in range(len(blocks)):
                    if k + 1 < len(blocks):
                        b_scores(k + 1, blocks[k + 1])
                    b_values(k, blocks[k])
            P.barrier()


        P3W_LO = 84224
        P3W = {"issued": False}

        def p3w():
            A.at(P3W_LO)
            Wbra_ = A.alloc([4, D], BF16)
            Wbrb_ = A.alloc([2, D], BF16)
            assert A.off <= OFF_QKV_END
            A.at(OFF_P2T)
            Wg_ = A.alloc([8, 2048], BF16)
            Wo_ = A.alloc([8, D], BF16)
            return Wg_, Wbra_, Wbrb_, Wo_

        def p3w_issue():
            P3W["issued"] = True
            Wg, Wbra, Wbrb, Wo = p3w()
            wgv = I["wg"].rearrange("(kc p) n -> p kc n", p=128)
            def _wg(c4):
                P.dma("pool", "Wg%d" % c4, DMA(Wg[:, :, c4 * 512:(c4 + 1) * 512], wgv[:, :, c4 * 512:(c4 + 1) * 512]), w=["Wg%d" % c4])

            def _wbr(c2):
                P.dma("pool", "Wbra%d" % c2, DMA(Wbra[:, :, c2 * 512:(c2 + 1) * 512],
                                                I["wbra"].rearrange("(kc p) n -> p kc n", p=128)[:, :, c2 * 512:(c2 + 1) * 512]), w=["Wbra%d" % c2])
                P.dma("pool", "Wbrb%d" % c2, DMA(Wbrb[:, :, c2 * 512:(c2 + 1) * 512],
                                                I["wbrb"].rearrange("(kc p) n -> p kc n", p=128)[:, :, c2 * 512:(c2 + 1) * 512]), w=["Wbrb%d" % c2])

            _wg(0); _wg(2); _wbr(0); _wg(1); _wg(3); _wbr(1)
            for c2 in range(2):
                P.dma("pool", "Wo%d" % c2, DMA(Wo[:, :, c2 * 512:(c2 + 1) * 512],
                                              I["wo"].rearrange("(kc p) n -> p kc n", p=128)[:, :, c2 * 512:(c2 + 1) * 512]), w=["Wo%d" % c2])

        if "2" in PHASES and "s" in SUB:
            A.at(0)
            Cs = [A.alloc([4, 512], F32) for _ in range(3)]
            Nw = [A.alloc([1, 512], F32) for _ in range(2)]
            Kb = [A.alloc([4, 256], BF16) for _ in range(2)]
            Knb = [A.alloc([1, 256], BF16) for _ in range(2)]
            KTs = [A.alloc([1, 1088], BF16) for _ in range(2)]
            Vb = A.alloc([NS, 1024], BF16)
            Vnb = A.alloc([NS, 256], BF16)
            PTc = A.alloc([2, 256], BF16)
            PTn = A.alloc([2, 256], BF16)
            recs = A.alloc([1, 256], F32)
            prods = A.alloc([1, 256], F32)
            sink2 = A.alloc([2, 256], BF16)
            assert A.off <= P3W_LO, (A.off, P3W_LO)
            if "3" in PHASES:
                p3w_issue()
            P.op("dve", TC(sink2[0:1, :, :].rearrange("p k (n c) -> p k n c", c=4),
                           sink_f[0:1, :].rearrange("p (k c) -> p k c", c=4).unsqueeze(2).to_broadcast([1, 2, NTS, 4])), r=["sink_f"], w=["sink2"])
            sgroups = [dict(nm="A", cache=I["ca"], new=O["ca_s"], HD=128, dil=0, qb=0, nh2=4, mk=(0, 2)),
                       dict(nm="b1", cache=I["cb1"], new=O["cb1_s"], HD=256, dil=0, qb=5, nh2=2, mk=(0, 2)),
                       dict(nm="b2", cache=I["cb2"], new=O["cb2_s"], HD=256, dil=4, qb=9, nh2=2, mk=(1, 3)),
                       dict(nm="b3", cache=I["cb3"], new=O["cb3_s"], HD=256, dil=16, qb=13, nh2=2, mk=(1, 3))]
            cnt = 0
            for gi, G in enumerate(sgroups):
                HD, dil, nh2, qb = G["HD"], G["dil"], G["nh2"], G["qb"]
                isA = G["nm"] == "A"
                T = 1 if dil == 0 else 4
                nch = HD // 128
                nc2 = NS * 4 * nh2
                Sc = [bank(rg, nc2).rearrange("p (n t h) -> p n t h", t=4, h=nh2) for rg in range(2)]
                Sn = [bank(2 + rg, nc2).rearrange("p (n t h) -> p n t h", t=4, h=nh2) for rg in range(2)]
                base = cnt
                cnt += NS

                def s_load(n):
                    q3, par = (base + n) % 3, (base + n) % 2
                    C_, N_ = Cs[q3], Nw[par]
                    cn, nn = "C%d" % q3, "Nw%d" % par
                    if dil == 0:
                        P.dma("sp", cn, DMA(C_[:, 0, 0:2 * HD], G["cache"][n, :, :]), w=[cn])
                    else:
                        P.dma("sp", cn, DMA(C_[:, 0:4, :], G["cache"][n].rearrange("(m r) c -> m r c", r=dil)[:, 0:4, :]), w=[cn])
                    P.dma("act", nn, DMA(N_[0:4, 0, 0:2 * HD], G["new"][n * 4:(n + 1) * 4, :]), w=[nn])

                def s_pre(n):
                    q3, par = (base + n) % 3, (base + n) % 2
                    C_, N_, K_, Kn_, KT_ = Cs[q3], Nw[par], Kb[par], Knb[par], KTs[par]
                    cn, nn, kn, knn, ktn = "C%d" % q3, "Nw%d" % par, "Kb%d" % par, "Knb%d" % par, "KTs%d" % par
                    P.op("dve", TC(K_[:, 0:T, 0:HD], C_[:, 0:T, 0:HD]), r=[cn], w=[kn])
                    P.op("dve", TC(Vb[:, n, 0:T * HD].rearrange("p (t d) -> p t d", d=HD), C_[:, 0:T, HD:2 * HD]), r=[cn], w=["Vb"], ss=True)
                    P.op("act", CP(Kn_[0:4, 0, 0:HD], N_[0:4, 0, 0:HD]), r=[nn], w=[knn])
                    P.op("act", CP(Vnb[0:4, n, 0:HD], N_[0:4, 0, HD:2 * HD]), r=[nn], w=["Vnb"], ss=True)
                    bk = 6 + par
                    fns = []
                    for tau in range(T):
                        for ch in range(nch):
                            sl = tau * nch + ch
                            fns.append(TR(bank_bf(bk)[:, sl * 128:(sl + 1) * 128], K_[:, tau, ch * 128:(ch + 1) * 128], idb[:, :]))
                    P.op("pe", fns, r=[kn, "idb"], w=pb(bk))
                    P.op("pe", [TR(bank_bf(5)[:, par * 64 + ch * 4:par * 64 + ch * 4 + 4], Kn_[0:4, 0, ch * 128:(ch + 1) * 128], idb[0:4, 0:4])
                                for ch in range(nch)], r=[knn, "idb"], w=pb(5))
                    P.op("act", CP(KT_[:, 0, 0:T * nch * 128], bank_bf(bk)[:, 0:T * nch * 128]), r=pb(bk), w=[ktn])
                    P.op("act", CP(KT_[:, 0, 1024:1024 + nch * 4], bank_bf(5)[:, par * 64:par * 64 + nch * 4]), r=pb(5), w=[ktn])

                def s_scores(n):
                    par = (base + n) % 2
                    KT_, ktn = KTs[par], "KTs%d" % par
                    fl = {0: ([], []), 1: ([], [])}
                    for rg in range(2):
                        p0 = rg * 64
                        fns, fnn = fl[rg]
                        for hh in range(nh2):
                            kcol = 0 if isA else hh
                            qch = hh if isA else qb + hh
                            q_all = QS[p0:p0 + 64, qch, n * 4:(n + 1) * 4]
                            if dil == 0:
                                fns.append(MM(Sc[rg][:, n, :, hh], KT_[p0:p0 + 64, 0, kcol * 128:(kcol + 1) * 128], q_all))
                            else:
                                for t in range(4):
                                    fns.append(MM(Sc[rg][:, n, t, hh:hh + 1], KT_[p0:p0 + 64, 0, (t * 2 + kcol) * 128:(t * 2 + kcol + 1) * 128],
                                                  QS[p0:p0 + 64, qch, n * 4 + t:n * 4 + t + 1]))
                            fnn.append(MM(Sn[rg][0:4, n, :, hh], KT_[p0:p0 + 64, 0, 1024 + kcol * 4:1024 + kcol * 4 + 4], q_all))
                    mix_ = lambda a, b_: [x_ for pr in zip(a, b_) for x_ in pr]
                    P.op("pe", mix_(fl[0][0], fl[1][0]), r=[ktn], w=pb(0, 1))
                    P.op("pe", mix_(fl[0][1], fl[1][1]), r=[ktn], w=pb(2, 3))

                s_load(0)
                s_load(1)
                s_pre(0)
                for n in range(NS):
                    if n + 2 < NS:
                        s_load(n + 2)
                    if n + 1 < NS:
                        s_pre(n + 1)
                    s_scores(n)
                    if gi == 0 and "b" in SUB:
                        sl_ = slice(n * 128, (n + 1) * 128)
                        P.op("dve", lambda e, sl_=sl_: e.reciprocal(out=accD[:, :, sl_], in_=accD[:, :, sl_]), w=["obfin"], ss=True)
                        P.op("dve", TT(obT[:, :, sl_], accN[:, :, sl_], accD[:, :, sl_], MUL), r=["obfin"], w=["obfin2"])
                mc, mn = G["mk"]
                for rg in range(2):
                    P.op("act", ACTF(PTc[:, rg, 0:nc2], bank(rg, nc2), AF.Exp, scale=SCALE), r=pb(rg), w=["PTc%d" % rg])
                    P.op("act", ACTF(PTn[0:4, rg, 0:nc2], bank(2 + rg, nc2, 0, 0, 4), AF.Exp, scale=SCALE), r=pb(2 + rg), w=["PTn%d" % rg])
                    pcv = PTc[:, rg, 0:nc2].rearrange("p (n t h) -> p n t h", t=4, h=nh2)
                    pnv = PTn[0:4, rg, 0:nc2].rearrange("p (n t h) -> p n t h", t=4, h=nh2)
                    P.op("pool", TT(pcv, pcv, smask_b[:, mc, :].unsqueeze(1).unsqueeze(3).to_broadcast([128, NS, 4, nh2]), MUL),
                         r=["PTc%d" % rg, "smask_b"], w=["PTc%d" % rg])
                    P.op("pool", TT(pnv, pnv, smask_b[0:4, mn, :].unsqueeze(1).unsqueeze(3).to_broadcast([4, NS, 4, nh2]), MUL),
                         r=["PTn%d" % rg, "smask_b"], w=["PTn%d" % rg])
                ptn = ["PTc0", "PTc1", "PTn0", "PTn1"]
                fo_l, fd_l = {0: [], 1: []}, {0: [], 1: []}
                for rg in range(2):
                    fo, fd = fo_l[rg], fd_l[rg]
                    p0, p1 = rg * 64, rg * 64 + 64
                    pc = PTc[:, rg, 0:nc2].rearrange("p (n t h) -> p n t h", t=4, h=nh2)
                    pn = PTn[0:4, rg, 0:nc2].rearrange("p (n t h) -> p n t h", t=4, h=nh2)
                    if isA:
                        fd.append(MM(bank(0, 256, 0, p0, p1), ones_b[:, 0:64], PTc[:, rg, 0:256], start=True, stop=False))
                        fd.append(MM(bank(0, 256, 0, p0, p1), ones_b[0:4, 0:64], PTn[0:4, rg, 0:256], start=False, stop=False))
                        fd.append(MM(bank(0, 256, 0, p0, p1), ones_b[0:1, 0:64], sink2[0:1, rg, :], start=False, stop=True))
                        for n in range(NS):
                            fo.append(MM(bank(4, 16, n * 16, p0, p1), Vnb[0:4, n, p0:p1], PTn[0:4, rg, n * 16:(n + 1) * 16], start=True, stop=False))
                            fo.append(MM(bank(4, 16, n * 16, p0, p1), Vb[:, n, p0:p1], PTc[:, rg, n * 16:(n + 1) * 16], start=False, stop=True))
                    else:
                        for c in range(2):
                            o_all = bank(0, NTS, c * NTS, p0, p1)
                            fd.append(MM(o_all, ones_b[:, 0:64], pc[:, :, :, c], start=True, stop=False))
                            fd.append(MM(o_all, ones_b[0:4, 0:64], pn[:, :, :, c], start=False, stop=True))
                            vcol = c * 128 + rg * 64
                            for n in range(NS):
                                o_n = bank(4, 4, c * NTS + n * 4, p0, p1)
                                fo.append(MM(o_n, Vnb[0:4, n, vcol:vcol + 64], pn[:, n, :, c], start=True, stop=False, skip=(dil != 0)))
                                if dil == 0:
                                    fo.append(MM(o_n, Vb[:, n, vcol:vcol + 64], pc[:, n, :, c], start=False, stop=True))
                                else:
                                    for t in range(4):
                                        fo.append(MM(bank(4, 1, c * NTS + n * 4 + t, p0, p1), Vb[:, n, t * 256 + vcol:t * 256 + vcol + 64],
                                                     pc[:, n, t, c:c + 1], start=False, stop=(t == 3), skip=True))
                mix2 = lambda a, b_: [x_ for pr in zip(a, b_) for x_ in pr]
                P.op("pe", mix2(fd_l[0], fd_l[1]), r=ptn + ["ones_b", "sink2"], w=pb(0))
                P.op("pe", mix2(fo_l[0], fo_l[1]), r=ptn + ["Vb", "Vnb"], w=pb(4))
                if isA:
                    P.op("dve", lambda e: e.reciprocal(out=recs[:, 0, :], in_=bank(0, 256)), r=pb(0), w=["recs"])
                    P.op("dve", TT(prods[:, 0, :], bank(4, 256), recs[:, 0, :], MUL), r=pb(4) + ["recs"], w=["prods"])
                    P.op("pool", TC(oaT[:, 0:4, SEQ:SEQ + NTS], prods[:, 0, :].rearrange("p (n c) -> p c n", c=4)), r=["prods"])
                else:
                    dN = accN[:, :, SEQ:SEQ + NTS]
                    dD = accD[:, :, SEQ:SEQ + NTS]
                    sN = bank(4, 2 * NTS).rearrange("p (c n) -> p c n", c=2)
                    sD = bank(0, 2 * NTS).rearrange("p (c n) -> p c n", c=2)
                    if gi == 1:
                        P.op("dve", TC(dN, sN), r=pb(4), w=["accS"])
                        P.op("dve", TC(dD, sD), r=pb(0), w=["accS"])
                    else:
                        P.op("dve", TT(dN, sN, dN, ADD), r=pb(4), w=["accS"])
                        P.op("dve", TT(dD, sD, dD, ADD), r=pb(0), w=["accS"])
            P.barrier()
            lo_ = SEQ if ("s" in SUB and "b" in SUB and "2" in PHASES) else 0
            P.op("dve", lambda e: e.reciprocal(out=accD[:, :, lo_:NT], in_=accD[:, :, lo_:NT]), w=["accDf"])
            P.op("dve", TT(obT[:, :, lo_:NT], accN[:, :, lo_:NT], accD[:, :, lo_:NT], MUL), r=["accDf"])
            P.barrier()

        def emit_ln(tag, z, ntok, out, lnb, gi):
            st_ = lnst[tag]
            P.op("dve", [lambda e: e.bn_stats(out=st_[0:ntok, 0, 0:6], in_=z[0:ntok, 0, 0:512]),
                         lambda e: e.bn_stats(out=st_[0:ntok, 0, 6:12], in_=z[0:ntok, 0, 512:1024])], r=[z_name[tag]], w=["st" + tag])
            P.op("dve", lambda e: e.bn_aggr(out=st_[0:ntok, 0, 12:14], in_=st_[0:ntok, 0, 0:12]), r=["st" + tag], w=["mv" + tag])
            P.op("act", ACTF(st_[0:ntok, 0, 14:15], st_[0:ntok, 0, 13:14], AF.Sqrt, bias=EPS), r=["mv" + tag], w=["rs" + tag])
            P.op("dve", lambda e: e.reciprocal(out=st_[0:ntok, 0, 14:15], in_=st_[0:ntok, 0, 14:15]), r=["rs" + tag], w=["rs" + tag])
            P.op("dve", lambda e: e.tensor_scalar(out=st_[0:ntok, 0, 15:16], in0=st_[0:ntok, 0, 12:13], scalar1=-1.0,
                                                  scalar2=st_[0:ntok, 0, 14:15], op0=ALU.mult, op1=ALU.mult),
                 r=["mv" + tag, "rs" + tag], w=["nm" + tag])
            P.op("act", ACTF(out[0:ntok, 0, :], z[0:ntok, 0, :], AF.Identity, scale=st_[0:ntok, 0, 14:15], bias=st_[0:ntok, 0, 15:16]),
                 r=[z_name[tag], "rs" + tag, "nm" + tag], w=[out_name[tag]])
            P.op("dve", TT(out[0:ntok, 0, :], out[0:ntok, 0, :], lnb[0:ntok, gi, :], MUL), r=[out_name[tag], "lnb"], w=[out_name[tag]])
            P.op("dve", TT(out[0:ntok, 0, :], out[0:ntok, 0, :], lnb[0:ntok, gi + 1, :], ADD), r=[out_name[tag], "lnb"], w=[out_name[tag]])

        lnst, z_name, out_name = {}, {}, {}
        tiles = [(0, 512), (512, 512), (1024, 512), (1536, 512), (SEQ, NTS)]

        def xrows(t0, n):
            return I["xp"][t0:t0 + n, :] if t0 < SEQ else I["xs"][t0 - SEQ:t0 - SEQ + n, :]

        if "3" in PHASES:
            A.at(0)
            hT = A.alloc([8, NT], BF16)
            OFF_HT_END = A.off
            Wg, Wbra, Wbrb, Wo = p3w()
            A.at(OFF_HT_END)
            xs6 = [A.alloc([1, D], F32) for _ in range(6)]
            lnb = A.alloc([2, D], F32)
            xTt = A.alloc([8, 512], BF16)
            mT = A.alloc([8, 512], BF16)
            gcol = A.alloc([2, 8], F32)
            st3 = [A.alloc([1, 16], F32) for _ in range(2)]
            assert A.off <= P3W_LO, (A.off, P3W_LO)
            A.at(OFF_P2T - 2 * 2 * NT * 4)
            sg = [A.alloc([1, 512], F32) for _ in range(4)]
            xb3 = [A.alloc([1, D], BF16) for _ in range(2)]
            hb3 = [A.alloc([1, D], BF16) for _ in range(3)]
            hh = [A.alloc([1, D], F32) for _ in range(3)]
            assert A.off <= OFF_P2T, (A.off, OFF_P2T)
            P.dma("sp", "lnb", DMA(lnb[:, 0, :], I["ln"][0:1, :].partition_broadcast(128)), w=["lnb"])
            P.dma("sp", "lnb", DMA(lnb[:, 1, :], I["ln"][1:2, :].partition_broadcast(128)), w=["lnb"])
            P.dma("sp", "gcol", DMA(gcol[:, :, :], I["lncol"][:, 0:2, :]), w=["gcol"])
            if not P3W["issued"]:
                p3w_issue()
            lnst["3a"], lnst["3b"] = st3[0], st3[1]
            subs = []
            for ti, (t0, TT_) in enumerate(tiles):
                for sbk in range((TT_ + 127) // 128):
                    subs.append((ti, t0 + sbk * 128, min(128, TT_ - sbk * 128)))
            xbuf = {sidx: sidx % 6 for sidx in range(len(subs))}

            def p3_load(ti, late=None):
                busy = set(xbuf[q] for q, sq in enumerate(subs) if sq[0] == ti - 1)
                for sidx, (tj, s0, n) in enumerate(subs):
                    if tj == ti:
                        bi = xbuf[sidx]
                        if late is not None and ((bi in busy) != late):
                            continue
                        P.dma("sp", "xs%d" % bi, DMA(xs6[bi][0:n, 0, :], xrows(s0, n)), w=["xs%d" % bi])

            def p3_xT(ti, late=None):
                busy = set(xbuf[q] for q, sq in enumerate(subs) if sq[0] == ti - 1)
                for sidx, (tj, s0, n) in enumerate(subs):
                    if tj != ti:
                        continue
                    bi = xbuf[sidx]
                    if late is not None and ((bi in busy) != late):
                        continue
                    c0 = s0 - tiles[ti][0]
                    xb_ = xb3[sidx % 2]
                    if sidx % 2 == 0:
                        P.op("act", CP(xb_[0:n, 0, :], xs6[bi][0:n, 0, :]), r=["xs%d" % bi], w=["xb3_%d" % (sidx % 2)])
                    else:
                        P.op("dve", TC(xb_[0:n, 0, :], xs6[bi][0:n, 0, :]), r=["xs%d" % bi], w=["xb3_%d" % (sidx % 2)])
                    bk = 6 + sidx % 2
                    P.op("pe", [TR(bank_bf(bk)[:, kc * 128:kc * 128 + n], xb_[0:n, 0, kc * 128:(kc + 1) * 128], idb[0:n, 0:n]) for kc in range(8)],
                         r=["xb3_%d" % (sidx % 2), "idb"], w=pb(bk))
                    srcv = bank_bf(bk).rearrange("p (a b) -> p a b", b=128)[:, :, 0:n]
                    dstv = xTt[:, 0:8, c0:c0 + n]
                    if sidx % 2 == 0:
                        P.op("act", CP(dstv, srcv), r=pb(bk), w=["xTt"], ss=True)
                    else:
                        P.op("dve", TC(dstv, srcv), r=pb(bk), w=["xTt"], ss=True)

            def p3_gates(ti):
                t0, TT_ = tiles[ti]
                for f in range(8):
                    s_ = f % 2
                    bga, bgb, bra, brb = 4 * s_, 4 * s_ + 1, 4 * s_ + 2, 4 * s_ + 3
                    fcol = slice(f * 128, (f + 1) * 128)
                    P.op("pe", [MM(bank(bga, TT_), Wg[:, kc, f * 128:(f + 1) * 128], xTt[:, kc, 0:TT_], start=(kc == 0), stop=(kc == 7)) for kc in range(8)],
                         r=["xTt", "Wg%d" % (f // 4)], w=pb(bga))
                    P.op("pe", [MM(bank(bgb, TT_), Wg[:, kc, 1024 + f * 128:1024 + (f + 1) * 128], xTt[:, kc, 0:TT_], start=(kc == 0), stop=(kc == 7)) for kc in range(8)],
                         r=["xTt", "Wg%d" % (2 + f // 4)], w=pb(bgb))
                    P.op("pe", [MM(bank(bra, TT_), Wbra[:, c, f * 128:(f + 1) * 128], oaT[:, c, t0:t0 + TT_], start=(c == 0), stop=(c == 3)) for c in range(4)],
                         r=["Wbra%d" % (f // 4)], w=pb(bra))
                    P.op("pe", [MM(bank(brb, TT_), Wbrb[:, c, f * 128:(f + 1) * 128], obT[:, c, t0:t0 + TT_], start=(c == 0), stop=(c == 1)) for c in range(2)],
                         r=["Wbrb%d" % (f // 4)], w=pb(brb))
                    sa, sb_ = sg[2 * s_], sg[2 * s_ + 1]
                    P.op("act", ACTF(sa[:, 0, 0:TT_], bank(bga, TT_), AF.Sigmoid), r=pb(bga), w=["sg%d" % (2 * s_)])
                    P.op("act", ACTF(sb_[:, 0, 0:TT_], bank(bgb, TT_), AF.Sigmoid), r=pb(bgb), w=["sg%d" % (2 * s_ + 1)])
                    P.op("dve", TT(sa[:, 0, 0:TT_], bank(bra, TT_), sa[:, 0, 0:TT_], MUL), r=pb(bra), w=["sg%d" % (2 * s_)])
                    P.op("dve", TT(sb_[:, 0, 0:TT_], bank(brb, TT_), sb_[:, 0, 0:TT_], MUL), r=pb(brb), w=["sg%d" % (2 * s_ + 1)])
                    P.op("pool" if f < 6 else "dve", TT(mT[:, f, 0:TT_], sa[:, 0, 0:TT_], sb_[:, 0, 0:TT_], ADD),
                         r=["sg%d" % (2 * s_), "sg%d" % (2 * s_ + 1)], w=["mT"], ss=(f < 6))

            def p3_mix(sidx, pre_hT=None):
                tj, s0, n = subs[sidx]
                bi = xbuf[sidx]
                c0 = s0 - tiles[tj][0]
                b0 = 2 * (sidx % 2)
                q3 = sidx % 3
                zx = xs6[bi]
                for hf in range(2):
                    P.op("pe", [MM(bank(b0 + hf, 512, 0, 0, n), mT[:, kc, c0:c0 + n], Wo[:, kc, hf * 512:(hf + 1) * 512], start=(kc == 0), stop=(kc == 7))
                                for kc in range(8)], r=["mT", "Wo%d" % hf], w=pb(b0 + hf))
                if pre_hT is not None:
                    p3_hT(pre_hT)
                for hf in range(2):
                    P.op("dve", STT(zx[0:n, 0, hf * 512:(hf + 1) * 512], zx[0:n, 0, hf * 512:(hf + 1) * 512], ALPHA, bank(b0 + hf, 512, 0, 0, n), MUL, ADD),
                         r=pb(b0 + hf), w=["xs%d" % bi])
                st_ = st3[sidx % 2]
                tg = "3" + "ab"[sidx % 2]
                P.op("dve", [lambda e: e.bn_stats(out=st_[0:n, 0, 0:6], in_=zx[0:n, 0, 0:512]),
                             lambda e: e.bn_stats(out=st_[0:n, 0, 6:12], in_=zx[0:n, 0, 512:1024])], r=["xs%d" % bi], w=["st" + tg])
                P.op("dve", lambda e: e.bn_aggr(out=st_[0:n, 0, 12:14], in_=st_[0:n, 0, 0:12]), r=["st" + tg], w=["mv" + tg])
                P.op("act", ACTF(st_[0:n, 0, 14:15], st_[0:n, 0, 13:14], AF.Sqrt, bias=EPS), r=["mv" + tg], w=["rs" + tg])
                P.op("dve", lambda e: e.reciprocal(out=st_[0:n, 0, 14:15], in_=st_[0:n, 0, 14:15]), r=["rs" + tg], w=["rs" + tg])
                P.op("dve", lambda e: e.tensor_scalar(out=st_[0:n, 0, 15:16], in0=st_[0:n, 0, 12:13], scalar1=-1.0,
                                                      scalar2=st_[0:n, 0, 14:15], op0=ALU.mult, op1=ALU.mult), r=["mv" + tg, "rs" + tg], w=["nm" + tg])
                hbf_, hb_ = hb3[q3], hh[q3]
                P.op("act", ACTF(hbf_[0:n, 0, :], zx[0:n, 0, :], AF.Identity, scale=st_[0:n, 0, 14:15], bias=st_[0:n, 0, 15:16]),
                     r=["xs%d" % bi, "rs" + tg, "nm" + tg], w=["hb3_%d" % q3])
                P.op("act", ACTF(hb_[0:n, 0, :], zx[0:n, 0, :], AF.Identity, scale=st_[0:n, 0, 14:15], bias=st_[0:n, 0, 15:16]),
                     r=["xs%d" % bi, "rs" + tg, "nm" + tg], w=["hh%d" % q3])
                P.op("pool", TT(hb_[0:n, 0, :], hb_[0:n, 0, :], lnb[0:n, 0, :], MUL), r=["hh%d" % q3, "lnb"], w=["hh%d" % q3])
                P.op("pool", TT(hb_[0:n, 0, :], hb_[0:n, 0, :], lnb[0:n, 1, :], ADD), r=["hh%d" % q3, "lnb"], w=["hh%d" % q3])
                P.dma("pool", "hh%d" % q3, DMA(h_scr[s0:s0 + n, :], hb_[0:n, 0, :]), r=["hh%d" % q3])

            def p3_hT(sidx):
                tj, s0, n = subs[sidx]
                q3 = sidx % 3
                hbf_ = hb3[q3]
                P.op("pe", [TR(bank_bf(4)[:, kc * 128:kc * 128 + n], hbf_[0:n, 0, kc * 128:(kc + 1) * 128], idb[0:n, 0:n]) for kc in range(4)],
                     r=["hb3_%d" % q3, "idb"], w=pb(4))
                P.op("pe", [TR(bank_bf(5)[:, (kc - 4) * 128:(kc - 4) * 128 + n], hbf_[0:n, 0, kc * 128:(kc + 1) * 128], idb[0:n, 0:n]) for kc in range(4, 8)],
                     r=["hb3_%d" % q3, "idb"], w=pb(5))
                for kc in range(4):
                    P.op("dve", STT(hT[:, kc, s0:s0 + n], bank_bf(4)[:, kc * 128:kc * 128 + n], gcol[:, 0, kc:kc + 1],
                                    gcol[:, 1, kc:kc + 1].to_broadcast([128, n]), MUL, ADD), r=pb(4) + ["gcol"], ss=True)
                for kc in range(4, 8):
                    P.op("act", ACTF(hT[:, kc, s0:s0 + n], bank_bf(5)[:, (kc - 4) * 128:(kc - 4) * 128 + n], AF.Identity,
                                     scale=gcol[:, 0, kc:kc + 1], bias=gcol[:, 1, kc:kc + 1]), r=pb(5) + ["gcol"], ss=True)

            def p3_post(ti):
                ss_ = [q for q, sq in enumerate(subs) if sq[0] == ti]
                m = len(ss_)
                for k_, sidx in enumerate(ss_):
                    p3_mix(sidx, ss_[k_ - 2] if k_ >= 2 else None)
                if ti + 1 < len(tiles):
                    p3_load(ti + 1, late=True)
                    p3_xT(ti + 1, late=True)
                for k_ in range(max(0, m - 2), m):
                    p3_hT(ss_[k_])

            p3_load(0)
            p3_xT(0)
            for ti in range(len(tiles)):
                if ti + 1 < len(tiles):
                    p3_load(ti + 1, late=False)
                p3_gates(ti)
                if ti + 1 < len(tiles):
                    p3_xT(ti + 1, late=False)
                p3_post(ti)
            P.barrier()

        NWDA = 17
        WDA_LO = ARENA_BYTES - NWDA * 2048
        A.at(WDA_LO)
        WdA = A.alloc([NWDA, D], BF16)
        WDA = {"issued": False}

        if "4" in PHASES:
            A.at(8 * NT * 2)
            gT = A.alloc([NJ, NT], BF16)
            OFF_GT_END = A.off
            wu = [A.alloc([8, 256], BF16) for _ in range(3)]
            cw = A.alloc([NJ, 3], F32)
            cb = A.alloc([1, NJ], F32)
            ue = [A.alloc([1, 520], F32) for _ in range(2)]
            ues = A.alloc([NS, 6], F32)
            a0 = [A.alloc([1, 512], F32) for _ in range(2)]
            a1 = [A.alloc([1, 512], F32) for _ in range(2)]
            gg = [A.alloc([1, 512], F32) for _ in range(2)]
            stT = A.alloc([NJ, 32], F32)
            ust = A.alloc([2, NJ], F32)
            usts = A.alloc([NJ, 32], F32)
            stc_in = A.alloc([1, DFF], F32)
            so_p = A.alloc([1, 128], F32)
            so_s = stc_in
            assert A.off <= WDA_LO, (A.off, WDA_LO)
            hTv = arena_t[:, 0:8 * NT].rearrange("p (a b) -> p a b", b=NT)
            P.dma("sp", "cw", DMA(cw[:, :, :], I["convw"][:, :, :]), w=["cw"])
            P.dma("sp", "cb", DMA(cb[:, 0, :], I["convb"][:, :]), w=["cb"])
            P.dma("sp", "stc_in", DMA(stc_in[0:32, 0, :], I["stc"][:, :]), w=["stc_in"])
            for q4 in range(0, NJ, 4):
                bk = 6 + (q4 // 4) % 2
                js = list(range(q4, min(q4 + 4, NJ)))
                P.op("pe", [TR(bank(bk, 32, (j - q4) * 32), stc_in[0:32, 0, j * 128:(j + 1) * 128], ident[0:32, 0:32]) for j in js],
                     r=["stc_in", "ident"], w=pb(bk))
                P.op("act", CP(stT[:, q4:q4 + len(js), :], bank(bk, 32 * len(js)).rearrange("p (a b) -> p a b", b=32)), r=pb(bk), w=["stT"])
            wupv = I["wup"].rearrange("(kc p) n -> p kc n", p=128)

            def p4_w(j):
                w_ = wu[j % 3]
                P.dma("pool", "wu%da" % (j % 3), DMA(w_[:, :, 0:128], wupv[:, :, j * 128:(j + 1) * 128]), w=["wu%da" % (j % 3)])
                P.dma("pool", "wu%db" % (j % 3), DMA(w_[:, :, 128:256], wupv[:, :, DFF + j * 128:DFF + (j + 1) * 128]), w=["wu%db" % (j % 3)])

            p4_w(0)
            p4_w(1)
            cnt = 0
            wdv = I["wdn"].rearrange("(j p) n -> p j n", p=128)
            for j in range(NJ):
                if j + 2 < NJ:
                    p4_w(j + 2)
                if "5" in PHASES and j % 2 == 0 and j < NWDA:
                    nq = min(2, NWDA - j)
                    for hf in range(2):
                        P.dma("pool", "WdA%d_%d" % (j, hf), DMA(WdA[:, j:j + nq, hf * 512:(hf + 1) * 512], wdv[:, j:j + nq, hf * 512:(hf + 1) * 512]),
                              w=["WdA%d_%d" % (j, hf)])
                    WDA["issued"] = True
                w_ = wu[j % 3]
                for ti, (t0, TT_) in enumerate(tiles):
                    par = cnt % 4
                    cnt += 1
                    bu, bv = 2 * par, 2 * par + 1
                    P.op("pe", [MM(bank(bu, TT_), w_[:, kc, 0:128], hTv[:, kc, t0:t0 + TT_], start=(kc == 0), stop=(kc == 7)) for kc in range(8)],
                         r=["wu%da" % (j % 3)], w=pb(bu))
                    P.op("pe", [MM(bank(bv, TT_), w_[:, kc, 128:256], hTv[:, kc, t0:t0 + TT_], start=(kc == 0), stop=(kc == 7)) for kc in range(8)],
                         r=["wu%db" % (j % 3)], w=pb(bv))
                    A0, A1, G_ = a0[par % 2], a1[par % 2], gg[par % 2]
                    an, a1n, gn = "a0_%d" % (par % 2), "a1_%d" % (par % 2), "gg%d" % (par % 2)
                    if t0 < SEQ:
                        U = ue[ti % 2]
                        un = "ue%d" % (ti % 2)
                        if ti == 0:
                            P.op("pool", lambda e, U=U: e.memset(U[:, 0, 0:2], 0.0), w=[un + "h"])
                        else:
                            P.op("pool", TC(U[:, 0, 0:2], ue[(ti - 1) % 2][:, 0, 512:514]), r=["ue%d" % ((ti - 1) % 2)], w=[un + "h"])
                        P.op("act", CP(U[:, 0, 2:514], bank(bu)), r=pb(bu), w=[un])
                        P.op("act", ACTF(A0[:, 0, :], bank(bu), AF.Identity, scale=cw[:, j, 2:3], bias=cb[:, 0, j:j + 1]), r=pb(bu) + ["cw", "cb"], w=[an])
                        P.op("dve", STT(A1[:, 0, :], U[:, 0, 1:513], cw[:, j, 1:2], A0[:, 0, :], MUL, ADD), r=[un, un + "h", an, "cw"], w=[a1n])
                        P.op("dve", STT(A0[:, 0, :], U[:, 0, 0:512], cw[:, j, 0:1], A1[:, 0, :], MUL, ADD), r=[un, un + "h", a1n, "cw"], w=[an])
                        P.op("act", ACTF(G_[:, 0, :], A0[:, 0, :], AF.Gelu), r=[an], w=[gn])
                        P.op("dve", TT(gT[:, j, t0:t0 + 512], bank(bv), G_[:, 0, :], MUL), r=pb(bv) + [gn])
                        if ti == 3:
                            P.op("pool", TC(ust[:, :, j], U[:, 0, 512:514]), r=[un], w=["ust"], ss=True)
                    else:
                        P.op("pool", TC(ues[:, :, 0:2], stT[:, j, :].rearrange("p (n r) -> p n r", r=2)), r=["stT"], w=["uesh"])
                        P.op("act", CP(ues[:, :, 2:6], bank(bu, NTS).rearrange("p (n t) -> p n t", t=4)), r=pb(bu), w=["ues"])
                        P.op("act", ACTF(A0[:, 0, 0:NTS], bank(bu, NTS), AF.Identity, scale=cw[:, j, 2:3], bias=cb[:, 0, j:j + 1]), r=pb(bu) + ["cw", "cb"], w=[an])
                        v4 = lambda ap: ap.rearrange("p (n t) -> p n t", t=4)
                        P.op("dve", STT(v4(A1[:, 0, 0:NTS]), ues[:, :, 1:5], cw[:, j, 1:2], v4(A0[:, 0, 0:NTS]), MUL, ADD), r=["ues", "uesh", an, "cw"], w=[a1n])
                        P.op("dve", STT(v4(A0[:, 0, 0:NTS]), ues[:, :, 0:4], cw[:, j, 0:1], v4(A1[:, 0, 0:NTS]), MUL, ADD), r=["ues", "uesh", a1n, "cw"], w=[an])
                        P.op("act", ACTF(G_[:, 0, 0:NTS], A0[:, 0, 0:NTS], AF.Gelu), r=[an], w=[gn])
                        P.op("dve", TT(gT[:, j, SEQ:SEQ + NTS], bank(bv, NTS), G_[:, 0, 0:NTS], MUL), r=pb(bv) + [gn])
                        P.op("pool", TC(usts[:, j, :].rearrange("p (n r) -> p n r", r=2), ues[:, :, 4:6]), r=["ues"], w=["usts"], ss=True)
            P.op("pe", TR(bank(4, 128, 0, 0, 2 * NJ), ust[:, :, :].rearrange("p r j -> p (r j)"), ident[:, :]), r=["ident", "ust"], w=pb(4))
            P.op("act", CP(so_p[0:2 * NJ, 0, :], bank(4, 128, 0, 0, 2 * NJ)), r=pb(4), w=["so_p"])
            for r_ in range(2):
                P.dma("sp", "so_p%d" % r_, DMA(O["sc_p"][r_:r_ + 1, :].rearrange("o (j f) -> (o j) f", f=128), so_p[r_ * NJ:(r_ + 1) * NJ, 0, :]), r=["so_p"])
            for q4 in range(0, NJ, 4):
                bk = 6 + (q4 // 4) % 2
                js = list(range(q4, min(q4 + 4, NJ)))
                P.op("pe", [TR(bank(bk, 128, (j - q4) * 128, 0, 32), usts[:, j, :], ident[:, :]) for j in js], r=["usts", "ident"], w=pb(bk))
                P.op("act", CP(so_s[0:32, 0, q4 * 128:(q4 + len(js)) * 128], bank(bk, 128 * len(js), 0, 0, 32)), r=pb(bk), w=["stc_in"])
            P.dma("sp", "so_s", DMA(O["sc_s"][:, :], so_s[0:32, 0, :]), r=["stc_in"])
            P.barrier()

        if "5" in PHASES:
            A.at(8 * NT * 2)
            gT = A.alloc([NJ, NT], BF16)
            WdB = A.alloc([NJ - NWDA, D], BF16)
            assert A.off <= WDA_LO
            Wd_of = lambda j: (WdA[:, j, :] if j < NWDA else WdB[:, j - NWDA, :])
            A.at(0)
            hb5 = [A.alloc([1, D], F32) for _ in range(3)]
            yb = [A.alloc([1, D], F32) for _ in range(2)]
            lnb2 = A.alloc([2, D], F32)
            z5 = A.alloc([1, D], F32)
            st5 = A.alloc([1, 16], F32)
            assert A.off <= 8 * NT * 2
            P.dma("sp", "lnb", DMA(lnb2[:, 0, :], I["ln"][2:3, :].partition_broadcast(128)), w=["lnb"])
            P.dma("sp", "lnb", DMA(lnb2[:, 1, :], I["ln"][3:4, :].partition_broadcast(128)), w=["lnb"])
            wdv = I["wdn"].rearrange("(j p) n -> p j n", p=128)
            wd_names = {0: [], 1: []}
            for hf in range(2):
                P.dma("pool", "WdB_%d" % hf, DMA(WdB[:, :, hf * 512:(hf + 1) * 512], wdv[:, NWDA:NJ, hf * 512:(hf + 1) * 512]), w=["WdB_%d" % hf])
                wd_names[hf].append("WdB_%d" % hf)
            for q in range(0, NWDA, 2):
                nq = min(2, NWDA - q)
                for hf in range(2):
                    if not WDA["issued"]:
                        P.dma("pool", "WdA%d_%d" % (q, hf), DMA(WdA[:, q:q + nq, hf * 512:(hf + 1) * 512], wdv[:, q:q + nq, hf * 512:(hf + 1) * 512]),
                              w=["WdA%d_%d" % (q, hf)])
                    wd_names[hf].append("WdA%d_%d" % (q, hf))
            lnst["5"], z_name["5"] = st5, "z5"
            blks = [(i * 128, 128) for i in range(NB)] + [(SEQ, NTS)]

            def p5_load(i):
                s0, n = blks[i]
                P.dma("sp" if i % 2 == 0 else "act", "hb%d" % (i % 3), DMA(hb5[i % 3][0:n, 0, :], h_scr[s0:s0 + n, :]), w=["hb%d" % (i % 3)])

            p5_load(0)
            p5_load(1)
            for i, (s0, n) in enumerate(blks):
                if i + 2 < len(blks):
                    p5_load(i + 2)
                b0 = 2 * (i % 4)
                for hf in range(2):
                    P.op("pe", [MM(bank(b0 + hf, 512, 0, 0, n), gT[:, j, s0:s0 + n], Wd_of(j)[:, hf * 512:(hf + 1) * 512], start=(j == 0), stop=(j == NJ - 1))
                                for j in range(NJ)], r=wd_names[hf], w=pb(b0 + hf))
                for hf in range(2):
                    P.op("dve", STT(z5[0:n, 0, hf * 512:(hf + 1) * 512], hb5[i % 3][0:n, 0, hf * 512:(hf + 1) * 512], ALPHA, bank(b0 + hf, 512, 0, 0, n), MUL, ADD),
                         r=["hb%d" % (i % 3)] + pb(b0 + hf), w=["z5"])
                out_name["5"] = "yb%d" % (i % 2)
                emit_ln("5", z5, n, yb[i % 2], lnb2, 0)
                dst = O["y_p"][s0:s0 + n, :] if s0 < SEQ else O["y_s"][:, :]
                P.dma("sp", "yb%d" % (i % 2), DMA(dst, yb[i % 2][0:n, 0, :]), r=["yb%d" % (i % 2)])

        P.barrier()
        with nc.Block() as block:
            P.replay(block)
    return nc


_CACHE = {}


def kernel(x_prompt, x_sample, cache_a, cache_b1, cache_b2, cache_b3, state_conv,
           w_in, sink_a, w_br_a, w_br_b, w_o, ln1_g, ln1_b, w_up, conv_w, conv_b, w_down, ln2_g, ln2_b):
    f = lambda a: np.ascontiguousarray(np.asarray(a, dtype=np.float32))
    cst = _consts()
    ln = np.stack([f(ln1_g)[0], f(ln1_b)[0], f(ln2_g)[0], f(ln2_b)[0]], axis=0)
    wts = _prep_weights(f(w_in)[0], f(w_br_a)[0], f(conv_w)[0], f(conv_b)[0], ln)
    shared = dict(wts)
    shared.update(cst)
    shared.update(wbrb=f(w_br_b)[0], wo=f(w_o)[0], wup=f(w_up)[0], wdn=f(w_down)[0], sink=f(sink_a).reshape(1, 8))
    xp, xs = f(x_prompt), f(x_sample)
    ca, cb1, cb2, cb3, stc = f(cache_a)[0], f(cache_b1)[0], f(cache_b2)[0], f(cache_b3)[0], f(state_conv)[0]
    in_maps = []
    for c in range(NCORES):
        n0, n1 = c * NS, (c + 1) * NS
        m = dict(shared)
        m.update(xp=xp[c], xs=xs[n0:n1].reshape(NTS, D),
                 ca=ca[n0:n1].reshape(NS, 128, 256), cb1=cb1[n0:n1].reshape(NS, 128, 512),
                 cb2=cb2[n0:n1].reshape(NS, 512, 512), cb3=cb3[n0:n1].reshape(NS, 2048, 512),
                 stc=stc[n0:n1].reshape(NS * 2, DFF))
        in_maps.append({k: np.ascontiguousarray(m[k]) for k, _ in IN_SPECS})
    if "nc" not in _CACHE:
        _CACHE["nc"] = build()
    res = run_bass_kernel_spmd(_CACHE["nc"], in_maps, core_ids=list(range(NCORES)))
    R = res.results
    cat = lambda k: np.stack([R[c][k] for c in range(NCORES)], axis=0)
    y_p = cat("y_p")
    y_s = cat("y_s").reshape(128, TS, D)
    outs = [y_p, y_s,
            cat("ca_p").reshape(1, 8, 128, 2, 2, 64), cat("ca_s").reshape(1, 128, TS, 2, 2, 64),
            cat("cb1_p").reshape(1, 8, 128, 2, 4, 64), cat("cb1_s").reshape(1, 128, TS, 2, 4, 64),
            cat("cb2_p").reshape(1, 8, 512, 2, 4, 64), cat("cb2_s").reshape(1, 128, TS, 2, 4, 64),
            cat("cb3_p").reshape(1, 8, 2048, 2, 4, 64), cat("cb3_s").reshape(1, 128, TS, 2, 4, 64),
            cat("sc_p").reshape(1, 8, 2, DFF), cat("sc_s").reshape(1, 128, 2, DFF)]
    return tuple(np.ascontiguousarray(o.astype(np.float32)) for o in outs)
```

```python
import os
from contextlib import ExitStack
import numpy as np
import concourse.bass as bass
import concourse.mybir as mybir
from concourse.bass_utils import run_bass_kernel_spmd

F32 = mybir.dt.float32
BF16 = mybir.dt.bfloat16
AF = mybir.ActivationFunctionType
ALU = mybir.AluOpType

NCORES = 8
D = 1024
SEQ = 2048
NB = SEQ // 128
NS = 16
TS = 4
NTS = NS * TS
NT = SEQ + NTS
PAST = 16384
DFF = 2816
NJ = DFF // 128
ALPHA = float(2.0 ** 0.25)
SCALE = 0.125
EPS = 1e-5
ENGS = ("pe", "act", "dve", "pool", "sp")


class Prog:
    def __init__(self, nc, stack):
        self.nc = nc
        self.streams = {e: [] for e in ENGS}
        self.cnt = {e: 0 for e in ENGS}
        self.seen = {e: {} for e in ENGS}
        self.last_w = {}
        self.readers = {}
        self.dma_sems = {}
        self._stack = stack
        self.eng_sem = {e: stack.enter_context(nc.semaphore("c_" + e)) for e in ENGS}

    def _dma_sem(self, key):
        if key not in self.dma_sems:
            h = self._stack.enter_context(self.nc.semaphore("d_" + key))
            self.dma_sems[key] = [h, 0]
        return self.dma_sems[key]

    @staticmethod
    def _excl(reads, writes):
        return list(writes) + [b for b in reads if b.startswith("ps")]

    def _deps(self, reads, writes):
        toks = []
        writes = self._excl(reads, writes)
        for b in reads:
            t = self.last_w.get(b)
            if t is not None:
                toks.append(t)
        for b in writes:
            t = self.last_w.get(b)
            if t is not None:
                toks.append(t)
            toks.extend(self.readers.get(b, ()))
        return toks

    def _emit_waits(self, eng, toks, ss=False):
        need = {}
        for kind, key, val in toks:
            if kind == "c" and key == eng and (ss or eng in ("pe", "sp")):
                continue
            k = (kind, key)
            if val > need.get(k, 0):
                need[k] = val
        for (kind, key), val in need.items():
            if self.seen[eng].get((kind, key), 0) >= val:
                continue
            self.seen[eng][(kind, key)] = val
            sem = self.eng_sem[key] if kind == "c" else self.dma_sems[key][0]
            self.streams[eng].append(("wait", sem, val))

    def _commit(self, tok, reads, writes):
        writes = self._excl(reads, writes)
        for b in reads:
            self.readers.setdefault(b, []).append(tok)
        for b in writes:
            self.last_w[b] = tok
            self.readers[b] = []

    def op(self, eng, fns, r=(), w=(), ss=False):
        if callable(fns):
            fns = [fns]
        self._emit_waits(eng, self._deps(r, w), ss)
        self.cnt[eng] += 1
        tok = ("c", eng, self.cnt[eng])
        for f in fns[:-1]:
            self.streams[eng].append(("op", f, None))
        self.streams[eng].append(("op", fns[-1], self.eng_sem[eng]))
        self._commit(tok, r, w)
        return tok

    def dma(self, queue, key, fn, r=(), w=()):
        self._emit_waits(queue, self._deps(r, w))
        s = self._dma_sem(key)
        s[1] += 16
        tok = ("d", key, s[1])
        self.streams[queue].append(("dma", fn, s[0]))
        self._commit(tok, r, w)
        return tok

    def barrier(self):
        toks = [("c", f, self.cnt[f]) for f in ENGS if self.cnt[f] > 0]
        toks += [("d", k, v[1]) for k, v in self.dma_sems.items() if v[1] > 0]
        for e in ENGS:
            self._emit_waits(e, toks)
        self.last_w = {}
        self.readers = {}

    def replay(self, block):
        def run(engine, stream):
            for it in stream:
                if it[0] == "wait":
                    engine.wait_ge(it[1], it[2])
                elif it[0] == "op":
                    ins = it[1](engine)
                    if it[2] is not None:
                        ins.then_inc(it[2], 1)
                else:
                    it[1](engine).then_inc(it[2], 16)

        @block.tensor
        def _(e):
            run(e, self.streams["pe"])

        @block.scalar
        def _(e):
            run(e, self.streams["act"])

        @block.vector
        def _(e):
            run(e, self.streams["dve"])

        @block.gpsimd
        def _(e):
            run(e, self.streams["pool"])

        @block.sync
        def _(e):
            run(e, self.streams["sp"])


def MM(out, lhsT, rhs, start=True, stop=True, skip=False):
    return lambda e: e.matmul(out, lhsT, rhs, start=start, stop=stop, skip_group_check=skip)


def TR(out, in_, idn):
    return lambda e: e.transpose(out, in_, idn)


def ACTF(out, in_, func, scale=1.0, bias=0.0):
    return lambda e: e.activation(out=out, in_=in_, func=func, bias=bias, scale=scale)


def CP(out, in_):
    return lambda e: e.copy(out=out, in_=in_)


def TC(out, in_):
    return lambda e: e.tensor_copy(out=out, in_=in_)


def TT(out, in0, in1, op):
    return lambda e: e.tensor_tensor(out=out, in0=in0, in1=in1, op=op)


def STT(out, in0, scalar, in1, op0=None, op1=None):
    return lambda e: e.scalar_tensor_tensor(out=out, in0=in0, scalar=scalar, in1=in1, op0=op0, op1=op1)


def DMA(out, in_):
    return lambda e: e.dma_start(out=out, in_=in_)


class Arena:
    def __init__(self, t, nbytes):
        self.t = t
        self.nbytes = nbytes
        self.off = 0

    def at(self, off):
        self.off = off

    def alloc(self, shape, dt):
        n = int(np.prod(shape))
        sz = 4 if dt == F32 else 2
        off = (self.off + 63) // 64 * 64
        nb = n * sz
        assert off + nb <= self.nbytes, ("arena overflow", off, nb, self.nbytes)
        self.off = off + nb
        v = self.t[:, off // 2:(off + nb) // 2]
        if dt == F32:
            v = v.bitcast(F32)
        if len(shape) == 2:
            return v.rearrange("p (a b) -> p a b", a=shape[0], b=shape[1])
        if len(shape) == 3:
            return v.rearrange("p (a b c) -> p a b c", a=shape[0], b=shape[1], c=shape[2])
        return v


def _rope_tables():
    half = 32
    inv = (np.float32(10000.0) ** (-(np.arange(half, dtype=np.float32) / np.float32(half)))).astype(np.float32)
    p = np.arange(128)
    pos = np.zeros((128, 49), np.int64)
    for b in range(16):
        pos[:, b] = b * 128 + p
    for r in range(4):
        for mb in range(4):
            pos[:, 16 + r * 4 + mb] = (mb * 128 + p) * 4 + r
    for r in range(16):
        pos[:, 32 + r] = p * 16 + r
    pos[:, 48] = PAST + (p % 4)
    ang = pos.astype(np.float32)[:, :, None] * inv[None, None, :]
    tab = np.stack([np.cos(ang), np.sin(ang)], axis=2).astype(np.float32)
    return np.ascontiguousarray(tab)


def _consts():
    s = np.arange(128)[:, None]
    q = np.arange(128)[None, :]
    masks = np.stack([(s <= q), (s > q)], axis=1).astype(np.float32)
    t = np.arange(4)[None, :]
    sm = np.stack([(s > t), (s > 0) & (t >= 0), (s <= t), (s == t)], axis=1).astype(np.float32)
    return dict(ident=np.eye(128, dtype=np.float32), masks=np.ascontiguousarray(masks),
                smask=np.ascontiguousarray(sm), rope=_rope_tables())


def _prep_weights(w_in, w_br_a, conv_w, conv_b, ln):
    qa, ka, va = w_in[:, 0:512], w_in[:, 512:640], w_in[:, 640:768]
    qb, kb, vb = w_in[:, 768:1536], w_in[:, 1536:2304], w_in[:, 2304:3072]
    hperm = np.concatenate([np.arange(64) + 64 * h for c in range(4) for h in (c, 4 + c)])
    wN = np.concatenate([qa[:, hperm], ka, qb[:, 0:256], kb[:, 0:256], va, vb[:, 0:256]], axis=1)
    wD = np.concatenate([qb[:, 256:512], kb[:, 256:512], vb[:, 256:512],
                         qb[:, 512:768], kb[:, 512:768], vb[:, 512:768]], axis=1)
    wg = w_in[:, 3072:5120]
    return dict(wN=np.ascontiguousarray(wN), wD=np.ascontiguousarray(wD), wg=np.ascontiguousarray(wg),
                wbra=np.ascontiguousarray(w_br_a[hperm, :]),
                convw=np.ascontiguousarray(conv_w.reshape(3, NJ, 128).transpose(2, 1, 0)),
                convb=np.ascontiguousarray(conv_b.reshape(NJ, 128).T), ln=np.ascontiguousarray(ln),
                lncol=np.ascontiguousarray(ln.reshape(4, 8, 128).transpose(2, 0, 1)))


IN_SPECS = [
    ("xp", [SEQ, D]), ("xs", [NTS, D]),
    ("ca", [NS, 128, 256]), ("cb1", [NS, 128, 512]), ("cb2", [NS, 512, 512]), ("cb3", [NS, 2048, 512]),
    ("stc", [NS * 2, DFF]),
    ("wN", [D, 1536]), ("wD", [D, 1536]), ("wg", [D, 2048]), ("wbra", [512, D]), ("wbrb", [256, D]),
    ("wo", [D, D]), ("wup", [D, 2 * DFF]), ("wdn", [DFF, D]),
    ("convw", [128, NJ, 3]), ("convb", [128, NJ]), ("ln", [4, D]), ("lncol", [128, 4, 8]), ("sink", [1, 8]),
    ("ident", [128, 128]), ("masks", [128, 2, 128]), ("smask", [128, 4, 4]), ("rope", [128, 49, 2, 32]),
]
OUT_SPECS = [
    ("y_p", [SEQ, D]), ("y_s", [NTS, D]),
    ("ca_p", [128, 256]), ("ca_s", [NTS, 256]), ("cb1_p", [128, 512]), ("cb1_s", [NTS, 512]),
    ("cb2_p", [512, 512]), ("cb2_s", [NTS, 512]), ("cb3_p", [2048, 512]), ("cb3_s", [NTS, 512]),
    ("sc_p", [2, DFF]), ("sc_s", [NS * 2, DFF]),
]

PHASES = os.environ.get("MK_PHASES", "12345")
SUB = os.environ.get("MK_SUB", "abs")


def build():
    nc = bass.Bass("TRN2", target_bir_lowering=False)
    I = {n: nc.dram_tensor(n, s, F32, kind="ExternalInput").ap() for n, s in IN_SPECS}
    O = {n: nc.dram_tensor(n, s, F32, kind="ExternalOutput").ap() for n, s in OUT_SPECS}
    h_scr = nc.dram_tensor("h_scr", [NT, D], F32).ap()

    with ExitStack() as st:
        P = Prog(nc, st)
        ARENA_BYTES = 204 * 1024
        arena_t = st.enter_context(nc.sbuf_tensor("arena", [128, ARENA_BYTES // 2], BF16))
        A = Arena(arena_t, ARENA_BYTES)
        sb = lambda name, shape, dt: st.enter_context(nc.sbuf_tensor(name, shape, dt))
        ident = sb("ident_sb", [128, 128], F32)
        idb = sb("idb", [128, 128], BF16)
        mask_f = sb("mask_f", [128, 2, 128], F32)
        mask_b = sb("mask_b", [128, 2, 128], BF16)
        smask_f = sb("smask_f", [128, 4, 4], F32)
        smask_b = sb("smask_b", [128, 4, 4], BF16)
        ones_b = sb("ones_b", [128, 128], BF16)
        sink_f = sb("sink_f", [1, 8], F32)
        PS = st.enter_context(nc.psum_tensor("psum", [128, 4096], F32))

        def bank(b, n=512, o=0, p0=0, p1=128):
            return PS[p0:p1, b * 512 + o:b * 512 + o + n]

        def bank_bf(b):
            return PS[:, b * 512:(b + 1) * 512].bitcast(BF16)

        pb = lambda *bs: ["ps%d" % b for b in bs]

        P.dma("sp", "ident", lambda e: e.dma_start(out=ident[:], in_=I["ident"][:, :]), w=["ident"])
        P.dma("sp", "mask_f", lambda e: e.dma_start(out=mask_f[:], in_=I["masks"][:, :, :]), w=["mask_f"])
        P.dma("sp", "smask_f", lambda e: e.dma_start(out=smask_f[:], in_=I["smask"][:, :, :]), w=["smask_f"])
        P.dma("sp", "sink_f", lambda e: e.dma_start(out=sink_f[:], in_=I["sink"][:, :]), w=["sink_f"])
        P.op("pool", lambda e: e.tensor_copy(out=idb[:], in_=ident[:]), r=["ident"], w=["idb"])
        P.op("pool", lambda e: e.tensor_copy(out=mask_b[:], in_=mask_f[:]), r=["mask_f"], w=["mask_b"])
        P.op("pool", lambda e: e.tensor_copy(out=smask_b[:], in_=smask_f[:]), r=["smask_f"], w=["smask_b"])
        P.op("pool", lambda e: e.memset(ones_b[:], 1.0), w=["ones_b"])
        P.op("act", lambda e: e.activation(out=sink_f[:], in_=sink_f[:], func=AF.Exp), r=["sink_f"], w=["sink_f"])

        A.at(0)
        QKN = A.alloc([9, SEQ], BF16)
        QKD4 = A.alloc([4, SEQ], BF16)
        QKD16 = A.alloc([4, SEQ], BF16)
        VN = A.alloc([NB, 384], BF16)
        VD4 = A.alloc([NB, 256], BF16)
        VD16 = A.alloc([NB, 256], BF16)
        OFF_QKV_END = A.off
        QS = A.alloc([17, NTS], BF16)
        OFF_P = A.off
        oaT = A.alloc([4, NT], BF16)
        obT = A.alloc([2, NT], BF16)
        accN = A.alloc([2, NT], F32)
        accD = A.alloc([2, NT], F32)
        OFF_P2T = A.off

        if "1" in PHASES:
            A.at(OFF_P)
            xT = A.alloc([8, NT], BF16)
            Wb = A.alloc([8, 1536], BF16)
            rope = A.alloc([49, 64], F32)
            xst = [A.alloc([1, D], F32) for _ in range(2)]
            xbf = [A.alloc([1, D], BF16) for _ in range(2)]
            R = [A.alloc([1, 1536], F32) for _ in range(2)]
            Bt = A.alloc([1, 1152], F32)
            Rb = [A.alloc([1, 1152], BF16) for _ in range(3)]

            P.dma("sp", "rope", lambda e: e.dma_start(out=rope[:, :, :], in_=I["rope"].rearrange("p b c d -> p b (c d)")), w=["rope"])
            for c3 in range(3):
                P.dma("pool", "Wb%d" % c3, lambda e, c3=c3: e.dma_start(
                    out=Wb[:, :, c3 * 512:(c3 + 1) * 512],
                    in_=I["wN"].rearrange("(kc p) n -> p kc n", p=128)[:, :, c3 * 512:(c3 + 1) * 512]), w=["Wb%d" % c3])
            for tb in range(NB + 1):
                ntok = 128 if tb < NB else NTS
                src = I["xp"][tb * 128:(tb + 1) * 128, :] if tb < NB else I["xs"][:, :]
                xs_ = xst[tb % 2]
                P.dma("sp" if tb % 2 == 0 else "act", "xst%d" % (tb % 2),
                      lambda e, xs_=xs_, src=src, ntok=ntok: e.dma_start(out=xs_[0:ntok, 0, :], in_=src), w=["xst%d" % (tb % 2)])
                xb_ = xbf[tb % 2]
                P.op("act", CP(xb_[0:ntok, 0, :], xs_[0:ntok, 0, :]), r=["xst%d" % (tb % 2)], w=["xbf%d" % (tb % 2)])
                bk = 6 + tb % 2
                P.op("pe", [TR(bank_bf(bk)[:, kc * 128:kc * 128 + ntok], xb_[0:ntok, 0, kc * 128:(kc + 1) * 128], idb[0:ntok, 0:ntok])
                            for kc in range(8)], r=["xbf%d" % (tb % 2), "idb"], w=pb(bk))
                src_v = bank_bf(bk).rearrange("p (a b) -> p a b", b=128)[:, :, 0:ntok]
                dst_v = xT[:, 0:8, tb * 128:tb * 128 + ntok]
                P.op("dve", TC(dst_v, src_v), r=pb(bk), w=["xT%da" % tb, "xT%db" % tb])

            def xcols(kind, i):
                if kind == "N":
                    return lambda kc: xT[:, kc, i * 128:(i + 1) * 128]
                if kind == "S":
                    return lambda kc: xT[:, kc, SEQ:SEQ + NTS]
                if kind == "D4":
                    r_, mb = i // 4, i % 4
                    return lambda kc: xT[:, kc, mb * 512:(mb + 1) * 512].rearrange("p (m r) -> p r m", r=4)[:, r_, :]
                return lambda kc: xT[:, kc, 0:SEQ].rearrange("p (m r) -> p r m", r=16)[:, i, :]

            xn = lambda tbs: [("xT%d" % t) + h for t in tbs for h in "ab"]
            jobs = []
            for b in range(NB):
                outs = []
                if b == NB - 1:
                    outs = [(O["ca_p"][:, 0:128], 512, 128), (O["ca_p"][:, 128:256], 1152, 128),
                            (O["cb1_p"][:, 0:256], 896, 256), (O["cb1_p"][:, 256:512], 1280, 256)]
                jobs.append(dict(x=xcols("N", b), ntok=128, c0=0, nc_=1536, nqk=1152, ti=b,
                                 qk=[(QKN[:, 0:9, b * 128:(b + 1) * 128], 0, 9)],
                                 v=[(VN[:, b, :], 1152, 384)], outs=outs, wsel="N", xr=xn([b])))
            jobs.append(dict(x=xcols("S", 0), ntok=NTS, c0=0, nc_=1536, nqk=1152, ti=48,
                             qk=[(QS[:, 0:9, :], 0, 9)], v=[],
                             outs=[(O["ca_s"][:, 0:128], 512, 128), (O["ca_s"][:, 128:256], 1152, 128),
                                   (O["cb1_s"][:, 0:256], 896, 256), (O["cb1_s"][:, 256:512], 1280, 256)], wsel="N", xr=xn([NB])))
            for i in range(16):
                r_, mb = i // 4, i % 4
                outs = []
                if mb == 3:
                    dv = O["cb2_p"].rearrange("(p r) c -> r p c", r=4)
                    outs = [(dv[r_, :, 0:256], 256, 256), (dv[r_, :, 256:512], 512, 256)]
                jobs.append(dict(x=xcols("D4", i), ntok=128, c0=0, nc_=768, nqk=512, ti=16 + i,
                                 qk=[(QKD4[:, 0:4, i * 128:(i + 1) * 128], 0, 4)],
                                 v=[(VD4[:, i, :], 512, 256)], outs=outs, wsel="D", xr=xn(range(4 * mb, 4 * mb + 4))))
            for i in range(16):
                dv = O["cb3_p"].rearrange("(p r) c -> r p c", r=16)
                jobs.append(dict(x=xcols("D16", i), ntok=128, c0=768, nc_=768, nqk=512, ti=32 + i,
                                 qk=[(QKD16[:, 0:4, i * 128:(i + 1) * 128], 0, 4)],
                                 v=[(VD16[:, i, :], 512, 256)],
                                 outs=[(dv[i, :, 0:256], 256, 256), (dv[i, :, 256:512], 512, 256)], wsel="D", xr=xn(range(NB))))
            jobs.append(dict(x=xcols("S", 0), ntok=NTS, c0=0, nc_=1536, nqk=None, ti=48,
                             ropes=[(0, 512), (768, 512)],
                             qk=[(QS[:, 9:13, :], 0, 4), (QS[:, 13:17, :], 768, 4)], v=[],
                             outs=[(O["cb2_s"][:, 0:256], 256, 256), (O["cb2_s"][:, 256:512], 512, 256),
                                   (O["cb3_s"][:, 0:256], 1024, 256), (O["cb3_s"][:, 256:512], 1280, 256)], wsel="D", xr=xn([NB])))

            def emit_mm(j, jb):
                if jb["wsel"] == "D" and jobs[j - 1]["wsel"] == "N":
                    for c3 in range(3):
                        P.dma("pool", "Wb%d" % c3, lambda e, c3=c3: e.dma_start(
                            out=Wb[:, :, c3 * 512:(c3 + 1) * 512],
                            in_=I["wD"].rearrange("(kc p) n -> p kc n", p=128)[:, :, c3 * 512:(c3 + 1) * 512]), w=["Wb%d" % c3])
                s = j % 2
                ntok, c0, ncol = jb["ntok"], jb["c0"], jb["nc_"]
                for cc in range(0, ncol, 512):
                    n = min(512, ncol - cc)
                    bk = 3 * s + cc // 512
                    wnames = sorted(set(["Wb%d" % ((c0 + cc) // 512), "Wb%d" % ((c0 + cc + n - 1) // 512)]))
                    P.op("pe", [(lambda e, kc=kc, bk=bk, n=n, cc=cc: e.matmul(
                        bank(bk, n, 0, 0, ntok), jb["x"](kc), Wb[:, kc, c0 + cc:c0 + cc + n], start=(kc == 0), stop=(kc == 7)))
                        for kc in range(8)], r=jb["xr"] + wnames, w=pb(bk))

            def emit_rope(j, jb):
                s = j % 2
                ntok, ncol = jb["ntok"], jb["nc_"]
                banks = pb(*range(3 * s, 3 * s + (ncol + 511) // 512))
                psj = PS[0:ntok, 3 * s * 512:3 * s * 512 + ncol]
                Rj, Rbj = R[s], Rb[j % 3]
                cosb = rope[0:ntok, jb["ti"], 0:32]
                sinb = rope[0:ntok, jb["ti"], 32:64]
                ropes = jb.get("ropes") or [(0, jb["nqk"])]
                for (r0, rn) in ropes:
                    nh = rn // 64
                    pv = psj[:, r0:r0 + rn].rearrange("p (h two d) -> p h two d", two=2, d=32)
                    Rv = Rj[0:ntok, 0, r0:r0 + rn].rearrange("p (h two d) -> p h two d", two=2, d=32)
                    Bv = Bt[0:ntok, 0, 0:rn].rearrange("p (h two d) -> p h two d", two=2, d=32)
                    cos4 = cosb.unsqueeze(1).unsqueeze(1).to_broadcast([ntok, nh, 2, 32])
                    sin3 = sinb.unsqueeze(1).to_broadcast([ntok, nh, 32])
                    P.op("dve", lambda e, Rv=Rv, pv=pv, cos4=cos4: e.tensor_tensor(out=Rv, in0=pv, in1=cos4, op=ALU.mult),
                         r=banks + ["rope"], w=["R%d" % s])
                    P.op("dve", lambda e, Bv=Bv, pv=pv, sin3=sin3: e.scalar_tensor_tensor(
                        out=Bv[:, :, 0, :], in0=pv[:, :, 1, :], scalar=-1.0, in1=sin3, op0=ALU.mult, op1=ALU.mult),
                        r=banks + ["rope"], w=["Bt"])
                    P.op("dve", lambda e, Bv=Bv, pv=pv, sin3=sin3: e.tensor_tensor(
                        out=Bv[:, :, 1, :], in0=pv[:, :, 0, :], in1=sin3, op=ALU.mult), r=banks + ["rope"], w=["Bt"])
                    P.op("dve", TT(Rj[0:ntok, 0, r0:r0 + rn], Rj[0:ntok, 0, r0:r0 + rn], Bt[0:ntok, 0, 0:rn], ALU.add), r=["R%d" % s, "Bt"], w=["R%d" % s])
                if jb.get("ropes"):
                    vranges = [(512, 256), (1280, 256)]
                else:
                    vranges = [(jb["nqk"], ncol - jb["nqk"])]
                for (v0, vn) in vranges:
                    P.op("act", lambda e, d=Rj[0:ntok, 0, v0:v0 + vn], s_=psj[:, v0:v0 + vn]: e.copy(out=d, in_=s_),
                         r=banks, w=["RV%d" % s])
                for (dst, v0, vn) in jb["v"]:
                    P.op("act", lambda e, d=dst, s_=psj[:, v0:v0 + vn]: e.copy(out=d, in_=s_), r=banks)
                for (r0, rn) in ropes:
                    P.op("act", CP(Rbj[0:ntok, 0, jb_rb_off(jb, r0):jb_rb_off(jb, r0) + rn], Rj[0:ntok, 0, r0:r0 + rn]), r=["R%d" % s], w=["Rb%d" % (j % 3)])
                for k, (dram, c_, n_) in enumerate(jb["outs"]):
                    P.dma("sp", "R%d_o%d" % (s, k), lambda e, dram=dram, src=Rj[0:ntok, 0, c_:c_ + n_]: e.dma_start(out=dram, in_=src),
                          r=["R%d" % s, "RV%d" % s])

            def jb_rb_off(jb, r0):
                if jb.get("ropes"):
                    return 0 if r0 == 0 else 512
                return r0

            def emit_tr(j, jb):
                s = j % 3
                ntok = jb["ntok"]
                Rbj = Rb[s]
                pos = 0
                for (dst, c_, nch) in jb["qk"]:
                    rb0 = jb_rb_off(jb, c_) if jb.get("ropes") else c_
                    slots = list(range(pos, pos + nch))
                    for bk in (6, 7):
                        sl = [q_ for q_ in slots if q_ // 8 == bk - 6]
                        if not sl:
                            continue
                        fns = []
                        for q_ in sl:
                            o_ap = bank_bf(bk)[:, (q_ % 8) * 128:(q_ % 8) * 128 + ntok]
                            i_ap = Rbj[0:ntok, 0, rb0 + (q_ - pos) * 128:rb0 + (q_ - pos + 1) * 128]
                            fns.append(lambda e, o_ap=o_ap, i_ap=i_ap, id_ap=idb[0:ntok, 0:ntok]: e.transpose(o_ap, i_ap, id_ap))
                        P.op("pe", fns, r=["Rb%d" % s, "idb"], w=pb(bk))
                        srcv = bank_bf(bk).rearrange("p (a b) -> p a b", b=128)[:, sl[0] % 8:sl[-1] % 8 + 1, 0:ntok]
                        dstv = dst[:, sl[0] - pos:sl[-1] - pos + 1, :]
                        P.op("act", lambda e, d=dstv, s_=srcv: e.copy(out=d, in_=s_), r=pb(bk))
                    pos += nch

            for j in range(len(jobs) + 2):
                if j < len(jobs):
                    emit_mm(j, jobs[j])
                if j >= 2:
                    emit_tr(j - 2, jobs[j - 2])
                if j < len(jobs):
                    emit_rope(j, jobs[j])
            P.barrier()

        MUL, ADD = ALU.mult, ALU.add

        if "2" in PHASES:
            A.at(OFF_P2T)
            PT = [[A.alloc([1, 512], BF16) for _ in range(4)] for _ in range(2)]
            rec = [A.alloc([1, 512], F32) for _ in range(2)]
            sink_row = A.alloc([8, 128], BF16)
            P.op("dve", TC(sink_row[0:1, :, :], sink_f[0:1, :].unsqueeze(2).to_broadcast([1, 8, 128])), r=["sink_f"], w=["sink_row"])
            mbA = A.alloc([2, 512], BF16)
            mbB = A.alloc([4, 128], BF16)
            for wh in range(2):
                P.op("dve", lambda e, wh=wh: e.tensor_scalar(
                    out=mbA[:, wh, :].rearrange("p (h q) -> p h q", q=128), in0=mask_f[:, wh, :].unsqueeze(1).to_broadcast([128, 4, 128]),
                    scalar1=-1.0, scalar2=30000.0, op0=ALU.add, op1=ALU.mult), r=["mask_f"], w=["mbA"])
                P.op("dve", lambda e, wh=wh: e.tensor_scalar(
                    out=mbB[:, 2 * wh:2 * wh + 2, :], in0=mask_f[:, wh, :].unsqueeze(1).to_broadcast([128, 2, 128]),
                    scalar1=-1.0, scalar2=30000.0, op0=ALU.add, op1=ALU.mult), r=["mask_f"], w=["mbB"])

            def a_scores(b):
                par = b % 2
                whs = [0] + ([1] if b > 0 else [])
                items = [(kvh, wh) for wh in whs for kvh in range(2)]
                P.op("pe", [MM(bank(kvh * 2 + wh), idb[:, :], mbA[:, wh, :], start=True, stop=False) for (kvh, wh) in items],
                     r=["mbA", "idb"], w=pb(*[kvh * 2 + wh for (kvh, wh) in items]))
                for (kvh, wh) in items:
                    p0, p1 = kvh * 64, kvh * 64 + 64
                    kb = b - wh
                    bk = kvh * 2 + wh
                    P.op("pe", MM(bank(bk), QKN[p0:p1, 4, kb * 128:(kb + 1) * 128], QKN[p0:p1, 0:4, b * 128:(b + 1) * 128], start=False, stop=True),
                         w=pb(bk))
                for (kvh, wh) in items:
                    bk = kvh * 2 + wh
                    nm = "PT%d_%d" % (par, kvh * 2 + wh)
                    P.op("act", ACTF(PT[par][kvh * 2 + wh][:, 0, :], bank(bk), AF.Exp, scale=SCALE), r=pb(bk), w=[nm])

            def a_values(b):
                par = b % 2
                bo, bd = (4, 5) if par == 0 else (6, 7)
                lst = [(0, b)] + ([(1, b - 1)] if b > 0 else [])
                nms = ["PT%d_%d" % (par, kvh * 2 + wh) for wh, _ in lst for kvh in range(2)]
                fo, fd = [], []
                for i, (wh, kb) in enumerate(lst):
                    for kvh in range(2):
                        p0, p1 = kvh * 64, kvh * 64 + 64
                        pt = PT[par][kvh * 2 + wh][:, 0, :]
                        fo.append(MM(bank(bo, 512, 0, p0, p1), VN[:, kb, kvh * 64:(kvh + 1) * 64], pt, start=(i == 0), stop=(i == len(lst) - 1)))
                for i, (wh, kb) in enumerate(lst):
                    for kvh in range(2):
                        p0, p1 = kvh * 64, kvh * 64 + 64
                        pt = PT[par][kvh * 2 + wh][:, 0, :]
                        fd.append(MM(bank(bd, 512, 0, p0, p1), ones_b[:, 0:64], pt, start=(i == 0), stop=False))
                for kvh in range(2):
                    p0, p1 = kvh * 64, kvh * 64 + 64
                    fd.append(MM(bank(bd, 512, 0, p0, p1), ones_b[0:1, 0:64], sink_row[0:1, kvh * 4:(kvh + 1) * 4, :], start=False, stop=True))
                P.op("pe", fo, r=nms, w=pb(bo))
                P.op("pe", fd, r=nms + ["ones_b", "sink_row"], w=pb(bd))
                rc = rec[par]
                P.op("dve", lambda e, rc=rc, bd=bd: e.reciprocal(out=rc[:, 0, :], in_=bank(bd)), r=pb(bd), w=["rec%d" % par])
                P.op("dve", TT(oaT[:, 0:4, b * 128:(b + 1) * 128], bank(bo).rearrange("p (h q) -> p h q", q=128),
                               rc[:, 0, :].rearrange("p (h q) -> p h q", q=128), MUL), r=pb(bo) + ["rec%d" % par], ss=False)

            if "a" in SUB:
                a_scores(0)
                for b in range(NB):
                    if b + 1 < NB:
                        a_scores(b + 1)
                    a_values(b)

            blocks = []
            for i in range(NB):
                blocks.append(dict(g=0, QK=QKN, qc=5, kc=7, V=VN, vo=128, i=i, prev=(i - 1 if i > 0 else None),
                                   dst=lambda T_, i=i: T_[:, :, i * 128:(i + 1) * 128], span=[i // 4]))
            for i in range(16):
                r_, mb = i // 4, i % 4
                blocks.append(dict(g=1, QK=QKD4, qc=0, kc=2, V=VD4, vo=0, i=i, prev=(i - 1 if mb > 0 else None),
                                   dst=lambda T_, r_=r_, mb=mb: T_[:, :, mb * 512:(mb + 1) * 512].rearrange("p c (m r) -> p c r m", r=4)[:, :, r_, :],
                                   span=[mb]))
            for i in range(16):
                blocks.append(dict(g=2, QK=QKD16, qc=0, kc=2, V=VD16, vo=0, i=i, prev=None,
                                   dst=lambda T_, i=i: T_[:, :, 0:SEQ].rearrange("p c (m r) -> p c r m", r=16)[:, :, i, :],
                                   span=[0, 1, 2, 3]))

            def b_scores(k, bl):
                par = k % 2
                lst = [(0, bl["i"])] + ([(1, bl["prev"])] if bl["prev"] is not None else [])
                ncol = 256 * len(lst)
                bks = [2 * par, 2 * par + 1]
                fns = []
                for wi, (wh, kb) in enumerate(lst):
                    for c in range(2):
                        for half in range(2):
                            p0 = half * 64
                            fns.append(MM(bank(bks[half], 128, wh * 256 + c * 128), bl["QK"][p0:p0 + 64, bl["kc"] + c, kb * 128:(kb + 1) * 128],
                                          bl["QK"][p0:p0 + 64, bl["qc"] + c, bl["i"] * 128:(bl["i"] + 1) * 128]))
                P.op("pe", fns, w=pb(*bks))
                for half in range(2):
                    nm = "PT%d_%d" % (par, half)
                    pt = PT[par][half]
                    P.op("act", ACTF(pt[:, 0, 0:ncol], bank(bks[half], ncol), AF.Exp, scale=SCALE), r=pb(bks[half]), w=[nm])
                    ptv = pt[:, 0, 0:ncol].rearrange("p (w c q) -> p w c q", c=2, q=128)
                    mv = mask_b[:, 0:len(lst), :].unsqueeze(2).to_broadcast([128, len(lst), 2, 128])
                    P.op("dve", TT(ptv, ptv, mv, MUL), r=[nm, "mask_b"], w=[nm])

            def b_values(k, bl):
                par = k % 2
                bo, bd = (4, 5) if par == 0 else (6, 7)
                lst = [(0, bl["i"])] + ([(1, bl["prev"])] if bl["prev"] is not None else [])
                nms = ["PT%d_%d" % (par, hf_) for hf_ in range(2)]
                fo, fd = [], []
                for c in range(2):
                    for i_, (wh, kb) in enumerate(lst):
                        for hf_ in range(2):
                            h, p0 = 2 * c + hf_, hf_ * 64
                            pt = PT[par][hf_][:, 0, wh * 256 + c * 128:wh * 256 + (c + 1) * 128]
                            fo.append(MM(bank(bo, 128, c * 128, p0, p0 + 64), bl["V"][:, kb, bl["vo"] + h * 64:bl["vo"] + (h + 1) * 64], pt,
                                         start=(i_ == 0), stop=(i_ == len(lst) - 1)))
                for c in range(2):
                    for i_, (wh, kb) in enumerate(lst):
                        for hf_ in range(2):
                            p0 = hf_ * 64
                            pt = PT[par][hf_][:, 0, wh * 256 + c * 128:wh * 256 + (c + 1) * 128]
                            fd.append(MM(bank(bd, 128, c * 128, p0, p0 + 64), ones_b[:, 0:64], pt, start=(i_ == 0), stop=(i_ == len(lst) - 1)))
                P.op("pe", fo, r=nms, w=pb(bo))
                P.op("pe", fd, r=nms + ["ones_b"], w=pb(bd))
                names = ["acc%d" % sp for sp in bl["span"]]
                srcO = bank(bo, 256).rearrange("p (c q) -> p c q", q=128)
                srcD = bank(bd, 256).rearrange("p (c q) -> p c q", q=128)
                if bl["g"] == 0:
                    P.op("act", CP(bl["dst"](accN), srcO), r=pb(bo), w=names, ss=True)
                    P.op("act", CP(bl["dst"](accD), srcD), r=pb(bd), w=names, ss=True)
                else:
                    P.op("dve", TT(bl["dst"](accN), srcO, bl["dst"](accN), ADD), r=pb(bo), w=names, ss=(bl["g"] != 2))
                    P.op("dve", TT(bl["dst"](accD), srcD, bl["dst"](accD), ADD), r=pb(bd), w=names, ss=(bl["g"] != 2))

            if "b" in SUB:
                b_scores(0, blocks[0])
                for k in range(len(blocks)):
                    if k + 1 < len(blocks):
                        b_scores(k + 1, blocks[k + 1])
                    b_values(k, blocks[k])
            P.barrier()


        P3W_LO = 84224
        P3W = {"issued": False}

        def p3w():
            A.at(P3W_LO)
            Wbra_ = A.alloc([4, D], BF16)
            Wbrb_ = A.alloc([2, D], BF16)
            assert A.off <= OFF_QKV_END
            A.at(OFF_P2T)
            Wg_ = A.alloc([8, 2048], BF16)
            Wo_ = A.alloc([8, D], BF16)
            return Wg_, Wbra_, Wbrb_, Wo_

        def p3w_issue():
            P3W["issued"] = True
            Wg, Wbra, Wbrb, Wo = p3w()
            wgv = I["wg"].rearrange("(kc p) n -> p kc n", p=128)
            def _wg(c4):
                P.dma("pool", "Wg%d" % c4, DMA(Wg[:, :, c4 * 512:(c4 + 1) * 512], wgv[:, :, c4 * 512:(c4 + 1) * 512]), w=["Wg%d" % c4])

            def _wbr(c2):
                P.dma("pool", "Wbra%d" % c2, DMA(Wbra[:, :, c2 * 512:(c2 + 1) * 512],
                                                I["wbra"].rearrange("(kc p) n -> p kc n", p=128)[:, :, c2 * 512:(c2 + 1) * 512]), w=["Wbra%d" % c2])
                P.dma("pool", "Wbrb%d" % c2, DMA(Wbrb[:, :, c2 * 512:(c2 + 1) * 512],
                                                I["wbrb"].rearrange("(kc p) n -> p kc n", p=128)[:, :, c2 * 512:(c2 + 1) * 512]), w=["Wbrb%d" % c2])

            _wg(0); _wg(2); _wbr(0); _wg(1); _wg(3); _wbr(1)
            for c2 in range(2):
                P.dma("pool", "Wo%d" % c2, DMA(Wo[:, :, c2 * 512:(c2 + 1) * 512],
                                              I["wo"].rearrange("(kc p) n -> p kc n", p=128)[:, :, c2 * 512:(c2 + 1) * 512]), w=["Wo%d" % c2])

        if "2" in PHASES and "s" in SUB:
            A.at(0)
            Cs = [A.alloc([4, 512], F32) for _ in range(3)]
            Nw = [A.alloc([1, 512], F32) for _ in range(2)]
            Kb = [A.alloc([4, 256], BF16) for _ in range(2)]
            Knb = [A.alloc([1, 256], BF16) for _ in range(2)]
            KTs = [A.alloc([1, 1088], BF16) for _ in range(2)]
            Vb = A.alloc([NS, 1024], BF16)
            Vnb = A.alloc([NS, 256], BF16)
            PTc = A.alloc([2, 256], BF16)
            PTn = A.alloc([2, 256], BF16)
            recs = A.alloc([1, 256], F32)
            prods = A.alloc([1, 256], F32)
            sink2 = A.alloc([2, 256], BF16)
            assert A.off <= P3W_LO, (A.off, P3W_LO)
            if "3" in PHASES:
                p3w_issue()
            P.op("dve", TC(sink2[0:1, :, :].rearrange("p k (n c) -> p k n c", c=4),
                           sink_f[0:1, :].rearrange("p (k c) -> p k c", c=4).unsqueeze(2).to_broadcast([1, 2, NTS, 4])), r=["sink_f"], w=["sink2"])
            sgroups = [dict(nm="A", cache=I["ca"], new=O["ca_s"], HD=128, dil=0, qb=0, nh2=4, mk=(0, 2)),
                       dict(nm="b1", cache=I["cb1"], new=O["cb1_s"], HD=256, dil=0, qb=5, nh2=2, mk=(0, 2)),
                       dict(nm="b2", cache=I["cb2"], new=O["cb2_s"], HD=256, dil=4, qb=9, nh2=2, mk=(1, 3)),
                       dict(nm="b3", cache=I["cb3"], new=O["cb3_s"], HD=256, dil=16, qb=13, nh2=2, mk=(1, 3))]
            cnt = 0
            for gi, G in enumerate(sgroups):
                HD, dil, nh2, qb = G["HD"], G["dil"], G["nh2"], G["qb"]
                isA = G["nm"] == "A"
                T = 1 if dil == 0 else 4
                nch = HD // 128
                nc2 = NS * 4 * nh2
                Sc = [bank(rg, nc2).rearrange("p (n t h) -> p n t h", t=4, h=nh2) for rg in range(2)]
                Sn = [bank(2 + rg, nc2).rearrange("p (n t h) -> p n t h", t=4, h=nh2) for rg in range(2)]
                base = cnt
                cnt += NS

                def s_load(n):
                    q3, par = (base + n) % 3, (base + n) % 2
                    C_, N_ = Cs[q3], Nw[par]
                    cn, nn = "C%d" % q3, "Nw%d" % par
                    if dil == 0:
                        P.dma("sp", cn, DMA(C_[:, 0, 0:2 * HD], G["cache"][n, :, :]), w=[cn])
                    else:
                        P.dma("sp", cn, DMA(C_[:, 0:4, :], G["cache"][n].rearrange("(m r) c -> m r c", r=dil)[:, 0:4, :]), w=[cn])
                    P.dma("act", nn, DMA(N_[0:4, 0, 0:2 * HD], G["new"][n * 4:(n + 1) * 4, :]), w=[nn])

                def s_pre(n):
                    q3, par = (base + n) % 3, (base + n) % 2
                    C_, N_, K_, Kn_, KT_ = Cs[q3], Nw[par], Kb[par], Knb[par], KTs[par]
                    cn, nn, kn, knn, ktn = "C%d" % q3, "Nw%d" % par, "Kb%d" % par, "Knb%d" % par, "KTs%d" % par
                    P.op("dve", TC(K_[:, 0:T, 0:HD], C_[:, 0:T, 0:HD]), r=[cn], w=[kn])
                    P.op("dve", TC(Vb[:, n, 0:T * HD].rearrange("p (t d) -> p t d", d=HD), C_[:, 0:T, HD:2 * HD]), r=[cn], w=["Vb"], ss=True)
                    P.op("act", CP(Kn_[0:4, 0, 0:HD], N_[0:4, 0, 0:HD]), r=[nn], w=[knn])
                    P.op("act", CP(Vnb[0:4, n, 0:HD], N_[0:4, 0, HD:2 * HD]), r=[nn], w=["Vnb"], ss=True)
                    bk = 6 + par
                    fns = []
                    for tau in range(T):
                        for ch in range(nch):
                            sl = tau * nch + ch
                            fns.append(TR(bank_bf(bk)[:, sl * 128:(sl + 1) * 128], K_[:, tau, ch * 128:(ch + 1) * 128], idb[:, :]))
                    P.op("pe", fns, r=[kn, "idb"], w=pb(bk))
                    P.op("pe", [TR(bank_bf(5)[:, par * 64 + ch * 4:par * 64 + ch * 4 + 4], Kn_[0:4, 0, ch * 128:(ch + 1) * 128], idb[0:4, 0:4])
                                for ch in range(nch)], r=[knn, "idb"], w=pb(5))
                    P.op("act", CP(KT_[:, 0, 0:T * nch * 128], bank_bf(bk)[:, 0:T * nch * 128]), r=pb(bk), w=[ktn])
                    P.op("act", CP(KT_[:, 0, 1024:1024 + nch * 4], bank_bf(5)[:, par * 64:par * 64 + nch * 4]), r=pb(5), w=[ktn])

                def s_scores(n):
                    par = (base + n) % 2
                    KT_, ktn = KTs[par], "KTs%d" % par
                    fl = {0: ([], []), 1: ([], [])}
                    for rg in range(2):
                        p0 = rg * 64
                        fns, fnn = fl[rg]
                        for hh in range(nh2):
                            kcol = 0 if isA else hh
                            qch = hh if isA else qb + hh
                            q_all = QS[p0:p0 + 64, qch, n * 4:(n + 1) * 4]
                            if dil == 0:
                                fns.append(MM(Sc[rg][:, n, :, hh], KT_[p0:p0 + 64, 0, kcol * 128:(kcol + 1) * 128], q_all))
                            else:
                                for t in range(4):
                                    fns.append(MM(Sc[rg][:, n, t, hh:hh + 1], KT_[p0:p0 + 64, 0, (t * 2 + kcol) * 128:(t * 2 + kcol + 1) * 128],
                                                  QS[p0:p0 + 64, qch, n * 4 + t:n * 4 + t + 1]))
                            fnn.append(MM(Sn[rg][0:4, n, :, hh], KT_[p0:p0 + 64, 0, 1024 + kcol * 4:1024 + kcol * 4 + 4], q_all))
                    mix_ = lambda a, b_: [x_ for pr in zip(a, b_) for x_ in pr]
                    P.op("pe", mix_(fl[0][0], fl[1][0]), r=[ktn], w=pb(0, 1))
                    P.op("pe", mix_(fl[0][1], fl[1][1]), r=[ktn], w=pb(2, 3))

                s_load(0)
                s_load(1)
                s_pre(0)
                for n in range(NS):
                    if n + 2 < NS:
                        s_load(n + 2)
                    if n + 1 < NS:
                        s_pre(n + 1)
                    s_scores(n)
                    if gi == 0 and "b" in SUB:
                        sl_ = slice(n * 128, (n + 1) * 128)
                        P.op("dve", lambda e, sl_=sl_: e.reciprocal(out=accD[:, :, sl_], in_=accD[:, :, sl_]), w=["obfin"], ss=True)
                        P.op("dve", TT(obT[:, :, sl_], accN[:, :, sl_], accD[:, :, sl_], MUL), r=["obfin"], w=["obfin2"])
                mc, mn = G["mk"]
                for rg in range(2):
                    P.op("act", ACTF(PTc[:, rg, 0:nc2], bank(rg, nc2), AF.Exp, scale=SCALE), r=pb(rg), w=["PTc%d" % rg])
                    P.op("act", ACTF(PTn[0:4, rg, 0:nc2], bank(2 + rg, nc2, 0, 0, 4), AF.Exp, scale=SCALE), r=pb(2 + rg), w=["PTn%d" % rg])
                    pcv = PTc[:, rg, 0:nc2].rearrange("p (n t h) -> p n t h", t=4, h=nh2)
                    pnv = PTn[0:4, rg, 0:nc2].rearrange("p (n t h) -> p n t h", t=4, h=nh2)
                    P.op("pool", TT(pcv, pcv, smask_b[:, mc, :].unsqueeze(1).unsqueeze(3).to_broadcast([128, NS, 4, nh2]), MUL),
                         r=["PTc%d" % rg, "smask_b"], w=["PTc%d" % rg])
                    P.op("pool", TT(pnv, pnv, smask_b[0:4, mn, :].unsqueeze(1).unsqueeze(3).to_broadcast([4, NS, 4, nh2]), MUL),
                         r=["PTn%d" % rg, "smask_b"], w=["PTn%d" % rg])
                ptn = ["PTc0", "PTc1", "PTn0", "PTn1"]
                fo_l, fd_l = {0: [], 1: []}, {0: [], 1: []}
                for rg in range(2):
                    fo, fd = fo_l[rg], fd_l[rg]
                    p0, p1 = rg * 64, rg * 64 + 64
                    pc = PTc[:, rg, 0:nc2].rearrange("p (n t h) -> p n t h", t=4, h=nh2)
                    pn = PTn[0:4, rg, 0:nc2].rearrange("p (n t h) -> p n t h", t=4, h=nh2)
                    if isA:
                        fd.append(MM(bank(0, 256, 0, p0, p1), ones_b[:, 0:64], PTc[:, rg, 0:256], start=True, stop=False))
                        fd.append(MM(bank(0, 256, 0, p0, p1), ones_b[0:4, 0:64], PTn[0:4, rg, 0:256], start=False, stop=False))
                        fd.append(MM(bank(0, 256, 0, p0, p1), ones_b[0:1, 0:64], sink2[0:1, rg, :], start=False, stop=True))
                        for n in range(NS):
                            fo.append(MM(bank(4, 16, n * 16, p0, p1), Vnb[0:4, n, p0:p1], PTn[0:4, rg, n * 16:(n + 1) * 16], start=True, stop=False))
                            fo.append(MM(bank(4, 16, n * 16, p0, p1), Vb[:, n, p0:p1], PTc[:, rg, n * 16:(n + 1) * 16], start=False, stop=True))
                    else:
                        for c in range(2):
                            o_all = bank(0, NTS, c * NTS, p0, p1)
                            fd.append(MM(o_all, ones_b[:, 0:64], pc[:, :, :, c], start=True, stop=False))
                            fd.append(MM(o_all, ones_b[0:4, 0:64], pn[:, :, :, c], start=False, stop=True))
                            vcol = c * 128 + rg * 64
                            for n in range(NS):
                                o_n = bank(4, 4, c * NTS + n * 4, p0, p1)
                                fo.append(MM(o_n, Vnb[0:4, n, vcol:vcol + 64], pn[:, n, :, c], start=True, stop=False, skip=(dil != 0)))
                                if dil == 0:
                                    fo.append(MM(o_n, Vb[:, n, vcol:vcol + 64], pc[:, n, :, c], start=False, stop=True))
                                else:
                                    for t in range(4):
                                        fo.append(MM(bank(4, 1, c * NTS + n * 4 + t, p0, p1), Vb[:, n, t * 256 + vcol:t * 256 + vcol + 64],
                                                     pc[:, n, t, c:c + 1], start=False, stop=(t == 3), skip=True))
                mix2 = lambda a, b_: [x_ for pr in zip(a, b_) for x_ in pr]
                P.op("pe", mix2(fd_l[0], fd_l[1]), r=ptn + ["ones_b", "sink2"], w=pb(0))
                P.op("pe", mix2(fo_l[0], fo_l[1]), r=ptn + ["Vb", "Vnb"], w=pb(4))
                if isA:
                    P.op("dve", lambda e: e.reciprocal(out=recs[:, 0, :], in_=bank(0, 256)), r=pb(0), w=["recs"])
                    P.op("dve", TT(prods[:, 0, :], bank(4, 256), recs[:, 0, :], MUL), r=pb(4) + ["recs"], w=["prods"])
                    P.op("pool", TC(oaT[:, 0:4, SEQ:SEQ + NTS], prods[:, 0, :].rearrange("p (n c) -> p c n", c=4)), r=["prods"])
                else:
                    dN = accN[:, :, SEQ:SEQ + NTS]
                    dD = accD[:, :, SEQ:SEQ + NTS]
                    sN = bank(4, 2 * NTS).rearrange("p (c n) -> p c n", c=2)
                    sD = bank(0, 2 * NTS).rearrange("p (c n) -> p c n", c=2)
                    if gi == 1:
                        P.op("dve", TC(dN, sN), r=pb(4), w=["accS"])
                        P.op("dve", TC(dD, sD), r=pb(0), w=["accS"])
                    else:
                        P.op("dve", TT(dN, sN, dN, ADD), r=pb(4), w=["accS"])
                        P.op("dve", TT(dD, sD, dD, ADD), r=pb(0), w=["accS"])
            P.barrier()
            lo_ = SEQ if ("s" in SUB and "b" in SUB and "2" in PHASES) else 0
            P.op("dve", lambda e: e.reciprocal(out=accD[:, :, lo_:NT], in_=accD[:, :, lo_:NT]), w=["accDf"])
            P.op("dve", TT(obT[:, :, lo_:NT], accN[:, :, lo_:NT], accD[:, :, lo_:NT], MUL), r=["accDf"])
            P.barrier()

        def emit_ln(tag, z, ntok, out, lnb, gi):
            st_ = lnst[tag]
            P.op("dve", [lambda e: e.bn_stats(out=st_[0:ntok, 0, 0:6], in_=z[0:ntok, 0, 0:512]),
                         lambda e: e.bn_stats(out=st_[0:ntok, 0, 6:12], in_=z[0:ntok, 0, 512:1024])], r=[z_name[tag]], w=["st" + tag])
            P.op("dve", lambda e: e.bn_aggr(out=st_[0:ntok, 0, 12:14], in_=st_[0:ntok, 0, 0:12]), r=["st" + tag], w=["mv" + tag])
            P.op("act", ACTF(st_[0:ntok, 0, 14:15], st_[0:ntok, 0, 13:14], AF.Sqrt, bias=EPS), r=["mv" + tag], w=["rs" + tag])
            P.op("dve", lambda e: e.reciprocal(out=st_[0:ntok, 0, 14:15], in_=st_[0:ntok, 0, 14:15]), r=["rs" + tag], w=["rs" + tag])
            P.op("dve", lambda e: e.tensor_scalar(out=st_[0:ntok, 0, 15:16], in0=st_[0:ntok, 0, 12:13], scalar1=-1.0,
                                                  scalar2=st_[0:ntok, 0, 14:15], op0=ALU.mult, op1=ALU.mult),
                 r=["mv" + tag, "rs" + tag], w=["nm" + tag])
            P.op("act", ACTF(out[0:ntok, 0, :], z[0:ntok, 0, :], AF.Identity, scale=st_[0:ntok, 0, 14:15], bias=st_[0:ntok, 0, 15:16]),
                 r=[z_name[tag], "rs" + tag, "nm" + tag], w=[out_name[tag]])
            P.op("dve", TT(out[0:ntok, 0, :], out[0:ntok, 0, :], lnb[0:ntok, gi, :], MUL), r=[out_name[tag], "lnb"], w=[out_name[tag]])
            P.op("dve", TT(out[0:ntok, 0, :], out[0:ntok, 0, :], lnb[0:ntok, gi + 1, :], ADD), r=[out_name[tag], "lnb"], w=[out_name[tag]])

        lnst, z_name, out_name = {}, {}, {}
        tiles = [(0, 512), (512, 512), (1024, 512), (1536, 512), (SEQ, NTS)]

        def xrows(t0, n):
            return I["xp"][t0:t0 + n, :] if t0 < SEQ else I["xs"][t0 - SEQ:t0 - SEQ + n, :]

        if "3" in PHASES:
            A.at(0)
            hT = A.alloc([8, NT], BF16)
            OFF_HT_END = A.off
            Wg, Wbra, Wbrb, Wo = p3w()
            A.at(OFF_HT_END)
            xs6 = [A.alloc([1, D], F32) for _ in range(6)]
            lnb = A.alloc([2, D], F32)
            xTt = A.alloc([8, 512], BF16)
            mT = A.alloc([8, 512], BF16)
            gcol = A.alloc([2, 8], F32)
            st3 = [A.alloc([1, 16], F32) for _ in range(2)]
            assert A.off <= P3W_LO, (A.off, P3W_LO)
            A.at(OFF_P2T - 2 * 2 * NT * 4)
            sg = [A.alloc([1, 512], F32) for _ in range(4)]
            xb3 = [A.alloc([1, D], BF16) for _ in range(2)]
            hb3 = [A.alloc([1, D], BF16) for _ in range(3)]
            hh = [A.alloc([1, D], F32) for _ in range(3)]
            assert A.off <= OFF_P2T, (A.off, OFF_P2T)
            P.dma("sp", "lnb", DMA(lnb[:, 0, :], I["ln"][0:1, :].partition_broadcast(128)), w=["lnb"])
            P.dma("sp", "lnb", DMA(lnb[:, 1, :], I["ln"][1:2, :].partition_broadcast(128)), w=["lnb"])
            P.dma("sp", "gcol", DMA(gcol[:, :, :], I["lncol"][:, 0:2, :]), w=["gcol"])
            if not P3W["issued"]:
                p3w_issue()
            lnst["3a"], lnst["3b"] = st3[0], st3[1]
            subs = []
            for ti, (t0, TT_) in enumerate(tiles):
                for sbk in range((TT_ + 127) // 128):
                    subs.append((ti, t0 + sbk * 128, min(128, TT_ - sbk * 128)))
            xbuf = {sidx: sidx % 6 for sidx in range(len(subs))}

            def p3_load(ti, late=None):
                busy = set(xbuf[q] for q, sq in enumerate(subs) if sq[0] == ti - 1)
                for sidx, (tj, s0, n) in enumerate(subs):
                    if tj == ti:
                        bi = xbuf[sidx]
                        if late is not None and ((bi in busy) != late):
                            continue
                        P.dma("sp", "xs%d" % bi, DMA(xs6[bi][0:n, 0, :], xrows(s0, n)), w=["xs%d" % bi])

            def p3_xT(ti, late=None):
                busy = set(xbuf[q] for q, sq in enumerate(subs) if sq[0] == ti - 1)
                for sidx, (tj, s0, n) in enumerate(subs):
                    if tj != ti:
                        continue
                    bi = xbuf[sidx]
                    if late is not None and ((bi in busy) != late):
                        continue
                    c0 = s0 - tiles[ti][0]
                    xb_ = xb3[sidx % 2]
                    if sidx % 2 == 0:
                        P.op("act", CP(xb_[0:n, 0, :], xs6[bi][0:n, 0, :]), r=["xs%d" % bi], w=["xb3_%d" % (sidx % 2)])
                    else:
                        P.op("dve", TC(xb_[0:n, 0, :], xs6[bi][0:n, 0, :]), r=["xs%d" % bi], w=["xb3_%d" % (sidx % 2)])
                    bk = 6 + sidx % 2
                    P.op("pe", [TR(bank_bf(bk)[:, kc * 128:kc * 128 + n], xb_[0:n, 0, kc * 128:(kc + 1) * 128], idb[0:n, 0:n]) for kc in range(8)],
                         r=["xb3_%d" % (sidx % 2), "idb"], w=pb(bk))
                    srcv = bank_bf(bk).rearrange("p (a b) -> p a b", b=128)[:, :, 0:n]
                    dstv = xTt[:, 0:8, c0:c0 + n]
                    if sidx % 2 == 0:
                        P.op("act", CP(dstv, srcv), r=pb(bk), w=["xTt"], ss=True)
                    else:
                        P.op("dve", TC(dstv, srcv), r=pb(bk), w=["xTt"], ss=True)

            def p3_gates(ti):
                t0, TT_ = tiles[ti]
                for f in range(8):
                    s_ = f % 2
                    bga, bgb, bra, brb = 4 * s_, 4 * s_ + 1, 4 * s_ + 2, 4 * s_ + 3
                    fcol = slice(f * 128, (f + 1) * 128)
                    P.op("pe", [MM(bank(bga, TT_), Wg[:, kc, f * 128:(f + 1) * 128], xTt[:, kc, 0:TT_], start=(kc == 0), stop=(kc == 7)) for kc in range(8)],
                         r=["xTt", "Wg%d" % (f // 4)], w=pb(bga))
                    P.op("pe", [MM(bank(bgb, TT_), Wg[:, kc, 1024 + f * 128:1024 + (f + 1) * 128], xTt[:, kc, 0:TT_], start=(kc == 0), stop=(kc == 7)) for kc in range(8)],
                         r=["xTt", "Wg%d" % (2 + f // 4)], w=pb(bgb))
                    P.op("pe", [MM(bank(bra, TT_), Wbra[:, c, f * 128:(f + 1) * 128], oaT[:, c, t0:t0 + TT_], start=(c == 0), stop=(c == 3)) for c in range(4)],
                         r=["Wbra%d" % (f // 4)], w=pb(bra))
                    P.op("pe", [MM(bank(brb, TT_), Wbrb[:, c, f * 128:(f + 1) * 128], obT[:, c, t0:t0 + TT_], start=(c == 0), stop=(c == 1)) for c in range(2)],
                         r=["Wbrb%d" % (f // 4)], w=pb(brb))
                    sa, sb_ = sg[2 * s_], sg[2 * s_ + 1]
                    P.op("act", ACTF(sa[:, 0, 0:TT_], bank(bga, TT_), AF.Sigmoid), r=pb(bga), w=["sg%d" % (2 * s_)])
                    P.op("act", ACTF(sb_[:, 0, 0:TT_], bank(bgb, TT_), AF.Sigmoid), r=pb(bgb), w=["sg%d" % (2 * s_ + 1)])
                    P.op("dve", TT(sa[:, 0, 0:TT_], bank(bra, TT_), sa[:, 0, 0:TT_], MUL), r=pb(bra), w=["sg%d" % (2 * s_)])
                    P.op("dve", TT(sb_[:, 0, 0:TT_], bank(brb, TT_), sb_[:, 0, 0:TT_], MUL), r=pb(brb), w=["sg%d" % (2 * s_ + 1)])
                    P.op("pool" if f < 6 else "dve", TT(mT[:, f, 0:TT_], sa[:, 0, 0:TT_], sb_[:, 0, 0:TT_], ADD),
                         r=["sg%d" % (2 * s_), "sg%d" % (2 * s_ + 1)], w=["mT"], ss=(f < 6))

            def p3_mix(sidx, pre_hT=None):
                tj, s0, n = subs[sidx]
                bi = xbuf[sidx]
                c0 = s0 - tiles[tj][0]
                b0 = 2 * (sidx % 2)
                q3 = sidx % 3
                zx = xs6[bi]
                for hf in range(2):
                    P.op("pe", [MM(bank(b0 + hf, 512, 0, 0, n), mT[:, kc, c0:c0 + n], Wo[:, kc, hf * 512:(hf + 1) * 512], start=(kc == 0), stop=(kc == 7))
                                for kc in range(8)], r=["mT", "Wo%d" % hf], w=pb(b0 + hf))
                if pre_hT is not None:
                    p3_hT(pre_hT)
                for hf in range(2):
                    P.op("dve", STT(zx[0:n, 0, hf * 512:(hf + 1) * 512], zx[0:n, 0, hf * 512:(hf + 1) * 512], ALPHA, bank(b0 + hf, 512, 0, 0, n), MUL, ADD),
                         r=pb(b0 + hf), w=["xs%d" % bi])
                st_ = st3[sidx % 2]
                tg = "3" + "ab"[sidx % 2]
                P.op("dve", [lambda e: e.bn_stats(out=st_[0:n, 0, 0:6], in_=zx[0:n, 0, 0:512]),
                             lambda e: e.bn_stats(out=st_[0:n, 0, 6:12], in_=zx[0:n, 0, 512:1024])], r=["xs%d" % bi], w=["st" + tg])
                P.op("dve", lambda e: e.bn_aggr(out=st_[0:n, 0, 12:14], in_=st_[0:n, 0, 0:12]), r=["st" + tg], w=["mv" + tg])
                P.op("act", ACTF(st_[0:n, 0, 14:15], st_[0:n, 0, 13:14], AF.Sqrt, bias=EPS), r=["mv" + tg], w=["rs" + tg])
                P.op("dve", lambda e: e.reciprocal(out=st_[0:n, 0, 14:15], in_=st_[0:n, 0, 14:15]), r=["rs" + tg], w=["rs" + tg])
                P.op("dve", lambda e: e.tensor_scalar(out=st_[0:n, 0, 15:16], in0=st_[0:n, 0, 12:13], scalar1=-1.0,
                                                      scalar2=st_[0:n, 0, 14:15], op0=ALU.mult, op1=ALU.mult), r=["mv" + tg, "rs" + tg], w=["nm" + tg])
                hbf_, hb_ = hb3[q3], hh[q3]
                P.op("act", ACTF(hbf_[0:n, 0, :], zx[0:n, 0, :], AF.Identity, scale=st_[0:n, 0, 14:15], bias=st_[0:n, 0, 15:16]),
                     r=["xs%d" % bi, "rs" + tg, "nm" + tg], w=["hb3_%d" % q3])
                P.op("act", ACTF(hb_[0:n, 0, :], zx[0:n, 0, :], AF.Identity, scale=st_[0:n, 0, 14:15], bias=st_[0:n, 0, 15:16]),
                     r=["xs%d" % bi, "rs" + tg, "nm" + tg], w=["hh%d" % q3])
                P.op("pool", TT(hb_[0:n, 0, :], hb_[0:n, 0, :], lnb[0:n, 0, :], MUL), r=["hh%d" % q3, "lnb"], w=["hh%d" % q3])
                P.op("pool", TT(hb_[0:n, 0, :], hb_[0:n, 0, :], lnb[0:n, 1, :], ADD), r=["hh%d" % q3, "lnb"], w=["hh%d" % q3])
                P.dma("pool", "hh%d" % q3, DMA(h_scr[s0:s0 + n, :], hb_[0:n, 0, :]), r=["hh%d" % q3])

            def p3_hT(sidx):
                tj, s0, n = subs[sidx]
                q3 = sidx % 3
                hbf_ = hb3[q3]
                P.op("pe", [TR(bank_bf(4)[:, kc * 128:kc * 128 + n], hbf_[0:n, 0, kc * 128:(kc + 1) * 128], idb[0:n, 0:n]) for kc in range(4)],
                     r=["hb3_%d" % q3, "idb"], w=pb(4))
                P.op("pe", [TR(bank_bf(5)[:, (kc - 4) * 128:(kc - 4) * 128 + n], hbf_[0:n, 0, kc * 128:(kc + 1) * 128], idb[0:n, 0:n]) for kc in range(4, 8)],
                     r=["hb3_%d" % q3, "idb"], w=pb(5))
                for kc in range(4):
                    P.op("dve", STT(hT[:, kc, s0:s0 + n], bank_bf(4)[:, kc * 128:kc * 128 + n], gcol[:, 0, kc:kc + 1],
                                    gcol[:, 1, kc:kc + 1].to_broadcast([128, n]), MUL, ADD), r=pb(4) + ["gcol"], ss=True)
                for kc in range(4, 8):
                    P.op("act", ACTF(hT[:, kc, s0:s0 + n], bank_bf(5)[:, (kc - 4) * 128:(kc - 4) * 128 + n], AF.Identity,
                                     scale=gcol[:, 0, kc:kc + 1], bias=gcol[:, 1, kc:kc + 1]), r=pb(5) + ["gcol"], ss=True)

            def p3_post(ti):
                ss_ = [q for q, sq in enumerate(subs) if sq[0] == ti]
                m = len(ss_)
                for k_, sidx in enumerate(ss_):
                    p3_mix(sidx, ss_[k_ - 2] if k_ >= 2 else None)
                if ti + 1 < len(tiles):
                    p3_load(ti + 1, late=True)
                    p3_xT(ti + 1, late=True)
                for k_ in range(max(0, m - 2), m):
                    p3_hT(ss_[k_])

            p3_load(0)
            p3_xT(0)
            for ti in range(len(tiles)):
                if ti + 1 < len(tiles):
                    p3_load(ti + 1, late=False)
                p3_gates(ti)
                if ti + 1 < len(tiles):
                    p3_xT(ti + 1, late=False)
                p3_post(ti)
            P.barrier()

        NWDA = 17
        WDA_LO = ARENA_BYTES - NWDA * 2048
        A.at(WDA_LO)
        WdA = A.alloc([NWDA, D], BF16)
        WDA = {"issued": False}

        if "4" in PHASES:
            A.at(8 * NT * 2)
            gT = A.alloc([NJ, NT], BF16)
            OFF_GT_END = A.off
            wu = [A.alloc([8, 256], BF16) for _ in range(3)]
            cw = A.alloc([NJ, 3], F32)
            cb = A.alloc([1, NJ], F32)
            ue = [A.alloc([1, 520], F32) for _ in range(2)]
            ues = A.alloc([NS, 6], F32)
            a0 = [A.alloc([1, 512], F32) for _ in range(2)]
            a1 = [A.alloc([1, 512], F32) for _ in range(2)]
            gg = [A.alloc([1, 512], F32) for _ in range(2)]
            stT = A.alloc([NJ, 32], F32)
            ust = A.alloc([2, NJ], F32)
            usts = A.alloc([NJ, 32], F32)
            stc_in = A.alloc([1, DFF], F32)
            so_p = A.alloc([1, 128], F32)
            so_s = stc_in
            assert A.off <= WDA_LO, (A.off, WDA_LO)
            hTv = arena_t[:, 0:8 * NT].rearrange("p (a b) -> p a b", b=NT)
            P.dma("sp", "cw", DMA(cw[:, :, :], I["convw"][:, :, :]), w=["cw"])
            P.dma("sp", "cb", DMA(cb[:, 0, :], I["convb"][:, :]), w=["cb"])
            P.dma("sp", "stc_in", DMA(stc_in[0:32, 0, :], I["stc"][:, :]), w=["stc_in"])
            for q4 in range(0, NJ, 4):
                bk = 6 + (q4 // 4) % 2
                js = list(range(q4, min(q4 + 4, NJ)))
                P.op("pe", [TR(bank(bk, 32, (j - q4) * 32), stc_in[0:32, 0, j * 128:(j + 1) * 128], ident[0:32, 0:32]) for j in js],
                     r=["stc_in", "ident"], w=pb(bk))
                P.op("act", CP(stT[:, q4:q4 + len(js), :], bank(bk, 32 * len(js)).rearrange("p (a b) -> p a b", b=32)), r=pb(bk), w=["stT"])
            wupv = I["wup"].rearrange("(kc p) n -> p kc n", p=128)

            def p4_w(j):
                w_ = wu[j % 3]
                P.dma("pool", "wu%da" % (j % 3), DMA(w_[:, :, 0:128], wupv[:, :, j * 128:(j + 1) * 128]), w=["wu%da" % (j % 3)])
                P.dma("pool", "wu%db" % (j % 3), DMA(w_[:, :, 128:256], wupv[:, :, DFF + j * 128:DFF + (j + 1) * 128]), w=["wu%db" % (j % 3)])

            p4_w(0)
            p4_w(1)
            cnt = 0
            wdv = I["wdn"].rearrange("(j p) n -> p j n", p=128)
            for j in range(NJ):
                if j + 2 < NJ:
                    p4_w(j + 2)
                if "5" in PHASES and j % 2 == 0 and j < NWDA:
                    nq = min(2, NWDA - j)
                    for hf in range(2):
                        P.dma("pool", "WdA%d_%d" % (j, hf), DMA(WdA[:, j:j + nq, hf * 512:(hf + 1) * 512], wdv[:, j:j + nq, hf * 512:(hf + 1) * 512]),
                              w=["WdA%d_%d" % (j, hf)])
                    WDA["issued"] = True
                w_ = wu[j % 3]
                for ti, (t0, TT_) in enumerate(tiles):
                    par = cnt % 4
                    cnt += 1
                    bu, bv = 2 * par, 2 * par + 1
                    P.op("pe", [MM(bank(bu, TT_), w_[:, kc, 0:128], hTv[:, kc, t0:t0 + TT_], start=(kc == 0), stop=(kc == 7)) for kc in range(8)],
                         r=["wu%da" % (j % 3)], w=pb(bu))
                    P.op("pe", [MM(bank(bv, TT_), w_[:, kc, 128:256], hTv[:, kc, t0:t0 + TT_], start=(kc == 0), stop=(kc == 7)) for kc in range(8)],
                         r=["wu%db" % (j % 3)], w=pb(bv))
                    A0, A1, G_ = a0[par % 2], a1[par % 2], gg[par % 2]
                    an, a1n, gn = "a0_%d" % (par % 2), "a1_%d" % (par % 2), "gg%d" % (par % 2)
                    if t0 < SEQ:
                        U = ue[ti % 2]
                        un = "ue%d" % (ti % 2)
                        if ti == 0:
                            P.op("pool", lambda e, U=U: e.memset(U[:, 0, 0:2], 0.0), w=[un + "h"])
                        else:
                            P.op("pool", TC(U[:, 0, 0:2], ue[(ti - 1) % 2][:, 0, 512:514]), r=["ue%d" % ((ti - 1) % 2)], w=[un + "h"])
                        P.op("act", CP(U[:, 0, 2:514], bank(bu)), r=pb(bu), w=[un])
                        P.op("act", ACTF(A0[:, 0, :], bank(bu), AF.Identity, scale=cw[:, j, 2:3], bias=cb[:, 0, j:j + 1]), r=pb(bu) + ["cw", "cb"], w=[an])
                        P.op("dve", STT(A1[:, 0, :], U[:, 0, 1:513], cw[:, j, 1:2], A0[:, 0, :], MUL, ADD), r=[un, un + "h", an, "cw"], w=[a1n])
                        P.op("dve", STT(A0[:, 0, :], U[:, 0, 0:512], cw[:, j, 0:1], A1[:, 0, :], MUL, ADD), r=[un, un + "h", a1n, "cw"], w=[an])
                        P.op("act", ACTF(G_[:, 0, :], A0[:, 0, :], AF.Gelu), r=[an], w=[gn])
                        P.op("dve", TT(gT[:, j, t0:t0 + 512], bank(bv), G_[:, 0, :], MUL), r=pb(bv) + [gn])
                        if ti == 3:
                            P.op("pool", TC(ust[:, :, j], U[:, 0, 512:514]), r=[un], w=["ust"], ss=True)
                    else:
                        P.op("pool", TC(ues[:, :, 0:2], stT[:, j, :].rearrange("p (n r) -> p n r", r=2)), r=["stT"], w=["uesh"])
                        P.op("act", CP(ues[:, :, 2:6], bank(bu, NTS).rearrange("p (n t) -> p n t", t=4)), r=pb(bu), w=["ues"])
                        P.op("act", ACTF(A0[:, 0, 0:NTS], bank(bu, NTS), AF.Identity, scale=cw[:, j, 2:3], bias=cb[:, 0, j:j + 1]), r=pb(bu) + ["cw", "cb"], w=[an])
                        v4 = lambda ap: ap.rearrange("p (n t) -> p n t", t=4)
                        P.op("dve", STT(v4(A1[:, 0, 0:NTS]), ues[:, :, 1:5], cw[:, j, 1:2], v4(A0[:, 0, 0:NTS]), MUL, ADD), r=["ues", "uesh", an, "cw"], w=[a1n])
                        P.op("dve", STT(v4(A0[:, 0, 0:NTS]), ues[:, :, 0:4], cw[:, j, 0:1], v4(A1[:, 0, 0:NTS]), MUL, ADD), r=["ues", "uesh", a1n, "cw"], w=[an])
                        P.op("act", ACTF(G_[:, 0, 0:NTS], A0[:, 0, 0:NTS], AF.Gelu), r=[an], w=[gn])
                        P.op("dve", TT(gT[:, j, SEQ:SEQ + NTS], bank(bv, NTS), G_[:, 0, 0:NTS], MUL), r=pb(bv) + [gn])
                        P.op("pool", TC(usts[:, j, :].rearrange("p (n r) -> p n r", r=2), ues[:, :, 4:6]), r=["ues"], w=["usts"], ss=True)
            P.op("pe", TR(bank(4, 128, 0, 0, 2 * NJ), ust[:, :, :].rearrange("p r j -> p (r j)"), ident[:, :]), r=["ident", "ust"], w=pb(4))
            P.op("act", CP(so_p[0:2 * NJ, 0, :], bank(4, 128, 0, 0, 2 * NJ)), r=pb(4), w=["so_p"])
            for r_ in range(2):
                P.dma("sp", "so_p%d" % r_, DMA(O["sc_p"][r_:r_ + 1, :].rearrange("o (j f) -> (o j) f", f=128), so_p[r_ * NJ:(r_ + 1) * NJ, 0, :]), r=["so_p"])
            for q4 in range(0, NJ, 4):
                bk = 6 + (q4 // 4) % 2
                js = list(range(q4, min(q4 + 4, NJ)))
                P.op("pe", [TR(bank(bk, 128, (j - q4) * 128, 0, 32), usts[:, j, :], ident[:, :]) for j in js], r=["usts", "ident"], w=pb(bk))
                P.op("act", CP(so_s[0:32, 0, q4 * 128:(q4 + len(js)) * 128], bank(bk, 128 * len(js), 0, 0, 32)), r=pb(bk), w=["stc_in"])
            P.dma("sp", "so_s", DMA(O["sc_s"][:, :], so_s[0:32, 0, :]), r=["stc_in"])
            P.barrier()

        if "5" in PHASES:
            A.at(8 * NT * 2)
            gT = A.alloc([NJ, NT], BF16)
            WdB = A.alloc([NJ - NWDA, D], BF16)
            assert A.off <= WDA_LO
            Wd_of = lambda j: (WdA[:, j, :] if j < NWDA else WdB[:, j - NWDA, :])
            A.at(0)
            hb5 = [A.alloc([1, D], F32) for _ in range(3)]
            yb = [A.alloc([1, D], F32) for _ in range(2)]
            lnb2 = A.alloc([2, D], F32)
            z5 = A.alloc([1, D], F32)
            st5 = A.alloc([1, 16], F32)
            assert A.off <= 8 * NT * 2
            P.dma("sp", "lnb", DMA(lnb2[:, 0, :], I["ln"][2:3, :].partition_broadcast(128)), w=["lnb"])
            P.dma("sp", "lnb", DMA(lnb2[:, 1, :], I["ln"][3:4, :].partition_broadcast(128)), w=["lnb"])
            wdv = I["wdn"].rearrange("(j p) n -> p j n", p=128)
            wd_names = {0: [], 1: []}
            for hf in range(2):
                P.dma("pool", "WdB_%d" % hf, DMA(WdB[:, :, hf * 512:(hf + 1) * 512], wdv[:, NWDA:NJ, hf * 512:(hf + 1) * 512]), w=["WdB_%d" % hf])
                wd_names[hf].append("WdB_%d" % hf)
            for q in range(0, NWDA, 2):
                nq = min(2, NWDA - q)
                for hf in range(2):
                    if not WDA["issued"]:
                        P.dma("pool", "WdA%d_%d" % (q, hf), DMA(WdA[:, q:q + nq, hf * 512:(hf + 1) * 512], wdv[:, q:q + nq, hf * 512:(hf + 1) * 512]),
                              w=["WdA%d_%d" % (q, hf)])
                    wd_names[hf].append("WdA%d_%d" % (q, hf))
            lnst["5"], z_name["5"] = st5, "z5"
            blks = [(i * 128, 128) for i in range(NB)] + [(SEQ, NTS)]

            def p5_load(i):
                s0, n = blks[i]
                P.dma("sp" if i % 2 == 0 else "act", "hb%d" % (i % 3), DMA(hb5[i % 3][0:n, 0, :], h_scr[s0:s0 + n, :]), w=["hb%d" % (i % 3)])

            p5_load(0)
            p5_load(1)
            for i, (s0, n) in enumerate(blks):
                if i + 2 < len(blks):
                    p5_load(i + 2)
                b0 = 2 * (i % 4)
                for hf in range(2):
                    P.op("pe", [MM(bank(b0 + hf, 512, 0, 0, n), gT[:, j, s0:s0 + n], Wd_of(j)[:, hf * 512:(hf + 1) * 512], start=(j == 0), stop=(j == NJ - 1))
                                for j in range(NJ)], r=wd_names[hf], w=pb(b0 + hf))
                for hf in range(2):
                    P.op("dve", STT(z5[0:n, 0, hf * 512:(hf + 1) * 512], hb5[i % 3][0:n, 0, hf * 512:(hf + 1) * 512], ALPHA, bank(b0 + hf, 512, 0, 0, n), MUL, ADD),
                         r=["hb%d" % (i % 3)] + pb(b0 + hf), w=["z5"])
                out_name["5"] = "yb%d" % (i % 2)
                emit_ln("5", z5, n, yb[i % 2], lnb2, 0)
                dst = O["y_p"][s0:s0 + n, :] if s0 < SEQ else O["y_s"][:, :]
                P.dma("sp", "yb%d" % (i % 2), DMA(dst, yb[i % 2][0:n, 0, :]), r=["yb%d" % (i % 2)])

        P.barrier()
        with nc.Block() as block:
            P.replay(block)
    return nc


_CACHE = {}


def kernel(x_prompt, x_sample, cache_a, cache_b1, cache_b2, cache_b3, state_conv,
           w_in, sink_a, w_br_a, w_br_b, w_o, ln1_g, ln1_b, w_up, conv_w, conv_b, w_down, ln2_g, ln2_b):
    f = lambda a: np.ascontiguousarray(np.asarray(a, dtype=np.float32))
    cst = _consts()
    ln = np.stack([f(ln1_g)[0], f(ln1_b)[0], f(ln2_g)[0], f(ln2_b)[0]], axis=0)
    wts = _prep_weights(f(w_in)[0], f(w_br_a)[0], f(conv_w)[0], f(conv_b)[0], ln)
    shared = dict(wts)
    shared.update(cst)
    shared.update(wbrb=f(w_br_b)[0], wo=f(w_o)[0], wup=f(w_up)[0], wdn=f(w_down)[0], sink=f(sink_a).reshape(1, 8))
    xp, xs = f(x_prompt), f(x_sample)
    ca, cb1, cb2, cb3, stc = f(cache_a)[0], f(cache_b1)[0], f(cache_b2)[0], f(cache_b3)[0], f(state_conv)[0]
    in_maps = []
    for c in range(NCORES):
        n0, n1 = c * NS, (c + 1) * NS
        m = dict(shared)
        m.update(xp=xp[c], xs=xs[n0:n1].reshape(NTS, D),
                 ca=ca[n0:n1].reshape(NS, 128, 256), cb1=cb1[n0:n1].reshape(NS, 128, 512),
                 cb2=cb2[n0:n1].reshape(NS, 512, 512), cb3=cb3[n0:n1].reshape(NS, 2048, 512),
                 stc=stc[n0:n1].reshape(NS * 2, DFF))
        in_maps.append({k: np.ascontiguousarray(m[k]) for k, _ in IN_SPECS})
    if "nc" not in _CACHE:
        _CACHE["nc"] = build()
    res = run_bass_kernel_spmd(_CACHE["nc"], in_maps, core_ids=list(range(NCORES)))
    R = res.results
    cat = lambda k: np.stack([R[c][k] for c in range(NCORES)], axis=0)
    y_p = cat("y_p")
    y_s = cat("y_s").reshape(128, TS, D)
    outs = [y_p, y_s,
            cat("ca_p").reshape(1, 8, 128, 2, 2, 64), cat("ca_s").reshape(1, 128, TS, 2, 2, 64),
            cat("cb1_p").reshape(1, 8, 128, 2, 4, 64), cat("cb1_s").reshape(1, 128, TS, 2, 4, 64),
            cat("cb2_p").reshape(1, 8, 512, 2, 4, 64), cat("cb2_s").reshape(1, 128, TS, 2, 4, 64),
            cat("cb3_p").reshape(1, 8, 2048, 2, 4, 64), cat("cb3_s").reshape(1, 128, TS, 2, 4, 64),
            cat("sc_p").reshape(1, 8, 2, DFF), cat("sc_s").reshape(1, 128, 2, DFF)]
    return tuple(np.ascontiguousarray(o.astype(np.float32)) for o in outs)
```

```python
import os
from contextlib import ExitStack
import numpy as np
import concourse.bass as bass
import concourse.mybir as mybir
from concourse.bass_utils import run_bass_kernel_spmd

F32 = mybir.dt.float32
BF16 = mybir.dt.bfloat16
AF = mybir.ActivationFunctionType
ALU = mybir.AluOpType

NCORES = 8
D = 1024
SEQ = 2048
NB = SEQ // 128
NS = 16
TS = 4
NTS = NS * TS
NT = SEQ + NTS
PAST = 16384
DFF = 2816
NJ = DFF // 128
ALPHA = float(2.0 ** 0.25)
SCALE = 0.125
EPS = 1e-5
ENGS = ("pe", "act", "dve", "pool", "sp")


class Prog:
    def __init__(self, nc, stack):
        self.nc = nc
        self.streams = {e: [] for e in ENGS}
        self.cnt = {e: 0 for e in ENGS}
        self.seen = {e: {} for e in ENGS}
        self.last_w = {}
        self.readers = {}
        self.dma_sems = {}
        self._stack = stack
        self.eng_sem = {e: stack.enter_context(nc.semaphore("c_" + e)) for e in ENGS}

    def _dma_sem(self, key):
        if key not in self.dma_sems:
            h = self._stack.enter_context(self.nc.semaphore("d_" + key))
            self.dma_sems[key] = [h, 0]
        return self.dma_sems[key]

    @staticmethod
    def _excl(reads, writes):
        return list(writes) + [b for b in reads if b.startswith("ps")]

    def _deps(self, reads, writes):
        toks = []
        writes = self._excl(reads, writes)
        for b in reads:
            t = self.last_w.get(b)
            if t is not None:
                toks.append(t)
        for b in writes:
            t = self.last_w.get(b)
            if t is not None:
                toks.append(t)
            toks.extend(self.readers.get(b, ()))
        return toks

    def _emit_waits(self, eng, toks, ss=False):
        need = {}
        for kind, key, val in toks:
            if kind == "c" and key == eng and (ss or eng in ("pe", "sp")):
                continue
            k = (kind, key)
            if val > need.get(k, 0):
                need[k] = val
        for (kind, key), val in need.items():
            if self.seen[eng].get((kind, key), 0) >= val:
                continue
            self.seen[eng][(kind, key)] = val
            sem = self.eng_sem[key] if kind == "c" else self.dma_sems[key][0]
            self.streams[eng].append(("wait", sem, val))

    def _commit(self, tok, reads, writes):
        writes = self._excl(reads, writes)
        for b in reads:
            self.readers.setdefault(b, []).append(tok)
        for b in writes:
            self.last_w[b] = tok
            self.readers[b] = []

    def op(self, eng, fns, r=(), w=(), ss=False):
        if callable(fns):
            fns = [fns]
        self._emit_waits(eng, self._deps(r, w), ss)
        self.cnt[eng] += 1
        tok = ("c", eng, self.cnt[eng])
        for f in fns[:-1]:
            self.streams[eng].append(("op", f, None))
        self.streams[eng].append(("op", fns[-1], self.eng_sem[eng]))
        self._commit(tok, r, w)
        return tok

    def dma(self, queue, key, fn, r=(), w=()):
        self._emit_waits(queue, self._deps(r, w))
        s = self._dma_sem(key)
        s[1] += 16
        tok = ("d", key, s[1])
        self.streams[queue].append(("dma", fn, s[0]))
        self._commit(tok, r, w)
        return tok

    def barrier(self):
        toks = [("c", f, self.cnt[f]) for f in ENGS if self.cnt[f] > 0]
        toks += [("d", k, v[1]) for k, v in self.dma_sems.items() if v[1] > 0]
        for e in ENGS:
            self._emit_waits(e, toks)
        self.last_w = {}
        self.readers = {}

    def replay(self, block):
        def run(engine, stream):
            for it in stream:
                if it[0] == "wait":
                    engine.wait_ge(it[1], it[2])
                elif it[0] == "op":
                    ins = it[1](engine)
                    if it[2] is not None:
                        ins.then_inc(it[2], 1)
                else:
                    it[1](engine).then_inc(it[2], 16)

        @block.tensor
        def _(e):
            run(e, self.streams["pe"])

        @block.scalar
        def _(e):
            run(e, self.streams["act"])

        @block.vector
        def _(e):
            run(e, self.streams["dve"])

        @block.gpsimd
        def _(e):
            run(e, self.streams["pool"])

        @block.sync
        def _(e):
            run(e, self.streams["sp"])


def MM(out, lhsT, rhs, start=True, stop=True, skip=False):
    return lambda e: e.matmul(out, lhsT, rhs, start=start, stop=stop, skip_group_check=skip)


def TR(out, in_, idn):
    return lambda e: e.transpose(out, in_, idn)


def ACTF(out, in_, func, scale=1.0, bias=0.0):
    return lambda e: e.activation(out=out, in_=in_, func=func, bias=bias, scale=scale)


def CP(out, in_):
    return lambda e: e.copy(out=out, in_=in_)


def TC(out, in_):
    return lambda e: e.tensor_copy(out=out, in_=in_)


def TT(out, in0, in1, op):
    return lambda e: e.tensor_tensor(out=out, in0=in0, in1=in1, op=op)


def STT(out, in0, scalar, in1, op0=None, op1=None):
    return lambda e: e.scalar_tensor_tensor(out=out, in0=in0, scalar=scalar, in1=in1, op0=op0, op1=op1)


def DMA(out, in_):
    return lambda e: e.dma_start(out=out, in_=in_)


class Arena:
    def __init__(self, t, nbytes):
        self.t = t
        self.nbytes = nbytes
        self.off = 0

    def at(self, off):
        self.off = off

    def alloc(self, shape, dt):
        n = int(np.prod(shape))
        sz = 4 if dt == F32 else 2
        off = (self.off + 63) // 64 * 64
        nb = n * sz
        assert off + nb <= self.nbytes, ("arena overflow", off, nb, self.nbytes)
        self.off = off + nb
        v = self.t[:, off // 2:(off + nb) // 2]
        if dt == F32:
            v = v.bitcast(F32)
        if len(shape) == 2:
            return v.rearrange("p (a b) -> p a b", a=shape[0], b=shape[1])
        if len(shape) == 3:
            return v.rearrange("p (a b c) -> p a b c", a=shape[0], b=shape[1], c=shape[2])
        return v


def _rope_tables():
    half = 32
    inv = (np.float32(10000.0) ** (-(np.arange(half, dtype=np.float32) / np.float32(half)))).astype(np.float32)
    p = np.arange(128)
    pos = np.zeros((128, 49), np.int64)
    for b in range(16):
        pos[:, b] = b * 128 + p
    for r in range(4):
        for mb in range(4):
            pos[:, 16 + r * 4 + mb] = (mb * 128 + p) * 4 + r
    for r in range(16):
        pos[:, 32 + r] = p * 16 + r
    pos[:, 48] = PAST + (p % 4)
    ang = pos.astype(np.float32)[:, :, None] * inv[None, None, :]
    tab = np.stack([np.cos(ang), np.sin(ang)], axis=2).astype(np.float32)
    return np.ascontiguousarray(tab)


def _consts():
    s = np.arange(128)[:, None]
    q = np.arange(128)[None, :]
    masks = np.stack([(s <= q), (s > q)], axis=1).astype(np.float32)
    t = np.arange(4)[None, :]
    sm = np.stack([(s > t), (s > 0) & (t >= 0), (s <= t), (s == t)], axis=1).astype(np.float32)
    return dict(ident=np.eye(128, dtype=np.float32), masks=np.ascontiguousarray(masks),
                smask=np.ascontiguousarray(sm), rope=_rope_tables())


def _prep_weights(w_in, w_br_a, conv_w, conv_b, ln):
    qa, ka, va = w_in[:, 0:512], w_in[:, 512:640], w_in[:, 640:768]
    qb, kb, vb = w_in[:, 768:1536], w_in[:, 1536:2304], w_in[:, 2304:3072]
    hperm = np.concatenate([np.arange(64) + 64 * h for c in range(4) for h in (c, 4 + c)])
    wN = np.concatenate([qa[:, hperm], ka, qb[:, 0:256], kb[:, 0:256], va, vb[:, 0:256]], axis=1)
    wD = np.concatenate([qb[:, 256:512], kb[:, 256:512], vb[:, 256:512],
                         qb[:, 512:768], kb[:, 512:768], vb[:, 512:768]], axis=1)
    wg = w_in[:, 3072:5120]
    return dict(wN=np.ascontiguousarray(wN), wD=np.ascontiguousarray(wD), wg=np.ascontiguousarray(wg),
                wbra=np.ascontiguousarray(w_br_a[hperm, :]),
                convw=np.ascontiguousarray(conv_w.reshape(3, NJ, 128).transpose(2, 1, 0)),
                convb=np.ascontiguousarray(conv_b.reshape(NJ, 128).T), ln=np.ascontiguousarray(ln),
                lncol=np.ascontiguousarray(ln.reshape(4, 8, 128).transpose(2, 0, 1)))


IN_SPECS = [
    ("xp", [SEQ, D]), ("xs", [NTS, D]),
    ("ca", [NS, 128, 256]), ("cb1", [NS, 128, 512]), ("cb2", [NS, 512, 512]), ("cb3", [NS, 2048, 512]),
    ("stc", [NS * 2, DFF]),
    ("wN", [D, 1536]), ("wD", [D, 1536]), ("wg", [D, 2048]), ("wbra", [512, D]), ("wbrb", [256, D]),
    ("wo", [D, D]), ("wup", [D, 2 * DFF]), ("wdn", [DFF, D]),
    ("convw", [128, NJ, 3]), ("convb", [128, NJ]), ("ln", [4, D]), ("lncol", [128, 4, 8]), ("sink", [1, 8]),
    ("ident", [128, 128]), ("masks", [128, 2, 128]), ("smask", [128, 4, 4]), ("rope", [128, 49, 2, 32]),
]
OUT_SPECS = [
    ("y_p", [SEQ, D]), ("y_s", [NTS, D]),
    ("ca_p", [128, 256]), ("ca_s", [NTS, 256]), ("cb1_p", [128, 512]), ("cb1_s", [NTS, 512]),
    ("cb2_p", [512, 512]), ("cb2_s", [NTS, 512]), ("cb3_p", [2048, 512]), ("cb3_s", [NTS, 512]),
    ("sc_p", [2, DFF]), ("sc_s", [NS * 2, DFF]),
]

PHASES = os.environ.get("MK_PHASES", "12345")
SUB = os.environ.get("MK_SUB", "abs")


def build():
    nc = bass.Bass("TRN2", target_bir_lowering=False)
    I = {n: nc.dram_tensor(n, s, F32, kind="ExternalInput").ap() for n, s in IN_SPECS}
    O = {n: nc.dram_tensor(n, s, F32, kind="ExternalOutput").ap() for n, s in OUT_SPECS}
    h_scr = nc.dram_tensor("h_scr", [NT, D], F32).ap()

    with ExitStack() as st:
        P = Prog(nc, st)
        ARENA_BYTES = 204 * 1024
        arena_t = st.enter_context(nc.sbuf_tensor("arena", [128, ARENA_BYTES // 2], BF16))
        A = Arena(arena_t, ARENA_BYTES)
        sb = lambda name, shape, dt: st.enter_context(nc.sbuf_tensor(name, shape, dt))
        ident = sb("ident_sb", [128, 128], F32)
        idb = sb("idb", [128, 128], BF16)
        mask_f = sb("mask_f", [128, 2, 128], F32)
        mask_b = sb("mask_b", [128, 2, 128], BF16)
        smask_f = sb("smask_f", [128, 4, 4], F32)
        smask_b = sb("smask_b", [128, 4, 4], BF16)
        ones_b = sb("ones_b", [128, 128], BF16)
        sink_f = sb("sink_f", [1, 8], F32)
        PS = st.enter_context(nc.psum_tensor("psum", [128, 4096], F32))

        def bank(b, n=512, o=0, p0=0, p1=128):
            return PS[p0:p1, b * 512 + o:b * 512 + o + n]

        def bank_bf(b):
            return PS[:, b * 512:(b + 1) * 512].bitcast(BF16)

        pb = lambda *bs: ["ps%d" % b for b in bs]

        P.dma("sp", "ident", lambda e: e.dma_start(out=ident[:], in_=I["ident"][:, :]), w=["ident"])
        P.dma("sp", "mask_f", lambda e: e.dma_start(out=mask_f[:], in_=I["masks"][:, :, :]), w=["mask_f"])
        P.dma("sp", "smask_f", lambda e: e.dma_start(out=smask_f[:], in_=I["smask"][:, :, :]), w=["smask_f"])
        P.dma("sp", "sink_f", lambda e: e.dma_start(out=sink_f[:], in_=I["sink"][:, :]), w=["sink_f"])
        P.op("pool", lambda e: e.tensor_copy(out=idb[:], in_=ident[:]), r=["ident"], w=["idb"])
        P.op("pool", lambda e: e.tensor_copy(out=mask_b[:], in_=mask_f[:]), r=["mask_f"], w=["mask_b"])
        P.op("pool", lambda e: e.tensor_copy(out=smask_b[:], in_=smask_f[:]), r=["smask_f"], w=["smask_b"])
        P.op("pool", lambda e: e.memset(ones_b[:], 1.0), w=["ones_b"])
        P.op("act", lambda e: e.activation(out=sink_f[:], in_=sink_f[:], func=AF.Exp), r=["sink_f"], w=["sink_f"])

        A.at(0)
        QKN = A.alloc([9, SEQ], BF16)
        QKD4 = A.alloc([4, SEQ], BF16)
        QKD16 = A.alloc([4, SEQ], BF16)
        VN = A.alloc([NB, 384], BF16)
        VD4 = A.alloc([NB, 256], BF16)
        VD16 = A.alloc([NB, 256], BF16)
        OFF_QKV_END = A.off
        QS = A.alloc([17, NTS], BF16)
        OFF_P = A.off
        oaT = A.alloc([4, NT], BF16)
        obT = A.alloc([2, NT], BF16)
        accN = A.alloc([2, NT], F32)
        accD = A.alloc([2, NT], F32)
        OFF_P2T = A.off

        if "1" in PHASES:
            A.at(OFF_P)
            xT = A.alloc([8, NT], BF16)
            Wb = A.alloc([8, 1536], BF16)
            rope = A.alloc([49, 64], F32)
            xst = [A.alloc([1, D], F32) for _ in range(2)]
            xbf = [A.alloc([1, D], BF16) for _ in range(2)]
            R = [A.alloc([1, 1536], F32) for _ in range(2)]
            Bt = A.alloc([1, 1152], F32)
            Rb = [A.alloc([1, 1152], BF16) for _ in range(3)]

            P.dma("sp", "rope", lambda e: e.dma_start(out=rope[:, :, :], in_=I["rope"].rearrange("p b c d -> p b (c d)")), w=["rope"])
            for c3 in range(3):
                P.dma("pool", "Wb%d" % c3, lambda e, c3=c3: e.dma_start(
                    out=Wb[:, :, c3 * 512:(c3 + 1) * 512],
                    in_=I["wN"].rearrange("(kc p) n -> p kc n", p=128)[:, :, c3 * 512:(c3 + 1) * 512]), w=["Wb%d" % c3])
            for tb in range(NB + 1):
                ntok = 128 if tb < NB else NTS
                src = I["xp"][tb * 128:(tb + 1) * 128, :] if tb < NB else I["xs"][:, :]
                xs_ = xst[tb % 2]
                P.dma("sp" if tb % 2 == 0 else "act", "xst%d" % (tb % 2),
                      lambda e, xs_=xs_, src=src, ntok=ntok: e.dma_start(out=xs_[0:ntok, 0, :], in_=src), w=["xst%d" % (tb % 2)])
                xb_ = xbf[tb % 2]
                P.op("act", CP(xb_[0:ntok, 0, :], xs_[0:ntok, 0, :]), r=["xst%d" % (tb % 2)], w=["xbf%d" % (tb % 2)])
                bk = 6 + tb % 2
                P.op("pe", [TR(bank_bf(bk)[:, kc * 128:kc * 128 + ntok], xb_[0:ntok, 0, kc * 128:(kc + 1) * 128], idb[0:ntok, 0:ntok])
                            for kc in range(8)], r=["xbf%d" % (tb % 2), "idb"], w=pb(bk))
                src_v = bank_bf(bk).rearrange("p (a b) -> p a b", b=128)[:, :, 0:ntok]
                dst_v = xT[:, 0:8, tb * 128:tb * 128 + ntok]
                P.op("dve", TC(dst_v, src_v), r=pb(bk), w=["xT%da" % tb, "xT%db" % tb])

            def xcols(kind, i):
                if kind == "N":
                    return lambda kc: xT[:, kc, i * 128:(i + 1) * 128]
                if kind == "S":
                    return lambda kc: xT[:, kc, SEQ:SEQ + NTS]
                if kind == "D4":
                    r_, mb = i // 4, i % 4
                    return lambda kc: xT[:, kc, mb * 512:(mb + 1) * 512].rearrange("p (m r) -> p r m", r=4)[:, r_, :]
                return lambda kc: xT[:, kc, 0:SEQ].rearrange("p (m r) -> p r m", r=16)[:, i, :]

            xn = lambda tbs: [("xT%d" % t) + h for t in tbs for h in "ab"]
            jobs = []
            for b in range(NB):
                outs = []
                if b == NB - 1:
                    outs = [(O["ca_p"][:, 0:128], 512, 128), (O["ca_p"][:, 128:256], 1152, 128),
                            (O["cb1_p"][:, 0:256], 896, 256), (O["cb1_p"][:, 256:512], 1280, 256)]
                jobs.append(dict(x=xcols("N", b), ntok=128, c0=0, nc_=1536, nqk=1152, ti=b,
                                 qk=[(QKN[:, 0:9, b * 128:(b + 1) * 128], 0, 9)],
                                 v=[(VN[:, b, :], 1152, 384)], outs=outs, wsel="N", xr=xn([b])))
            jobs.append(dict(x=xcols("S", 0), ntok=NTS, c0=0, nc_=1536, nqk=1152, ti=48,
                             qk=[(QS[:, 0:9, :], 0, 9)], v=[],
                             outs=[(O["ca_s"][:, 0:128], 512, 128), (O["ca_s"][:, 128:256], 1152, 128),
                                   (O["cb1_s"][:, 0:256], 896, 256), (O["cb1_s"][:, 256:512], 1280, 256)], wsel="N", xr=xn([NB])))
            for i in range(16):
                r_, mb = i // 4, i % 4
                outs = []
                if mb == 3:
                    dv = O["cb2_p"].rearrange("(p r) c -> r p c", r=4)
                    outs = [(dv[r_, :, 0:256], 256, 256), (dv[r_, :, 256:512], 512, 256)]
                jobs.append(dict(x=xcols("D4", i), ntok=128, c0=0, nc_=768, nqk=512, ti=16 + i,
                                 qk=[(QKD4[:, 0:4, i * 128:(i + 1) * 128], 0, 4)],
                                 v=[(VD4[:, i, :], 512, 256)], outs=outs, wsel="D", xr=xn(range(4 * mb, 4 * mb + 4))))
            for i in range(16):
                dv = O["cb3_p"].rearrange("(p r) c -> r p c", r=16)
                jobs.append(dict(x=xcols("D16", i), ntok=128, c0=768, nc_=768, nqk=512, ti=32 + i,
                                 qk=[(QKD16[:, 0:4, i * 128:(i + 1) * 128], 0, 4)],
                                 v=[(VD16[:, i, :], 512, 256)],
                                 outs=[(dv[i, :, 0:256], 256, 256), (dv[i, :, 256:512], 512, 256)], wsel="D", xr=xn(range(NB))))
            jobs.append(dict(x=xcols("S", 0), ntok=NTS, c0=0, nc_=1536, nqk=None, ti=48,
                             ropes=[(0, 512), (768, 512)],
                             qk=[(QS[:, 9:13, :], 0, 4), (QS[:, 13:17, :], 768, 4)], v=[],
                             outs=[(O["cb2_s"][:, 0:256], 256, 256), (O["cb2_s"][:, 256:512], 512, 256),
                                   (O["cb3_s"][:, 0:256], 1024, 256), (O["cb3_s"][:, 256:512], 1280, 256)], wsel="D", xr=xn([NB])))

            def emit_mm(j, jb):
                if jb["wsel"] == "D" and jobs[j - 1]["wsel"] == "N":
                    for c3 in range(3):
                        P.dma("pool", "Wb%d" % c3, lambda e, c3=c3: e.dma_start(
                            out=Wb[:, :, c3 * 512:(c3 + 1) * 512],
                            in_=I["wD"].rearrange("(kc p) n -> p kc n", p=128)[:, :, c3 * 512:(c3 + 1) * 512]), w=["Wb%d" % c3])
                s = j % 2
                ntok, c0, ncol = jb["ntok"], jb["c0"], jb["nc_"]
                for cc in range(0, ncol, 512):
                    n = min(512, ncol - cc)
                    bk = 3 * s + cc // 512
                    wnames = sorted(set(["Wb%d" % ((c0 + cc) // 512), "Wb%d" % ((c0 + cc + n - 1) // 512)]))
                    P.op("pe", [(lambda e, kc=kc, bk=bk, n=n, cc=cc: e.matmul(
                        bank(bk, n, 0, 0, ntok), jb["x"](kc), Wb[:, kc, c0 + cc:c0 + cc + n], start=(kc == 0), stop=(kc == 7)))
                        for kc in range(8)], r=jb["xr"] + wnames, w=pb(bk))

            def emit_rope(j, jb):
                s = j % 2
                ntok, ncol = jb["ntok"], jb["nc_"]
                banks = pb(*range(3 * s, 3 * s + (ncol + 511) // 512))
                psj = PS[0:ntok, 3 * s * 512:3 * s * 512 + ncol]
                Rj, Rbj = R[s], Rb[j % 3]
                cosb = rope[0:ntok, jb["ti"], 0:32]
                sinb = rope[0:ntok, jb["ti"], 32:64]
                ropes = jb.get("ropes") or [(0, jb["nqk"])]
                for (r0, rn) in ropes:
                    nh = rn // 64
                    pv = psj[:, r0:r0 + rn].rearrange("p (h two d) -> p h two d", two=2, d=32)
                    Rv = Rj[0:ntok, 0, r0:r0 + rn].rearrange("p (h two d) -> p h two d", two=2, d=32)
                    Bv = Bt[0:ntok, 0, 0:rn].rearrange("p (h two d) -> p h two d", two=2, d=32)
                    cos4 = cosb.unsqueeze(1).unsqueeze(1).to_broadcast([ntok, nh, 2, 32])
                    sin3 = sinb.unsqueeze(1).to_broadcast([ntok, nh, 32])
                    P.op("dve", lambda e, Rv=Rv, pv=pv, cos4=cos4: e.tensor_tensor(out=Rv, in0=pv, in1=cos4, op=ALU.mult),
                         r=banks + ["rope"], w=["R%d" % s])
                    P.op("dve", lambda e, Bv=Bv, pv=pv, sin3=sin3: e.scalar_tensor_tensor(
                        out=Bv[:, :, 0, :], in0=pv[:, :, 1, :], scalar=-1.0, in1=sin3, op0=ALU.mult, op1=ALU.mult),
                        r=banks + ["rope"], w=["Bt"])
                    P.op("dve", lambda e, Bv=Bv, pv=pv, sin3=sin3: e.tensor_tensor(
                        out=Bv[:, :, 1, :], in0=pv[:, :, 0, :], in1=sin3, op=ALU.mult), r=banks + ["rope"], w=["Bt"])
                    P.op("dve", TT(Rj[0:ntok, 0, r0:r0 + rn], Rj[0:ntok, 0, r0:r0 + rn], Bt[0:ntok, 0, 0:rn], ALU.add), r=["R%d" % s, "Bt"], w=["R%d" % s])
                if jb.get("ropes"):
                    vranges = [(512, 256), (1280, 256)]
                else:
                    vranges = [(jb["nqk"], ncol - jb["nqk"])]
                for (v0, vn) in vranges:
                    P.op("act", lambda e, d=Rj[0:ntok, 0, v0:v0 + vn], s_=psj[:, v0:v0 + vn]: e.copy(out=d, in_=s_),
                         r=banks, w=["RV%d" % s])
                for (dst, v0, vn) in jb["v"]:
                    P.op("act", lambda e, d=dst, s_=psj[:, v0:v0 + vn]: e.copy(out=d, in_=s_), r=banks)
                for (r0, rn) in ropes:
                    P.op("act", CP(Rbj[0:ntok, 0, jb_rb_off(jb, r0):jb_rb_off(jb, r0) + rn], Rj[0:ntok, 0, r0:r0 + rn]), r=["R%d" % s], w=["Rb%d" % (j % 3)])
                for k, (dram, c_, n_) in enumerate(jb["outs"]):
                    P.dma("sp", "R%d_o%d" % (s, k), lambda e, dram=dram, src=Rj[0:ntok, 0, c_:c_ + n_]: e.dma_start(out=dram, in_=src),
                          r=["R%d" % s, "RV%d" % s])

            def jb_rb_off(jb, r0):
                if jb.get("ropes"):
                    return 0 if r0 == 0 else 512
                return r0

            def emit_tr(j, jb):
                s = j % 3
                ntok = jb["ntok"]
                Rbj = Rb[s]
                pos = 0
                for (dst, c_, nch) in jb["qk"]:
                    rb0 = jb_rb_off(jb, c_) if jb.get("ropes") else c_
                    slots = list(range(pos, pos + nch))
                    for bk in (6, 7):
                        sl = [q_ for q_ in slots if q_ // 8 == bk - 6]
                        if not sl:
                            continue
                        fns = []
                        for q_ in sl:
                            o_ap = bank_bf(bk)[:, (q_ % 8) * 128:(q_ % 8) * 128 + ntok]
                            i_ap = Rbj[0:ntok, 0, rb0 + (q_ - pos) * 128:rb0 + (q_ - pos + 1) * 128]
                            fns.append(lambda e, o_ap=o_ap, i_ap=i_ap, id_ap=idb[0:ntok, 0:ntok]: e.transpose(o_ap, i_ap, id_ap))
                        P.op("pe", fns, r=["Rb%d" % s, "idb"], w=pb(bk))
                        srcv = bank_bf(bk).rearrange("p (a b) -> p a b", b=128)[:, sl[0] % 8:sl[-1] % 8 + 1, 0:ntok]
                        dstv = dst[:, sl[0] - pos:sl[-1] - pos + 1, :]
                        P.op("act", lambda e, d=dstv, s_=srcv: e.copy(out=d, in_=s_), r=pb(bk))
                    pos += nch

            for j in range(len(jobs) + 2):
                if j < len(jobs):
                    emit_mm(j, jobs[j])
                if j >= 2:
                    emit_tr(j - 2, jobs[j - 2])
                if j < len(jobs):
                    emit_rope(j, jobs[j])
            P.barrier()

        MUL, ADD = ALU.mult, ALU.add

        if "2" in PHASES:
            A.at(OFF_P2T)
            PT = [[A.alloc([1, 512], BF16) for _ in range(4)] for _ in range(2)]
            rec = [A.alloc([1, 512], F32) for _ in range(2)]
            rsc = [A.alloc([1, 512], F32) for _ in range(2)]
            sink_row = A.alloc([8, 128], BF16)
            P.op("dve", TC(sink_row[0:1, :, :], sink_f[0:1, :].unsqueeze(2).to_broadcast([1, 8, 128])), r=["sink_f"], w=["sink_row"])
            mbA = A.alloc([2, 512], BF16)
            mbB = A.alloc([4, 128], BF16)
            for wh in range(2):
                P.op("dve", lambda e, wh=wh: e.tensor_scalar(
                    out=mbA[:, wh, :].rearrange("p (h q) -> p h q", q=128), in0=mask_f[:, wh, :].unsqueeze(1).to_broadcast([128, 4, 128]),
                    scalar1=-1.0, scalar2=30000.0, op0=ALU.add, op1=ALU.mult), r=["mask_f"], w=["mbA"])
                P.op("dve", lambda e, wh=wh: e.tensor_scalar(
                    out=mbB[:, 2 * wh:2 * wh + 2, :], in0=mask_f[:, wh, :].unsqueeze(1).to_broadcast([128, 2, 128]),
                    scalar1=-1.0, scalar2=30000.0, op0=ALU.add, op1=ALU.mult), r=["mask_f"], w=["mbB"])

            def a_scores(b):
                par = b % 2
                whs = [0] + ([1] if b > 0 else [])
                items = [(kvh, wh) for wh in whs for kvh in range(2)]
                P.op("pe", [MM(bank(kvh * 2 + wh), idb[:, :], mbA[:, wh, :], start=True, stop=False) for (kvh, wh) in items],
                     r=["mbA", "idb"], w=pb(*[kvh * 2 + wh for (kvh, wh) in items]))
                for (kvh, wh) in items:
                    p0, p1 = kvh * 64, kvh * 64 + 64
                    kb = b - wh
                    bk = kvh * 2 + wh
                    P.op("pe", MM(bank(bk), QKN[p0:p1, 4, kb * 128:(kb + 1) * 128], QKN[p0:p1, 0:4, b * 128:(b + 1) * 128], start=False, stop=True),
                         w=pb(bk))
                for (kvh, wh) in items:
                    bk = kvh * 2 + wh
                    nm = "PT%d_%d" % (par, kvh * 2 + wh)
                    P.op("act", ACTF(PT[par][kvh * 2 + wh][:, 0, :], bank(bk), AF.Exp, scale=SCALE), r=pb(bk), w=[nm])

            def a_values(b):
                par = b % 2
                bo, bd = (4, 5) if par == 0 else (6, 7)
                lst = [(0, b)] + ([(1, b - 1)] if b > 0 else [])
                nms = ["PT%d_%d" % (par, kvh * 2 + wh) for wh, _ in lst for kvh in range(2)]
                fo, fd = [], []
                for i, (wh, kb) in enumerate(lst):
                    for kvh in range(2):
                        p0, p1 = kvh * 64, kvh * 64 + 64
                        pt = PT[par][kvh * 2 + wh][:, 0, :]
                        fo.append(MM(bank(bo, 512, 0, p0, p1), VN[:, kb, kvh * 64:(kvh + 1) * 64], pt, start=(i == 0), stop=(i == len(lst) - 1)))
                for i, (wh, kb) in enumerate(lst):
                    for kvh in range(2):
                        p0, p1 = kvh * 64, kvh * 64 + 64
                        pt = PT[par][kvh * 2 + wh][:, 0, :]
                        fd.append(MM(bank(bd, 512, 0, p0, p1), ones_b[:, 0:64], pt, start=(i == 0), stop=False))
                for kvh in range(2):
                    p0, p1 = kvh * 64, kvh * 64 + 64
                    fd.append(MM(bank(bd, 512, 0, p0, p1), ones_b[0:1, 0:64], sink_row[0:1, kvh * 4:(kvh + 1) * 4, :], start=False, stop=True))
                P.op("pe", fo, r=nms, w=pb(bo))
                P.op("pe", fd, r=nms + ["ones_b", "sink_row"], w=pb(bd))
                rc = rec[par]
                ob_ = rsc[par]
                P.op("act", CP(ob_[:, 0, :], bank(bo)), r=pb(bo), w=["rsc%d" % par])
                P.op("dve", lambda e, rc=rc, bd=bd: e.reciprocal(out=rc[:, 0, :], in_=bank(bd)), r=pb(bd), w=["rec%d" % par])
                P.op("pool", TT(oaT[:, 0:4, b * 128:(b + 1) * 128], ob_[:, 0, :].rearrange("p (h q) -> p h q", q=128),
                                rc[:, 0, :].rearrange("p (h q) -> p h q", q=128), MUL), r=["rec%d" % par, "rsc%d" % par])

            if "a" in SUB:
                a_scores(0)
                for b in range(NB):
                    if b + 1 < NB:
                        a_scores(b + 1)
                    a_values(b)

            blocks = []
            for i in range(NB):
                blocks.append(dict(g=0, QK=QKN, qc=5, kc=7, V=VN, vo=128, i=i, prev=(i - 1 if i > 0 else None),
                                   dst=lambda T_, i=i: T_[:, :, i * 128:(i + 1) * 128], span=[i // 4]))
            for i in range(16):
                r_, mb = i // 4, i % 4
                blocks.append(dict(g=1, QK=QKD4, qc=0, kc=2, V=VD4, vo=0, i=i, prev=(i - 1 if mb > 0 else None),
                                   dst=lambda T_, r_=r_, mb=mb: T_[:, :, mb * 512:(mb + 1) * 512].rearrange("p c (m r) -> p c r m", r=4)[:, :, r_, :],
                                   span=[mb]))
            for i in range(16):
                blocks.append(dict(g=2, QK=QKD16, qc=0, kc=2, V=VD16, vo=0, i=i, prev=None,
                                   dst=lambda T_, i=i: T_[:, :, 0:SEQ].rearrange("p c (m r) -> p c r m", r=16)[:, :, i, :],
                                   span=[0, 1, 2, 3]))

            def b_scores(k, bl):
                par = k % 2
                lst = [(0, bl["i"])] + ([(1, bl["prev"])] if bl["prev"] is not None else [])
                ncol = 256 * len(lst)
                bks = [2 * par, 2 * par + 1]
                P.op("pe", [MM(bank(bk, ncol), idb[:, :], mbB[:, 0:2 * len(lst), :], start=True, stop=False) for bk in bks],
                     r=["mbB", "idb"], w=pb(*bks))
                fns = []
                for wi, (wh, kb) in enumerate(lst):
                    for c in range(2):
                        for half in range(2):
                            p0 = half * 64
                            fns.append(MM(bank(bks[half], 128, wh * 256 + c * 128), bl["QK"][p0:p0 + 64, bl["kc"] + c, kb * 128:(kb + 1) * 128],
                                          bl["QK"][p0:p0 + 64, bl["qc"] + c, bl["i"] * 128:(bl["i"] + 1) * 128],
                                          start=False, stop=(wi == len(lst) - 1 and c == 1)))
                P.op("pe", fns, w=pb(*bks))
                for half in range(2):
                    nm = "PT%d_%d" % (par, half)
                    P.op("act", ACTF(PT[par][half][:, 0, 0:ncol], bank(bks[half], ncol), AF.Exp, scale=SCALE), r=pb(bks[half]), w=[nm])

            def b_values(k, bl):
                par = k % 2
                bo, bd = (4, 5) if par == 0 else (6, 7)
                lst = [(0, bl["i"])] + ([(1, bl["prev"])] if bl["prev"] is not None else [])
                nms = ["PT%d_%d" % (par, hf_) for hf_ in range(2)]
                fo, fd = [], []
                for c in range(2):
                    for i_, (wh, kb) in enumerate(lst):
                        for hf_ in range(2):
                            h, p0 = 2 * c + hf_, hf_ * 64
                            pt = PT[par][hf_][:, 0, wh * 256 + c * 128:wh * 256 + (c + 1) * 128]
                            fo.append(MM(bank(bo, 128, c * 128, p0, p0 + 64), bl["V"][:, kb, bl["vo"] + h * 64:bl["vo"] + (h + 1) * 64], pt,
                                         start=(i_ == 0), stop=(i_ == len(lst) - 1)))
                for c in range(2):
                    for i_, (wh, kb) in enumerate(lst):
                        for hf_ in range(2):
                            p0 = hf_ * 64
                            pt = PT[par][hf_][:, 0, wh * 256 + c * 128:wh * 256 + (c + 1) * 128]
                            fd.append(MM(bank(bd, 128, c * 128, p0, p0 + 64), ones_b[:, 0:64], pt, start=(i_ == 0), stop=(i_ == len(lst) - 1)))
                P.op("pe", fo, r=nms, w=pb(bo))
                P.op("pe", fd, r=nms + ["ones_b"], w=pb(bd))
                names = ["acc%d" % sp for sp in bl["span"]]
                srcO = bank(bo, 256).rearrange("p (c q) -> p c q", q=128)
                srcD = bank(bd, 256).rearrange("p (c q) -> p c q", q=128)
                if bl["g"] == 0:
                    P.op("act", CP(bl["dst"](accN), srcO), r=pb(bo), w=names, ss=True)
                    P.op("act", CP(bl["dst"](accD), srcD), r=pb(bd), w=names, ss=True)
                else:
                    P.op("dve", TT(bl["dst"](accN), srcO, bl["dst"](accN), ADD), r=pb(bo), w=names, ss=(bl["g"] != 2))
                    P.op("dve", TT(bl["dst"](accD), srcD, bl["dst"](accD), ADD), r=pb(bd), w=names, ss=(bl["g"] != 2))

            if "b" in SUB:
                b_scores(0, blocks[0])
                for k in range(len(blocks)):
                    if k + 1 < len(blocks):
                        b_scores(k + 1, blocks[k + 1])
                    b_values(k, blocks[k])
            P.barrier()


        P3W_LO = 84224
        P3W = {"issued": False}

        def p3w():
            A.at(P3W_LO)
            Wbra_ = A.alloc([4, D], BF16)
            Wbrb_ = A.alloc([2, D], BF16)
            assert A.off <= OFF_QKV_END
            A.at(OFF_P2T)
            Wg_ = A.alloc([8, 2048], BF16)
            Wo_ = A.alloc([8, D], BF16)
            return Wg_, Wbra_, Wbrb_, Wo_

        def p3w_issue():
            P3W["issued"] = True
            Wg, Wbra, Wbrb, Wo = p3w()
            wgv = I["wg"].rearrange("(kc p) n -> p kc n", p=128)
            def _wg(c4):
                P.dma("pool", "Wg%d" % c4, DMA(Wg[:, :, c4 * 512:(c4 + 1) * 512], wgv[:, :, c4 * 512:(c4 + 1) * 512]), w=["Wg%d" % c4])

            def _wbr(c2):
                P.dma("pool", "Wbra%d" % c2, DMA(Wbra[:, :, c2 * 512:(c2 + 1) * 512],
                                                I["wbra"].rearrange("(kc p) n -> p kc n", p=128)[:, :, c2 * 512:(c2 + 1) * 512]), w=["Wbra%d" % c2])
                P.dma("pool", "Wbrb%d" % c2, DMA(Wbrb[:, :, c2 * 512:(c2 + 1) * 512],
                                                I["wbrb"].rearrange("(kc p) n -> p kc n", p=128)[:, :, c2 * 512:(c2 + 1) * 512]), w=["Wbrb%d" % c2])

            _wg(0); _wg(2); _wbr(0); _wg(1); _wg(3); _wbr(1)
            for c2 in range(2):
                P.dma("pool", "Wo%d" % c2, DMA(Wo[:, :, c2 * 512:(c2 + 1) * 512],
                                              I["wo"].rearrange("(kc p) n -> p kc n", p=128)[:, :, c2 * 512:(c2 + 1) * 512]), w=["Wo%d" % c2])

        if "2" in PHASES and "s" in SUB:
            A.at(0)
            Cs = [A.alloc([4, 512], F32) for _ in range(3)]
            Nw = [A.alloc([1, 512], F32) for _ in range(2)]
            Kb = [A.alloc([4, 256], BF16) for _ in range(2)]
            Knb = [A.alloc([1, 256], BF16) for _ in range(2)]
            KTs = [A.alloc([1, 1088], BF16) for _ in range(2)]
            Vb = A.alloc([NS, 1024], BF16)
            Vnb = A.alloc([NS, 256], BF16)
            PTc = A.alloc([2, 256], BF16)
            PTn = A.alloc([2, 256], BF16)
            recs = A.alloc([1, 256], F32)
            prods = A.alloc([1, 256], F32)
            sink2 = A.alloc([2, 256], BF16)
            assert A.off <= P3W_LO, (A.off, P3W_LO)
            if "3" in PHASES:
                p3w_issue()
            P.op("dve", TC(sink2[0:1, :, :].rearrange("p k (n c) -> p k n c", c=4),
                           sink_f[0:1, :].rearrange("p (k c) -> p k c", c=4).unsqueeze(2).to_broadcast([1, 2, NTS, 4])), r=["sink_f"], w=["sink2"])
            sgroups = [dict(nm="A", cache=I["ca"], new=O["ca_s"], HD=128, dil=0, qb=0, nh2=4, mk=(0, 2)),
                       dict(nm="b1", cache=I["cb1"], new=O["cb1_s"], HD=256, dil=0, qb=5, nh2=2, mk=(0, 2)),
                       dict(nm="b2", cache=I["cb2"], new=O["cb2_s"], HD=256, dil=4, qb=9, nh2=2, mk=(1, 3)),
                       dict(nm="b3", cache=I["cb3"], new=O["cb3_s"], HD=256, dil=16, qb=13, nh2=2, mk=(1, 3))]
            cnt = 0
            for gi, G in enumerate(sgroups):
                HD, dil, nh2, qb = G["HD"], G["dil"], G["nh2"], G["qb"]
                isA = G["nm"] == "A"
                T = 1 if dil == 0 else 4
                nch = HD // 128
                nc2 = NS * 4 * nh2
                Sc = [bank(rg, nc2).rearrange("p (n t h) -> p n t h", t=4, h=nh2) for rg in range(2)]
                Sn = [bank(2 + rg, nc2).rearrange("p (n t h) -> p n t h", t=4, h=nh2) for rg in range(2)]
                base = cnt
                cnt += NS

                def s_load(n):
                    q3, par = (base + n) % 3, (base + n) % 2
                    C_, N_ = Cs[q3], Nw[par]
                    cn, nn = "C%d" % q3, "Nw%d" % par
                    if dil == 0:
                        P.dma("sp", cn, DMA(C_[:, 0, 0:2 * HD], G["cache"][n, :, :]), w=[cn])
                    else:
                        P.dma("sp", cn, DMA(C_[:, 0:4, :], G["cache"][n].rearrange("(m r) c -> m r c", r=dil)[:, 0:4, :]), w=[cn])
                    P.dma("act", nn, DMA(N_[0:4, 0, 0:2 * HD], G["new"][n * 4:(n + 1) * 4, :]), w=[nn])

                def s_pre(n):
                    q3, par = (base + n) % 3, (base + n) % 2
                    C_, N_, K_, Kn_, KT_ = Cs[q3], Nw[par], Kb[par], Knb[par], KTs[par]
                    cn, nn, kn, knn, ktn = "C%d" % q3, "Nw%d" % par, "Kb%d" % par, "Knb%d" % par, "KTs%d" % par
                    P.op("dve", TC(K_[:, 0:T, 0:HD], C_[:, 0:T, 0:HD]), r=[cn], w=[kn])
                    P.op("dve", TC(Vb[:, n, 0:T * HD].rearrange("p (t d) -> p t d", d=HD), C_[:, 0:T, HD:2 * HD]), r=[cn], w=["Vb"], ss=True)
                    P.op("act", CP(Kn_[0:4, 0, 0:HD], N_[0:4, 0, 0:HD]), r=[nn], w=[knn])
                    P.op("act", CP(Vnb[0:4, n, 0:HD], N_[0:4, 0, HD:2 * HD]), r=[nn], w=["Vnb"], ss=True)
                    bk = 6 + par
                    fns = []
                    for tau in range(T):
                        for ch in range(nch):
                            sl = tau * nch + ch
                            fns.append(TR(bank_bf(bk)[:, sl * 128:(sl + 1) * 128], K_[:, tau, ch * 128:(ch + 1) * 128], idb[:, :]))
                    P.op("pe", fns, r=[kn, "idb"], w=pb(bk))
                    P.op("pe", [TR(bank_bf(5)[:, par * 64 + ch * 4:par * 64 + ch * 4 + 4], Kn_[0:4, 0, ch * 128:(ch + 1) * 128], idb[0:4, 0:4])
                                for ch in range(nch)], r=[knn, "idb"], w=pb(5))
                    P.op("act", CP(KT_[:, 0, 0:T * nch * 128], bank_bf(bk)[:, 0:T * nch * 128]), r=pb(bk), w=[ktn])
                    P.op("act", CP(KT_[:, 0, 1024:1024 + nch * 4], bank_bf(5)[:, par * 64:par * 64 + nch * 4]), r=pb(5), w=[ktn])

                def s_scores(n):
                    par = (base + n) % 2
                    KT_, ktn = KTs[par], "KTs%d" % par
                    fl = {0: ([], []), 1: ([], [])}
                    for rg in range(2):
                        p0 = rg * 64
                        fns, fnn = fl[rg]
                        for hh in range(nh2):
                            kcol = 0 if isA else hh
                            qch = hh if isA else qb + hh
                            q_all = QS[p0:p0 + 64, qch, n * 4:(n + 1) * 4]
                            if dil == 0:
                                fns.append(MM(Sc[rg][:, n, :, hh], KT_[p0:p0 + 64, 0, kcol * 128:(kcol + 1) * 128], q_all))
                            else:
                                for t in range(4):
                                    fns.append(MM(Sc[rg][:, n, t, hh:hh + 1], KT_[p0:p0 + 64, 0, (t * 2 + kcol) * 128:(t * 2 + kcol + 1) * 128],
                                                  QS[p0:p0 + 64, qch, n * 4 + t:n * 4 + t + 1]))
                            fnn.append(MM(Sn[rg][0:4, n, :, hh], KT_[p0:p0 + 64, 0, 1024 + kcol * 4:1024 + kcol * 4 + 4], q_all))
                    mix_ = lambda a, b_: [x_ for pr in zip(a, b_) for x_ in pr]
                    P.op("pe", mix_(fl[0][0], fl[1][0]), r=[ktn], w=pb(0, 1))
                    P.op("pe", mix_(fl[0][1], fl[1][1]), r=[ktn], w=pb(2, 3))

                s_load(0)
                s_load(1)
                s_pre(0)
                for n in range(NS):
                    if n + 2 < NS:
                        s_load(n + 2)
                    if n + 1 < NS:
                        s_pre(n + 1)
                    s_scores(n)
                    if gi == 0 and "b" in SUB:
                        sl_ = slice(n * 128, (n + 1) * 128)
                        P.op("dve", lambda e, sl_=sl_: e.reciprocal(out=accD[:, :, sl_], in_=accD[:, :, sl_]), w=["obfin"], ss=True)
                        P.op("dve", TT(obT[:, :, sl_], accN[:, :, sl_], accD[:, :, sl_], MUL), r=["obfin"], w=["obfin2"])
                mc, mn = G["mk"]
                for rg in range(2):
                    P.op("act", ACTF(PTc[:, rg, 0:nc2], bank(rg, nc2), AF.Exp, scale=SCALE), r=pb(rg), w=["PTc%d" % rg])
                    P.op("act", ACTF(PTn[0:4, rg, 0:nc2], bank(2 + rg, nc2, 0, 0, 4), AF.Exp, scale=SCALE), r=pb(2 + rg), w=["PTn%d" % rg])
                    pcv = PTc[:, rg, 0:nc2].rearrange("p (n t h) -> p n t h", t=4, h=nh2)
                    pnv = PTn[0:4, rg, 0:nc2].rearrange("p (n t h) -> p n t h", t=4, h=nh2)
                    P.op("pool", TT(pcv, pcv, smask_b[:, mc, :].unsqueeze(1).unsqueeze(3).to_broadcast([128, NS, 4, nh2]), MUL),
                         r=["PTc%d" % rg, "smask_b"], w=["PTc%d" % rg])
                    P.op("pool", TT(pnv, pnv, smask_b[0:4, mn, :].unsqueeze(1).unsqueeze(3).to_broadcast([4, NS, 4, nh2]), MUL),
                         r=["PTn%d" % rg, "smask_b"], w=["PTn%d" % rg])
                ptn = ["PTc0", "PTc1", "PTn0", "PTn1"]
                fo_l, fd_l = {0: [], 1: []}, {0: [], 1: []}
                for rg in range(2):
                    fo, fd = fo_l[rg], fd_l[rg]
                    p0, p1 = rg * 64, rg * 64 + 64
                    pc = PTc[:, rg, 0:nc2].rearrange("p (n t h) -> p n t h", t=4, h=nh2)
                    pn = PTn[0:4, rg, 0:nc2].rearrange("p (n t h) -> p n t h", t=4, h=nh2)
                    if isA:
                        fd.append(MM(bank(0, 256, 0, p0, p1), ones_b[:, 0:64], PTc[:, rg, 0:256], start=True, stop=False))
                        fd.append(MM(bank(0, 256, 0, p0, p1), ones_b[0:4, 0:64], PTn[0:4, rg, 0:256], start=False, stop=False))
                        fd.append(MM(bank(0, 256, 0, p0, p1), ones_b[0:1, 0:64], sink2[0:1, rg, :], start=False, stop=True))
                        for n in range(NS):
                            fo.append(MM(bank(4, 16, n * 16, p0, p1), Vnb[0:4, n, p0:p1], PTn[0:4, rg, n * 16:(n + 1) * 16], start=True, stop=False))
                            fo.append(MM(bank(4, 16, n * 16, p0, p1), Vb[:, n, p0:p1], PTc[:, rg, n * 16:(n + 1) * 16], start=False, stop=True))
                    else:
                        for c in range(2):
                            o_all = bank(0, NTS, c * NTS, p0, p1)
                            fd.append(MM(o_all, ones_b[:, 0:64], pc[:, :, :, c], start=True, stop=False))
                            fd.append(MM(o_all, ones_b[0:4, 0:64], pn[:, :, :, c], start=False, stop=True))
                            vcol = c * 128 + rg * 64
                            for n in range(NS):
                                o_n = bank(4, 4, c * NTS + n * 4, p0, p1)
                                fo.append(MM(o_n, Vnb[0:4, n, vcol:vcol + 64], pn[:, n, :, c], start=True, stop=False, skip=(dil != 0)))
                                if dil == 0:
                                    fo.append(MM(o_n, Vb[:, n, vcol:vcol + 64], pc[:, n, :, c], start=False, stop=True))
                                else:
                                    for t in range(4):
                                        fo.append(MM(bank(4, 1, c * NTS + n * 4 + t, p0, p1), Vb[:, n, t * 256 + vcol:t * 256 + vcol + 64],
                                                     pc[:, n, t, c:c + 1], start=False, stop=(t == 3), skip=True))
                mix2 = lambda a, b_: [x_ for pr in zip(a, b_) for x_ in pr]
                P.op("pe", mix2(fd_l[0], fd_l[1]), r=ptn + ["ones_b", "sink2"], w=pb(0))
                P.op("pe", mix2(fo_l[0], fo_l[1]), r=ptn + ["Vb", "Vnb"], w=pb(4))
                if isA:
                    P.op("dve", lambda e: e.reciprocal(out=recs[:, 0, :], in_=bank(0, 256)), r=pb(0), w=["recs"])
                    P.op("dve", TT(prods[:, 0, :], bank(4, 256), recs[:, 0, :], MUL), r=pb(4) + ["recs"], w=["prods"])
                    P.op("pool", TC(oaT[:, 0:4, SEQ:SEQ + NTS], prods[:, 0, :].rearrange("p (n c) -> p c n", c=4)), r=["prods"])
                else:
                    dN = accN[:, :, SEQ:SEQ + NTS]
                    dD = accD[:, :, SEQ:SEQ + NTS]
                    sN = bank(4, 2 * NTS).rearrange("p (c n) -> p c n", c=2)
                    sD = bank(0, 2 * NTS).rearrange("p (c n) -> p c n", c=2)
                    if gi == 1:
                        P.op("dve", TC(dN, sN), r=pb(4), w=["accS"])
                        P.op("dve", TC(dD, sD), r=pb(0), w=["accS"])
                    else:
                        P.op("dve", TT(dN, sN, dN, ADD), r=pb(4), w=["accS"])
                        P.op("dve", TT(dD, sD, dD, ADD), r=pb(0), w=["accS"])
            P.barrier()
            lo_ = SEQ if ("s" in SUB and "b" in SUB and "2" in PHASES) else 0
            P.op("dve", lambda e: e.reciprocal(out=accD[:, :, lo_:NT], in_=accD[:, :, lo_:NT]), w=["accDf"])
            P.op("dve", TT(obT[:, :, lo_:NT], accN[:, :, lo_:NT], accD[:, :, lo_:NT], MUL), r=["accDf"])
            P.barrier()

        def emit_ln(tag, z, ntok, out, lnb, gi):
            st_ = lnst[tag]
            P.op("dve", [lambda e: e.bn_stats(out=st_[0:ntok, 0, 0:6], in_=z[0:ntok, 0, 0:512]),
                         lambda e: e.bn_stats(out=st_[0:ntok, 0, 6:12], in_=z[0:ntok, 0, 512:1024])], r=[z_name[tag]], w=["st" + tag])
            P.op("dve", lambda e: e.bn_aggr(out=st_[0:ntok, 0, 12:14], in_=st_[0:ntok, 0, 0:12]), r=["st" + tag], w=["mv" + tag])
            P.op("act", ACTF(st_[0:ntok, 0, 14:15], st_[0:ntok, 0, 13:14], AF.Sqrt, bias=EPS), r=["mv" + tag], w=["rs" + tag])
            P.op("dve", lambda e: e.reciprocal(out=st_[0:ntok, 0, 14:15], in_=st_[0:ntok, 0, 14:15]), r=["rs" + tag], w=["rs" + tag])
            P.op("dve", lambda e: e.tensor_scalar(out=st_[0:ntok, 0, 15:16], in0=st_[0:ntok, 0, 12:13], scalar1=-1.0,
                                                  scalar2=st_[0:ntok, 0, 14:15], op0=ALU.mult, op1=ALU.mult),
                 r=["mv" + tag, "rs" + tag], w=["nm" + tag])
            P.op("act", ACTF(out[0:ntok, 0, :], z[0:ntok, 0, :], AF.Identity, scale=st_[0:ntok, 0, 14:15], bias=st_[0:ntok, 0, 15:16]),
                 r=[z_name[tag], "rs" + tag, "nm" + tag], w=[out_name[tag]])
            P.op("dve", TT(out[0:ntok, 0, :], out[0:ntok, 0, :], lnb[0:ntok, gi, :], MUL), r=[out_name[tag], "lnb"], w=[out_name[tag]])
            P.op("dve", TT(out[0:ntok, 0, :], out[0:ntok, 0, :], lnb[0:ntok, gi + 1, :], ADD), r=[out_name[tag], "lnb"], w=[out_name[tag]])

        lnst, z_name, out_name = {}, {}, {}
        tiles = [(0, 512), (512, 512), (1024, 512), (1536, 512), (SEQ, NTS)]

        def xrows(t0, n):
            return I["xp"][t0:t0 + n, :] if t0 < SEQ else I["xs"][t0 - SEQ:t0 - SEQ + n, :]

        if "3" in PHASES:
            A.at(0)
            hT = A.alloc([8, NT], BF16)
            OFF_HT_END = A.off
            Wg, Wbra, Wbrb, Wo = p3w()
            A.at(OFF_HT_END)
            xs6 = [A.alloc([1, D], F32) for _ in range(6)]
            lnb = A.alloc([2, D], F32)
            xTt = A.alloc([8, 512], BF16)
            mT = A.alloc([8, 512], BF16)
            gcol = A.alloc([2, 8], F32)
            st3 = [A.alloc([1, 16], F32) for _ in range(2)]
            assert A.off <= P3W_LO, (A.off, P3W_LO)
            A.at(OFF_P2T - 2 * 2 * NT * 4)
            sg = [A.alloc([1, 512], F32) for _ in range(4)]
            xb3 = [A.alloc([1, D], BF16) for _ in range(2)]
            hb3 = [A.alloc([1, D], BF16) for _ in range(3)]
            hh = [A.alloc([1, D], F32) for _ in range(3)]
            assert A.off <= OFF_P2T, (A.off, OFF_P2T)
            P.dma("sp", "lnb", DMA(lnb[:, 0, :], I["ln"][0:1, :].partition_broadcast(128)), w=["lnb"])
            P.dma("sp", "lnb", DMA(lnb[:, 1, :], I["ln"][1:2, :].partition_broadcast(128)), w=["lnb"])
            P.dma("sp", "gcol", DMA(gcol[:, :, :], I["lncol"][:, 0:2, :]), w=["gcol"])
            if not P3W["issued"]:
                p3w_issue()
            lnst["3a"], lnst["3b"] = st3[0], st3[1]
            subs = []
            for ti, (t0, TT_) in enumerate(tiles):
                for sbk in range((TT_ + 127) // 128):
                    subs.append((ti, t0 + sbk * 128, min(128, TT_ - sbk * 128)))
            xbuf = {sidx: sidx % 6 for sidx in range(len(subs))}

            def p3_load(ti, late=None):
                busy = set(xbuf[q] for q, sq in enumerate(subs) if sq[0] == ti - 1)
                for sidx, (tj, s0, n) in enumerate(subs):
                    if tj == ti:
                        bi = xbuf[sidx]
                        if late is not None and ((bi in busy) != late):
                            continue
                        P.dma("sp", "xs%d" % bi, DMA(xs6[bi][0:n, 0, :], xrows(s0, n)), w=["xs%d" % bi])

            def p3_xT(ti, late=None):
                busy = set(xbuf[q] for q, sq in enumerate(subs) if sq[0] == ti - 1)
                for sidx, (tj, s0, n) in enumerate(subs):
                    if tj != ti:
                        continue
                    bi = xbuf[sidx]
                    if late is not None and ((bi in busy) != late):
                        continue
                    c0 = s0 - tiles[ti][0]
                    xb_ = xb3[sidx % 2]
                    if sidx % 2 == 0:
                        P.op("act", CP(xb_[0:n, 0, :], xs6[bi][0:n, 0, :]), r=["xs%d" % bi], w=["xb3_%d" % (sidx % 2)])
                    else:
                        P.op("dve", TC(xb_[0:n, 0, :], xs6[bi][0:n, 0, :]), r=["xs%d" % bi], w=["xb3_%d" % (sidx % 2)])
                    bk = 6 + sidx % 2
                    P.op("pe", [TR(bank_bf(bk)[:, kc * 128:kc * 128 + n], xb_[0:n, 0, kc * 128:(kc + 1) * 128], idb[0:n, 0:n]) for kc in range(8)],
                         r=["xb3_%d" % (sidx % 2), "idb"], w=pb(bk))
                    srcv = bank_bf(bk).rearrange("p (a b) -> p a b", b=128)[:, :, 0:n]
                    dstv = xTt[:, 0:8, c0:c0 + n]
                    if sidx % 2 == 0:
                        P.op("act", CP(dstv, srcv), r=pb(bk), w=["xTt"], ss=True)
                    else:
                        P.op("dve", TC(dstv, srcv), r=pb(bk), w=["xTt"], ss=True)

            def p3_gates(ti):
                t0, TT_ = tiles[ti]
                for f in range(8):
                    s_ = f % 2
                    bga, bgb, bra, brb = 4 * s_, 4 * s_ + 1, 4 * s_ + 2, 4 * s_ + 3
                    fcol = slice(f * 128, (f + 1) * 128)
                    P.op("pe", [MM(bank(bga, TT_), Wg[:, kc, f * 128:(f + 1) * 128], xTt[:, kc, 0:TT_], start=(kc == 0), stop=(kc == 7)) for kc in range(8)],
                         r=["xTt", "Wg%d" % (f // 4)], w=pb(bga))
                    P.op("pe", [MM(bank(bgb, TT_), Wg[:, kc, 1024 + f * 128:1024 + (f + 1) * 128], xTt[:, kc, 0:TT_], start=(kc == 0), stop=(kc == 7)) for kc in range(8)],
                         r=["xTt", "Wg%d" % (2 + f // 4)], w=pb(bgb))
                    P.op("pe", [MM(bank(bra, TT_), Wbra[:, c, f * 128:(f + 1) * 128], oaT[:, c, t0:t0 + TT_], start=(c == 0), stop=(c == 3)) for c in range(4)],
                         r=["Wbra%d" % (f // 4)], w=pb(bra))
                    P.op("pe", [MM(bank(brb, TT_), Wbrb[:, c, f * 128:(f + 1) * 128], obT[:, c, t0:t0 + TT_], start=(c == 0), stop=(c == 1)) for c in range(2)],
                         r=["Wbrb%d" % (f // 4)], w=pb(brb))
                    sa, sb_ = sg[2 * s_], sg[2 * s_ + 1]
                    P.op("act", ACTF(sa[:, 0, 0:TT_], bank(bga, TT_), AF.Sigmoid), r=pb(bga), w=["sg%d" % (2 * s_)])
                    P.op("act", ACTF(sb_[:, 0, 0:TT_], bank(bgb, TT_), AF.Sigmoid), r=pb(bgb), w=["sg%d" % (2 * s_ + 1)])
                    P.op("dve", TT(sa[:, 0, 0:TT_], bank(bra, TT_), sa[:, 0, 0:TT_], MUL), r=pb(bra), w=["sg%d" % (2 * s_)])
                    P.op("dve", TT(sb_[:, 0, 0:TT_], bank(brb, TT_), sb_[:, 0, 0:TT_], MUL), r=pb(brb), w=["sg%d" % (2 * s_ + 1)])
                    P.op("pool" if f < 6 else "dve", TT(mT[:, f, 0:TT_], sa[:, 0, 0:TT_], sb_[:, 0, 0:TT_], ADD),
                         r=["sg%d" % (2 * s_), "sg%d" % (2 * s_ + 1)], w=["mT"], ss=(f < 6))

            def p3_mix(sidx, pre_hT=None):
                tj, s0, n = subs[sidx]
                bi = xbuf[sidx]
                c0 = s0 - tiles[tj][0]
                b0 = 2 * (sidx % 2)
                q3 = sidx % 3
                zx = xs6[bi]
                for hf in range(2):
                    P.op("pe", [MM(bank(b0 + hf, 512, 0, 0, n), mT[:, kc, c0:c0 + n], Wo[:, kc, hf * 512:(hf + 1) * 512], start=(kc == 0), stop=(kc == 7))
                                for kc in range(8)], r=["mT", "Wo%d" % hf], w=pb(b0 + hf))
                if pre_hT is not None:
                    p3_hT(pre_hT)
                for hf in range(2):
                    P.op("dve", STT(zx[0:n, 0, hf * 512:(hf + 1) * 512], zx[0:n, 0, hf * 512:(hf + 1) * 512], ALPHA, bank(b0 + hf, 512, 0, 0, n), MUL, ADD),
                         r=pb(b0 + hf), w=["xs%d" % bi])
                st_ = st3[sidx % 2]
                tg = "3" + "ab"[sidx % 2]
                P.op("dve", [lambda e: e.bn_stats(out=st_[0:n, 0, 0:6], in_=zx[0:n, 0, 0:512]),
                             lambda e: e.bn_stats(out=st_[0:n, 0, 6:12], in_=zx[0:n, 0, 512:1024])], r=["xs%d" % bi], w=["st" + tg])
                P.op("dve", lambda e: e.bn_aggr(out=st_[0:n, 0, 12:14], in_=st_[0:n, 0, 0:12]), r=["st" + tg], w=["mv" + tg])
                P.op("act", ACTF(st_[0:n, 0, 14:15], st_[0:n, 0, 13:14], AF.Sqrt, bias=EPS), r=["mv" + tg], w=["rs" + tg])
                P.op("dve", lambda e: e.reciprocal(out=st_[0:n, 0, 14:15], in_=st_[0:n, 0, 14:15]), r=["rs" + tg], w=["rs" + tg])
                P.op("dve", lambda e: e.tensor_scalar(out=st_[0:n, 0, 15:16], in0=st_[0:n, 0, 12:13], scalar1=-1.0,
                                                      scalar2=st_[0:n, 0, 14:15], op0=ALU.mult, op1=ALU.mult), r=["mv" + tg, "rs" + tg], w=["nm" + tg])
                hbf_, hb_ = hb3[q3], hh[q3]
                P.op("act", ACTF(hbf_[0:n, 0, :], zx[0:n, 0, :], AF.Identity, scale=st_[0:n, 0, 14:15], bias=st_[0:n, 0, 15:16]),
                     r=["xs%d" % bi, "rs" + tg, "nm" + tg], w=["hb3_%d" % q3])
                P.op("act", ACTF(hb_[0:n, 0, :], zx[0:n, 0, :], AF.Identity, scale=st_[0:n, 0, 14:15], bias=st_[0:n, 0, 15:16]),
                     r=["xs%d" % bi, "rs" + tg, "nm" + tg], w=["hh%d" % q3])
                P.op("pool", TT(hb_[0:n, 0, :], hb_[0:n, 0, :], lnb[0:n, 0, :], MUL), r=["hh%d" % q3, "lnb"], w=["hh%d" % q3])
                P.op("pool", TT(hb_[0:n, 0, :], hb_[0:n, 0, :], lnb[0:n, 1, :], ADD), r=["hh%d" % q3, "lnb"], w=["hh%d" % q3])
                P.dma("pool", "hh%d" % q3, DMA(h_scr[s0:s0 + n, :], hb_[0:n, 0, :]), r=["hh%d" % q3])

            def p3_hT(sidx):
                tj, s0, n = subs[sidx]
                q3 = sidx % 3
                hbf_ = hb3[q3]
                P.op("pe", [TR(bank_bf(4)[:, kc * 128:kc * 128 + n], hbf_[0:n, 0, kc * 128:(kc + 1) * 128], idb[0:n, 0:n]) for kc in range(4)],
                     r=["hb3_%d" % q3, "idb"], w=pb(4))
                P.op("pe", [TR(bank_bf(5)[:, (kc - 4) * 128:(kc - 4) * 128 + n], hbf_[0:n, 0, kc * 128:(kc + 1) * 128], idb[0:n, 0:n]) for kc in range(4, 8)],
                     r=["hb3_%d" % q3, "idb"], w=pb(5))
                for kc in range(4):
                    P.op("dve", STT(hT[:, kc, s0:s0 + n], bank_bf(4)[:, kc * 128:kc * 128 + n], gcol[:, 0, kc:kc + 1],
                                    gcol[:, 1, kc:kc + 1].to_broadcast([128, n]), MUL, ADD), r=pb(4) + ["gcol"], ss=True)
                for kc in range(4, 8):
                    P.op("act", ACTF(hT[:, kc, s0:s0 + n], bank_bf(5)[:, (kc - 4) * 128:(kc - 4) * 128 + n], AF.Identity,
                                     scale=gcol[:, 0, kc:kc + 1], bias=gcol[:, 1, kc:kc + 1]), r=pb(5) + ["gcol"], ss=True)

            def p3_post(ti):
                ss_ = [q for q, sq in enumerate(subs) if sq[0] == ti]
                m = len(ss_)
                for k_, sidx in enumerate(ss_):
                    p3_mix(sidx, ss_[k_ - 2] if k_ >= 2 else None)
                if ti + 1 < len(tiles):
                    p3_load(ti + 1, late=True)
                    p3_xT(ti + 1, late=True)
                for k_ in range(max(0, m - 2), m):
                    p3_hT(ss_[k_])

            p3_load(0)
            p3_xT(0)
            for ti in range(len(tiles)):
                if ti + 1 < len(tiles):
                    p3_load(ti + 1, late=False)
                p3_gates(ti)
                if ti + 1 < len(tiles):
                    p3_xT(ti + 1, late=False)
                p3_post(ti)
            P.barrier()

        NWDA = 17
        WDA_LO = ARENA_BYTES - NWDA * 2048
        A.at(WDA_LO)
        WdA = A.alloc([NWDA, D], BF16)
        WDA = {"issued": False}

        if "4" in PHASES:
            A.at(8 * NT * 2)
            gT = A.alloc([NJ, NT], BF16)
            OFF_GT_END = A.off
            wu = [A.alloc([8, 256], BF16) for _ in range(3)]
            cw = A.alloc([NJ, 3], F32)
            cb = A.alloc([1, NJ], F32)
            ue = [A.alloc([1, 520], F32) for _ in range(2)]
            ues = A.alloc([NS, 6], F32)
            a0 = [A.alloc([1, 512], F32) for _ in range(2)]
            a1 = [A.alloc([1, 512], F32) for _ in range(2)]
            gg = [A.alloc([1, 512], F32) for _ in range(2)]
            stT = A.alloc([NJ, 32], F32)
            ust = A.alloc([2, NJ], F32)
            usts = A.alloc([NJ, 32], F32)
            stc_in = A.alloc([1, DFF], F32)
            so_p = A.alloc([1, 128], F32)
            so_s = stc_in
            assert A.off <= WDA_LO, (A.off, WDA_LO)
            hTv = arena_t[:, 0:8 * NT].rearrange("p (a b) -> p a b", b=NT)
            P.dma("sp", "cw", DMA(cw[:, :, :], I["convw"][:, :, :]), w=["cw"])
            P.dma("sp", "cb", DMA(cb[:, 0, :], I["convb"][:, :]), w=["cb"])
            P.dma("sp", "stc_in", DMA(stc_in[0:32, 0, :], I["stc"][:, :]), w=["stc_in"])
            for q4 in range(0, NJ, 4):
                bk = 6 + (q4 // 4) % 2
                js = list(range(q4, min(q4 + 4, NJ)))
                P.op("pe", [TR(bank(bk, 32, (j - q4) * 32), stc_in[0:32, 0, j * 128:(j + 1) * 128], ident[0:32, 0:32]) for j in js],
                     r=["stc_in", "ident"], w=pb(bk))
                P.op("act", CP(stT[:, q4:q4 + len(js), :], bank(bk, 32 * len(js)).rearrange("p (a b) -> p a b", b=32)), r=pb(bk), w=["stT"])
            wupv = I["wup"].rearrange("(kc p) n -> p kc n", p=128)

            def p4_w(j):
                w_ = wu[j % 3]
                P.dma("pool", "wu%da" % (j % 3), DMA(w_[:, :, 0:128], wupv[:, :, j * 128:(j + 1) * 128]), w=["wu%da" % (j % 3)])
                P.dma("pool", "wu%db" % (j % 3), DMA(w_[:, :, 128:256], wupv[:, :, DFF + j * 128:DFF + (j + 1) * 128]), w=["wu%db" % (j % 3)])

            p4_w(0)
            p4_w(1)
            cnt = 0
            wdv = I["wdn"].rearrange("(j p) n -> p j n", p=128)
            for j in range(NJ):
                if j + 2 < NJ:
                    p4_w(j + 2)
                if "5" in PHASES and j % 2 == 0 and j < NWDA:
                    nq = min(2, NWDA - j)
                    for hf in range(2):
                        P.dma("pool", "WdA%d_%d" % (j, hf), DMA(WdA[:, j:j + nq, hf * 512:(hf + 1) * 512], wdv[:, j:j + nq, hf * 512:(hf + 1) * 512]),
                              w=["WdA%d_%d" % (j, hf)])
                    WDA["issued"] = True
                w_ = wu[j % 3]
                for ti, (t0, TT_) in enumerate(tiles):
                    par = cnt % 4
                    cnt += 1
                    bu, bv = 2 * par, 2 * par + 1
                    P.op("pe", [MM(bank(bu, TT_), w_[:, kc, 0:128], hTv[:, kc, t0:t0 + TT_], start=(kc == 0), stop=(kc == 7)) for kc in range(8)],
                         r=["wu%da" % (j % 3)], w=pb(bu))
                    P.op("pe", [MM(bank(bv, TT_), w_[:, kc, 128:256], hTv[:, kc, t0:t0 + TT_], start=(kc == 0), stop=(kc == 7)) for kc in range(8)],
                         r=["wu%db" % (j % 3)], w=pb(bv))
                    A0, A1, G_ = a0[par % 2], a1[par % 2], gg[par % 2]
                    an, a1n, gn = "a0_%d" % (par % 2), "a1_%d" % (par % 2), "gg%d" % (par % 2)
                    if t0 < SEQ:
                        U = ue[ti % 2]
                        un = "ue%d" % (ti % 2)
                        if ti == 0:
                            P.op("pool", lambda e, U=U: e.memset(U[:, 0, 0:2], 0.0), w=[un + "h"])
                        else:
                            P.op("pool", TC(U[:, 0, 0:2], ue[(ti - 1) % 2][:, 0, 512:514]), r=["ue%d" % ((ti - 1) % 2)], w=[un + "h"])
                        P.op("act", CP(U[:, 0, 2:514], bank(bu)), r=pb(bu), w=[un])
                        P.op("act", ACTF(A0[:, 0, :], bank(bu), AF.Identity, scale=cw[:, j, 2:3], bias=cb[:, 0, j:j + 1]), r=pb(bu) + ["cw", "cb"], w=[an])
                        P.op("dve", STT(A1[:, 0, :], U[:, 0, 1:513], cw[:, j, 1:2], A0[:, 0, :], MUL, ADD), r=[un, un + "h", an, "cw"], w=[a1n])
                        P.op("dve", STT(A0[:, 0, :], U[:, 0, 0:512], cw[:, j, 0:1], A1[:, 0, :], MUL, ADD), r=[un, un + "h", a1n, "cw"], w=[an])
                        P.op("act", ACTF(G_[:, 0, :], A0[:, 0, :], AF.Gelu), r=[an], w=[gn])
                        P.op("dve", TT(gT[:, j, t0:t0 + 512], bank(bv), G_[:, 0, :], MUL), r=pb(bv) + [gn])
                        if ti == 3:
                            P.op("pool", TC(ust[:, :, j], U[:, 0, 512:514]), r=[un], w=["ust"], ss=True)
                    else:
                        P.op("pool", TC(ues[:, :, 0:2], stT[:, j, :].rearrange("p (n r) -> p n r", r=2)), r=["stT"], w=["uesh"])
                        P.op("act", CP(ues[:, :, 2:6], bank(bu, NTS).rearrange("p (n t) -> p n t", t=4)), r=pb(bu), w=["ues"])
                        P.op("act", ACTF(A0[:, 0, 0:NTS], bank(bu, NTS), AF.Identity, scale=cw[:, j, 2:3], bias=cb[:, 0, j:j + 1]), r=pb(bu) + ["cw", "cb"], w=[an])
                        v4 = lambda ap: ap.rearrange("p (n t) -> p n t", t=4)
                        P.op("dve", STT(v4(A1[:, 0, 0:NTS]), ues[:, :, 1:5], cw[:, j, 1:2], v4(A0[:, 0, 0:NTS]), MUL, ADD), r=["ues", "uesh", an, "cw"], w=[a1n])
                        P.op("dve", STT(v4(A0[:, 0, 0:NTS]), ues[:, :, 0:4], cw[:, j, 0:1], v4(A1[:, 0, 0:NTS]), MUL, ADD), r=["ues", "uesh", a1n, "cw"], w=[an])
                        P.op("act", ACTF(G_[:, 0, 0:NTS], A0[:, 0, 0:NTS], AF.Gelu), r=[an], w=[gn])
                        P.op("dve", TT(gT[:, j, SEQ:SEQ + NTS], bank(bv, NTS), G_[:, 0, 0:NTS], MUL), r=pb(bv) + [gn])
                        P.op("pool", TC(usts[:, j, :].rearrange("p (n r) -> p n r", r=2), ues[:, :, 4:6]), r=["ues"], w=["usts"], ss=True)
            P.op("pe", TR(bank(4, 128, 0, 0, 2 * NJ), ust[:, :, :].rearrange("p r j -> p (r j)"), ident[:, :]), r=["ident", "ust"], w=pb(4))
            P.op("act", CP(so_p[0:2 * NJ, 0, :], bank(4, 128, 0, 0, 2 * NJ)), r=pb(4), w=["so_p"])
            for r_ in range(2):
                P.dma("sp", "so_p%d" % r_, DMA(O["sc_p"][r_:r_ + 1, :].rearrange("o (j f) -> (o j) f", f=128), so_p[r_ * NJ:(r_ + 1) * NJ, 0, :]), r=["so_p"])
            for q4 in range(0, NJ, 4):
                bk = 6 + (q4 // 4) % 2
                js = list(range(q4, min(q4 + 4, NJ)))
                P.op("pe", [TR(bank(bk, 128, (j - q4) * 128, 0, 32), usts[:, j, :], ident[:, :]) for j in js], r=["usts", "ident"], w=pb(bk))
                P.op("act", CP(so_s[0:32, 0, q4 * 128:(q4 + len(js)) * 128], bank(bk, 128 * len(js), 0, 0, 32)), r=pb(bk), w=["stc_in"])
            P.dma("sp", "so_s", DMA(O["sc_s"][:, :], so_s[0:32, 0, :]), r=["stc_in"])
            P.barrier()

        if "5" in PHASES:
            A.at(8 * NT * 2)
            gT = A.alloc([NJ, NT], BF16)
            WdB = A.alloc([NJ - NWDA, D], BF16)
            assert A.off <= WDA_LO
            Wd_of = lambda j: (WdA[:, j, :] if j < NWDA else WdB[:, j - NWDA, :])
            A.at(0)
            hb5 = [A.alloc([1, D], F32) for _ in range(3)]
            yb = [A.alloc([1, D], F32) for _ in range(2)]
            lnb2 = A.alloc([2, D], F32)
            z5 = A.alloc([1, D], F32)
            st5 = A.alloc([1, 16], F32)
            assert A.off <= 8 * NT * 2
            P.dma("sp", "lnb", DMA(lnb2[:, 0, :], I["ln"][2:3, :].partition_broadcast(128)), w=["lnb"])
            P.dma("sp", "lnb", DMA(lnb2[:, 1, :], I["ln"][3:4, :].partition_broadcast(128)), w=["lnb"])
            wdv = I["wdn"].rearrange("(j p) n -> p j n", p=128)
            wd_names = {0: [], 1: []}
            for hf in range(2):
                P.dma("pool", "WdB_%d" % hf, DMA(WdB[:, :, hf * 512:(hf + 1) * 512], wdv[:, NWDA:NJ, hf * 512:(hf + 1) * 512]), w=["WdB_%d" % hf])
                wd_names[hf].append("WdB_%d" % hf)
            for q in range(0, NWDA, 2):
                nq = min(2, NWDA - q)
                for hf in range(2):
                    if not WDA["issued"]:
                        P.dma("pool", "WdA%d_%d" % (q, hf), DMA(WdA[:, q:q + nq, hf * 512:(hf + 1) * 512], wdv[:, q:q + nq, hf * 512:(hf + 1) * 512]),
                              w=["WdA%d_%d" % (q, hf)])
                    wd_names[hf].append("WdA%d_%d" % (q, hf))
            lnst["5"], z_name["5"] = st5, "z5"
            blks = [(i * 128, 128) for i in range(NB)] + [(SEQ, NTS)]

            def p5_load(i):
                s0, n = blks[i]
                P.dma("sp" if i % 2 == 0 else "act", "hb%d" % (i % 3), DMA(hb5[i % 3][0:n, 0, :], h_scr[s0:s0 + n, :]), w=["hb%d" % (i % 3)])

            p5_load(0)
            p5_load(1)
            for i, (s0, n) in enumerate(blks):
                if i + 2 < len(blks):
                    p5_load(i + 2)
                b0 = 2 * (i % 4)
                for hf in range(2):
                    P.op("pe", [MM(bank(b0 + hf, 512, 0, 0, n), gT[:, j, s0:s0 + n], Wd_of(j)[:, hf * 512:(hf + 1) * 512], start=(j == 0), stop=(j == NJ - 1))
                                for j in range(NJ)], r=wd_names[hf], w=pb(b0 + hf))
                for hf in range(2):
                    P.op("dve", STT(z5[0:n, 0, hf * 512:(hf + 1) * 512], hb5[i % 3][0:n, 0, hf * 512:(hf + 1) * 512], ALPHA, bank(b0 + hf, 512, 0, 0, n), MUL, ADD),
                         r=["hb%d" % (i % 3)] + pb(b0 + hf), w=["z5"])
                out_name["5"] = "yb%d" % (i % 2)
                emit_ln("5", z5, n, yb[i % 2], lnb2, 0)
                dst = O["y_p"][s0:s0 + n, :] if s0 < SEQ else O["y_s"][:, :]
                P.dma("sp", "yb%d" % (i % 2), DMA(dst, yb[i % 2][0:n, 0, :]), r=["yb%d" % (i % 2)])

        P.barrier()
        with nc.Block() as block:
            P.replay(block)
    return nc


_CACHE = {}


def kernel(x_prompt, x_sample, cache_a, cache_b1, cache_b2, cache_b3, state_conv,
           w_in, sink_a, w_br_a, w_br_b, w_o, ln1_g, ln1_b, w_up, conv_w, conv_b, w_down, ln2_g, ln2_b):
    f = lambda a: np.ascontiguousarray(np.asarray(a, dtype=np.float32))
    cst = _consts()
    ln = np.stack([f(ln1_g)[0], f(ln1_b)[0], f(ln2_g)[0], f(ln2_b)[0]], axis=0)
    wts = _prep_weights(f(w_in)[0], f(w_br_a)[0], f(conv_w)[0], f(conv_b)[0], ln)
    shared = dict(wts)
    shared.update(cst)
    shared.update(wbrb=f(w_br_b)[0], wo=f(w_o)[0], wup=f(w_up)[0], wdn=f(w_down)[0], sink=f(sink_a).reshape(1, 8))
    xp, xs = f(x_prompt), f(x_sample)
    ca, cb1, cb2, cb3, stc = f(cache_a)[0], f(cache_b1)[0], f(cache_b2)[0], f(cache_b3)[0], f(state_conv)[0]
    in_maps = []
    for c in range(NCORES):
        n0, n1 = c * NS, (c + 1) * NS
        m = dict(shared)
        m.update(xp=xp[c], xs=xs[n0:n1].reshape(NTS, D),
                 ca=ca[n0:n1].reshape(NS, 128, 256), cb1=cb1[n0:n1].reshape(NS, 128, 512),
                 cb2=cb2[n0:n1].reshape(NS, 512, 512), cb3=cb3[n0:n1].reshape(NS, 2048, 512),
                 stc=stc[n0:n1].reshape(NS * 2, DFF))
        in_maps.append({k: np.ascontiguousarray(m[k]) for k, _ in IN_SPECS})
    if "nc" not in _CACHE:
        _CACHE["nc"] = build()
    res = run_bass_kernel_spmd(_CACHE["nc"], in_maps, core_ids=list(range(NCORES)))
    R = res.results
    cat = lambda k: np.stack([R[c][k] for c in range(NCORES)], axis=0)
    y_p = cat("y_p")
    y_s = cat("y_s").reshape(128, TS, D)
    outs = [y_p, y_s,
            cat("ca_p").reshape(1, 8, 128, 2, 2, 64), cat("ca_s").reshape(1, 128, TS, 2, 2, 64),
            cat("cb1_p").reshape(1, 8, 128, 2, 4, 64), cat("cb1_s").reshape(1, 128, TS, 2, 4, 64),
            cat("cb2_p").reshape(1, 8, 512, 2, 4, 64), cat("cb2_s").reshape(1, 128, TS, 2, 4, 64),
            cat("cb3_p").reshape(1, 8, 2048, 2, 4, 64), cat("cb3_s").reshape(1, 128, TS, 2, 4, 64),
            cat("sc_p").reshape(1, 8, 2, DFF), cat("sc_s").reshape(1, 128, 2, DFF)]
    return tuple(np.ascontiguousarray(o.astype(np.float32)) for o in outs)
```
